# Optimizing a Trainium2 kernel written in Bass

```python
import math
import jax, jax.numpy as jnp
from jax import lax
import numpy as np

D_MODEL = 2048
BATCH = 8
SEQ = 4096
DEPTH = 4

N_MIXERS = 4
N_HEADS = 16
HEAD_DIM = 128
MIX_WIDTH = N_HEADS * HEAD_DIM
N_MEM = 256
MEM_HEADS = 4
MEM_WIDTH = MEM_HEADS * HEAD_DIM
OUT_IN = MIX_WIDTH + MEM_WIDTH
D_FF = 5632
Q_BLOCK = 128
RMS_EPS = 1e-6
REL_BUCKETS = 32
REL_MAX_DIST = 2048
T5_INIT_SCALE = 0.2
MAX_POS_OFFSET = 1024
FOX_GATE_BIAS = 4.0
FOX_COLS = 3 * MIX_WIDTH + N_HEADS
Q_LORA = 512
KV_LORA = 512
NOPE_DIM = 128
ROPE_DIM = 64
ROPE_THETA = 10000.0
MLA_COLS = Q_LORA + KV_LORA + ROPE_DIM
DIL_GROUPS = ((128, 1), (512, 4), (2048, 16))
DIL_COLS = len(DIL_GROUPS) * 3 * MIX_WIDTH
DSA_KV_HEADS = 4
IDX_HEADS = 16
IDX_DIM = 64
TOPK_MAX = 256
DSA_COLS = MIX_WIDTH + 2 * DSA_KV_HEADS * HEAD_DIM + IDX_HEADS * IDX_DIM + IDX_DIM + IDX_HEADS

kernel_name = 'hybrid_fox_mla_dilated_dsa_trunk'

F32 = jnp.float32


def rms_norm(x, g):
    xf = x.astype(F32)
    y = xf * lax.rsqrt(jnp.mean(xf * xf, axis=-1, keepdims=True) + RMS_EPS)
    return (y * g.astype(F32)).astype(x.dtype)


def swiglu(x, w_gate, w_up, w_down):
    return (jax.nn.silu(x @ w_gate) * (x @ w_up)) @ w_down


def t5_bucket(dist):
    n = jnp.maximum(dist, 0)
    exact = REL_BUCKETS // 2
    nf = jnp.maximum(n, 1).astype(F32)
    large = exact + (jnp.log(nf / exact) / math.log(REL_MAX_DIST / exact) * (REL_BUCKETS - exact)).astype(jnp.int32)
    large = jnp.minimum(large, REL_BUCKETS - 1)
    return jnp.where(n < exact, n, large)


def rope(x, cos, sin):
    half = ROPE_DIM // 2
    x1, x2 = x[..., :half].astype(F32), x[..., half:].astype(F32)
    return jnp.concatenate([x1 * cos - x2 * sin, x1 * sin + x2 * cos], axis=-1).astype(x.dtype)


def causal_block_attention(q, k, v, scale, log_decay=None):
    B, S, H, _ = q.shape
    key_pos = jnp.arange(S)
    decay_k = None if log_decay is None else jnp.moveaxis(log_decay, 2, 1)

    def block(i):
        qb = lax.dynamic_slice_in_dim(q, i * Q_BLOCK, Q_BLOCK, axis=1)
        qpos = i * Q_BLOCK + jnp.arange(Q_BLOCK)
        s = jnp.einsum('bqhd,bkhd->bhqk', qb, k, preferred_element_type=F32) * scale
        if log_decay is not None:
            db = lax.dynamic_slice_in_dim(decay_k, i * Q_BLOCK, Q_BLOCK, axis=2)
            s = s + (db[..., None] - decay_k[:, :, None, :])
        s = jnp.where(key_pos[None, :] <= qpos[:, None], s, -jnp.inf)
        pr = jax.nn.softmax(s, axis=-1).astype(v.dtype)
        return jnp.einsum('bhqk,bkhd->bqhd', pr, v)

    o = lax.map(block, jnp.arange(S // Q_BLOCK))
    return jnp.moveaxis(o, 0, 1).reshape(B, S, H * v.shape[-1])


def fox_mixer(p, b_f, qk_g):
    B, S, _ = p.shape
    q, k, v, fg = jnp.split(p, [MIX_WIDTH, 2 * MIX_WIDTH, 3 * MIX_WIDTH], axis=-1)
    shp = (B, S, N_HEADS, HEAD_DIM)
    q = rms_norm(q.reshape(shp), qk_g[0])
    k = rms_norm(k.reshape(shp), qk_g[1])
    log_f = jax.nn.log_sigmoid(fg.astype(F32) + b_f.astype(F32))
    cum = lax.cumsum(log_f, axis=1)
    return causal_block_attention(q, k, v.reshape(shp), HEAD_DIM ** -0.5, cum)


def mla_mixer(p, positions, q_norm_g, w_uq, kv_norm_g, w_ukv, nope_g, rope_g):
    B, S, _ = p.shape
    cq, ckv, kr = jnp.split(p, [Q_LORA, Q_LORA + KV_LORA], axis=-1)
    q = (rms_norm(cq, q_norm_g) @ w_uq).reshape(B, S, N_HEADS, NOPE_DIM + ROPE_DIM)
    kv = (rms_norm(ckv, kv_norm_g) @ w_ukv).reshape(B, S, N_HEADS, NOPE_DIM + HEAD_DIM)
    half = ROPE_DIM // 2
    inv = ROPE_THETA ** (-jnp.arange(half, dtype=F32) / half)
    ang = positions.astype(F32)[..., None] * inv
    cos, sin = jnp.cos(ang), jnp.sin(ang)
    q_nope = rms_norm(q[..., :NOPE_DIM], nope_g[0])
    q_rope = rope(rms_norm(q[..., NOPE_DIM:], rope_g[0]), cos[:, :, None], sin[:, :, None])
    k_nope = rms_norm(kv[..., :NOPE_DIM], nope_g[1])
    v = kv[..., NOPE_DIM:]
    k_rope = rope(rms_norm(kr, rope_g[1]), cos, sin)
    qf = jnp.concatenate([q_nope, q_rope], axis=-1)
    kf = jnp.concatenate([k_nope, jnp.broadcast_to(k_rope[:, :, None, :], (B, S, N_HEADS, ROPE_DIM))], axis=-1)
    return causal_block_attention(qf, kf, v, (NOPE_DIM + ROPE_DIM) ** -0.5)


def dilated_group(q, k, v, dil, sub_window, t5_table):
    B, S, H, D = q.shape
    Ls = S // dil
    nb = -(-Ls // Q_BLOCK)
    Lp = nb * Q_BLOCK

    def by_stride(a):
        a = a.reshape(B, Ls, dil, H, D).transpose(0, 2, 1, 3, 4).reshape(B * dil, Ls, H, D)
        return jnp.pad(a, ((0, 0), (0, Lp - Ls), (0, 0), (0, 0)))

    front = ((0, 0), (Q_BLOCK, 0), (0, 0), (0, 0))
    qs = by_stride(q)
    ks = jnp.pad(by_stride(k), front)
    vs = jnp.pad(by_stride(v), front)
    kk = jnp.arange(2 * Q_BLOCK)
    rel = jnp.arange(Q_BLOCK)[:, None] + Q_BLOCK - kk[None, :]
    band = (rel >= 0) & (rel <= sub_window)
    bias = jnp.moveaxis(t5_table[t5_bucket(rel * dil)], -1, 0).astype(F32)
    scale = D ** -0.5

    def block(i):
        qb = lax.dynamic_slice_in_dim(qs, i * Q_BLOCK, Q_BLOCK, axis=1)
        kb = lax.dynamic_slice_in_dim(ks, i * Q_BLOCK, 2 * Q_BLOCK, axis=1)
        vb = lax.dynamic_slice_in_dim(vs, i * Q_BLOCK, 2 * Q_BLOCK, axis=1)
        s = jnp.einsum('nqhd,nkhd->nhqk', qb, kb, preferred_element_type=F32) * scale + bias
        valid = band & (kk[None, :] >= Q_BLOCK - i * Q_BLOCK)
        s = jnp.where(valid, s, -jnp.inf)
        m = jnp.max(s, axis=-1, keepdims=True)
        e = jnp.exp(s - m)
        den = jnp.sum(e, axis=-1, keepdims=True)
        o = jnp.einsum('nhqk,nkhd->nqhd', (e / den).astype(vb.dtype), vb)
        lse = (m + jnp.log(den))[..., 0]
        return o, jnp.moveaxis(lse, 1, 2)

    o, lse = lax.map(block, jnp.arange(nb))

    def back(a):
        a = jnp.moveaxis(a, 0, 1)
        a = a.reshape((B * dil, Lp) + a.shape[3:])[:, :Ls]
        a = jnp.swapaxes(a.reshape((B, dil, Ls) + a.shape[2:]), 1, 2)
        return a.reshape((B, S) + a.shape[3:])

    return back(o), back(lse)


def dilated_mixer(p, qk_g, t5_table):
    B, S, _ = p.shape
    p = p.reshape(B, S, len(DIL_GROUPS), 3, N_HEADS, HEAD_DIM)
    outs, lses = [], []
    for g, (win, dil) in enumerate(DIL_GROUPS):
        q = rms_norm(p[:, :, g, 0], qk_g[g, 0])
        k = rms_norm(p[:, :, g, 1], qk_g[g, 1])
        o, lse = dilated_group(q, k, p[:, :, g, 2], dil, win // dil, t5_table)
        outs.append(o)
        lses.append(lse)
    alpha = jax.nn.softmax(jnp.stack(lses), axis=0)
    out = jnp.einsum('gbsh,gbshd->bshd', alpha, jnp.stack(outs).astype(F32))
    return out.reshape(B, S, MIX_WIDTH).astype(p.dtype)


def dsa_mixer(p, qk_g, t5_table):
    B, S, _ = p.shape
    kvw = DSA_KV_HEADS * HEAD_DIM
    offs = np.cumsum([MIX_WIDTH, kvw, kvw, IDX_HEADS * IDX_DIM, IDX_DIM]).tolist()
    q, k, v, qi, ki, wi = jnp.split(p, offs, axis=-1)
    group = N_HEADS // DSA_KV_HEADS
    q = rms_norm(q.reshape(B, S, DSA_KV_HEADS, group, HEAD_DIM), qk_g[0])
    k = rms_norm(k.reshape(B, S, DSA_KV_HEADS, HEAD_DIM), qk_g[1])
    v = v.reshape(B, S, DSA_KV_HEADS, HEAD_DIM)
    qi = qi.reshape(B, S, IDX_HEADS, IDX_DIM)
    wi = wi.astype(F32) * IDX_HEADS ** -0.5
    n_sel = min(TOPK_MAX, S // 4)
    key_pos = jnp.arange(S)
    gather = jax.vmap(lambda a, idx: a[idx])

    def block(i):
        sl = lambda a: lax.dynamic_slice_in_dim(a, i * Q_BLOCK, Q_BLOCK, axis=1)
        qb, qib, wib = sl(q), sl(qi), sl(wi)
        qpos = i * Q_BLOCK + jnp.arange(Q_BLOCK)
        dots = jnp.einsum('bqhd,bkd->bqhk', qib, ki, preferred_element_type=F32) * IDX_DIM ** -0.5
        score = jnp.einsum('bqh,bqhk->bqk', wib, jax.nn.relu(dots))
        score = jnp.where(key_pos[None, None, :] <= qpos[None, :, None], score, -jnp.inf)
        _, idx = lax.top_k(score, n_sel)
        kg, vg = gather(k, idx), gather(v, idx)
        dist = qpos[None, :, None] - idx
        bias = t5_table[t5_bucket(dist)].reshape(B, Q_BLOCK, n_sel, DSA_KV_HEADS, group)
        s = jnp.einsum('bqgrd,bqkgd->bqgrk', qb, kg, preferred_element_type=F32) * HEAD_DIM ** -0.5
        s = s + jnp.moveaxis(bias, 2, 4).astype(F32)
        s = jnp.where((dist >= 0)[:, :, None, None, :], s, -jnp.inf)
        pr = jax.nn.softmax(s, axis=-1).astype(v.dtype)
        return jnp.einsum('bqgrk,bqkgd->bqgrd', pr, vg)

    o = lax.map(block, jnp.arange(S // Q_BLOCK))
    return jnp.moveaxis(o, 0, 1).reshape(B, S, MIX_WIDTH)


def memory_attention(qm, mem_kv, qk_g):
    B, S, _ = qm.shape
    q = rms_norm(qm.reshape(B, S, MEM_HEADS, HEAD_DIM), qk_g[0])
    k, v = jnp.split(mem_kv, 2, axis=-1)
    k = rms_norm(k.reshape(B, N_MEM, MEM_HEADS, HEAD_DIM), qk_g[1])
    v = v.reshape(B, N_MEM, MEM_HEADS, HEAD_DIM)
    s = jnp.einsum('bqhd,bkhd->bhqk', q, k, preferred_element_type=F32) * HEAD_DIM ** -0.5
    pr = jax.nn.softmax(s, axis=-1).astype(v.dtype)
    return jnp.einsum('bhqk,bkhd->bqhd', pr, v).reshape(B, S, MEM_WIDTH)


def setup_inputs(seed: int = 0) -> dict:
    key = jax.random.key(seed)
    keys = iter(jax.random.split(key, 48))

    def normal(shape, scale):
        return jax.random.normal(next(keys), shape, F32) * scale

    def gain(shape):
        return 1.0 + normal(shape, 0.05)

    n_a, n_b, n_c, n_d = [len(range(m, DEPTH, N_MIXERS)) for m in range(N_MIXERS)]
    offsets = jax.random.randint(next(keys), (BATCH, 1), 0, MAX_POS_OFFSET, dtype=jnp.int32)
    positions = offsets + jnp.arange(SEQ, dtype=jnp.int32)[None, :]
    return {
        'x': normal((BATCH, SEQ, D_MODEL), 1.0),
        'mem': normal((BATCH, N_MEM, D_MODEL), 1.0),
        'positions': positions,
        't5_table': normal((REL_BUCKETS, N_HEADS), T5_INIT_SCALE),
        'ffn_norm': gain((DEPTH, 2, D_MODEL)),
        'ffn_w_gate': normal((DEPTH, 2, D_MODEL, D_FF), D_MODEL ** -0.5),
        'ffn_w_up': normal((DEPTH, 2, D_MODEL, D_FF), D_MODEL ** -0.5),
        'ffn_w_down': normal((DEPTH, 2, D_FF, D_MODEL), D_FF ** -0.5),
        'attn_norm': gain((DEPTH, D_MODEL)),
        'mem_norm': gain((DEPTH, D_MODEL)),
        'mem_w_kv': normal((DEPTH, D_MODEL, 2 * MEM_WIDTH), D_MODEL ** -0.5),
        'mem_qk_g': gain((DEPTH, 2, HEAD_DIM)),
        'w_out': normal((DEPTH, OUT_IN, D_MODEL), OUT_IN ** -0.5),
        'a_w_in': normal((n_a, D_MODEL, FOX_COLS + MEM_WIDTH), D_MODEL ** -0.5),
        'a_b_f': FOX_GATE_BIAS + normal((n_a, N_HEADS), 0.5),
        'a_qk_g': gain((n_a, 2, HEAD_DIM)),
        'b_w_in': normal((n_b, D_MODEL, MLA_COLS + MEM_WIDTH), D_MODEL ** -0.5),
        'b_q_norm': gain((n_b, Q_LORA)),
        'b_w_uq': normal((n_b, Q_LORA, N_HEADS * (NOPE_DIM + ROPE_DIM)), Q_LORA ** -0.5),
        'b_kv_norm': gain((n_b, KV_LORA)),
        'b_w_ukv': normal((n_b, KV_LORA, N_HEADS * (NOPE_DIM + HEAD_DIM)), KV_LORA ** -0.5),
        'b_nope_g': gain((n_b, 2, NOPE_DIM)),
        'b_rope_g': gain((n_b, 2, ROPE_DIM)),
        'c_w_in': normal((n_c, D_MODEL, DIL_COLS + MEM_WIDTH), D_MODEL ** -0.5),
        'c_qk_g': gain((n_c, len(DIL_GROUPS), 2, HEAD_DIM)),
        'd_w_in': normal((n_d, D_MODEL, DSA_COLS + MEM_WIDTH), D_MODEL ** -0.5),
        'd_qk_g': gain((n_d, 2, HEAD_DIM)),
    }


def reference(x, mem, positions, t5_table, ffn_norm, ffn_w_gate, ffn_w_up, ffn_w_down, attn_norm,
              mem_norm, mem_w_kv, mem_qk_g, w_out, a_w_in, a_b_f, a_qk_g, b_w_in, b_q_norm, b_w_uq,
              b_kv_norm, b_w_ukv, b_nope_g, b_rope_g, c_w_in, c_qk_g, d_w_in, d_qk_g):
    for i in range(DEPTH):
        m, j = i % N_MIXERS, i // N_MIXERS
        x = x + 0.5 * swiglu(rms_norm(x, ffn_norm[i, 0]), ffn_w_gate[i, 0], ffn_w_up[i, 0], ffn_w_down[i, 0])
        h = rms_norm(x, attn_norm[i])
        if m == 0:
            p = h @ a_w_in[j]
            mix = fox_mixer(p[..., :FOX_COLS], a_b_f[j], a_qk_g[j])
        elif m == 1:
            p = h @ b_w_in[j]
            mix = mla_mixer(p[..., :MLA_COLS], positions, b_q_norm[j], b_w_uq[j], b_kv_norm[j],
                            b_w_ukv[j], b_nope_g[j], b_rope_g[j])
        elif m == 2:
            p = h @ c_w_in[j]
            mix = dilated_mixer(p[..., :DIL_COLS], c_qk_g[j], t5_table)
        else:
            p = h @ d_w_in[j]
            mix = dsa_mixer(p[..., :DSA_COLS], d_qk_g[j], t5_table)
        mem_kv = rms_norm(mem, mem_norm[i]) @ mem_w_kv[i]
        mo = memory_attention(p[..., -MEM_WIDTH:], mem_kv, mem_qk_g[i])
        x = x + jnp.concatenate([mix.astype(x.dtype), mo.astype(x.dtype)], axis=-1) @ w_out[i]
        x = x + 0.5 * swiglu(rms_norm(x, ffn_norm[i, 1]), ffn_w_gate[i, 1], ffn_w_up[i, 1], ffn_w_down[i, 1])
    return x
```

```python
import math
import os
import numpy as np
import ml_dtypes
import concourse.bass as bass
import concourse.mybir as mybir
from concourse.bass_utils import run_bass_kernel_spmd

F32 = mybir.dt.float32
BF16 = mybir.dt.bfloat16
I32 = mybir.dt.int32
AF = mybir.ActivationFunctionType
ALU = mybir.AluOpType
AX = mybir.AxisListType

S = 4096
D = 2048
DFF = 5632
NSUB = S // 512
DEPTH = 4
EPS = 1e-6
NEG = -30000.0


class T:
    __slots__ = ("ap", "w", "r", "psum")

    def __init__(self, ap, psum=False):
        self.ap = ap
        self.w = None
        self.r = {}
        self.psum = psum

    def __getitem__(self, idx):
        return V(self, self.ap[idx])


class V:
    __slots__ = ("t", "ap")

    def __init__(self, t, ap):
        self.t = t
        self.ap = ap

    def __getitem__(self, idx):
        return V(self.t, self.ap[idx])


def _tile(x):
    return x.t if isinstance(x, V) else x


class P:
    ENGS = ("pe", "act", "dve", "pool", "sp")

    def __init__(self, nc, n_dma_sems=8):
        self.nc = nc
        self.eng = {"pe": nc.tensor, "act": nc.scalar, "dve": nc.vector,
                    "pool": nc.gpsimd, "sp": nc.sync}
        self.sem = {}
        self.cnt = {}
        self.semid = {}
        self._nid = 0
        for e in self.ENGS:
            self._nid += 1
            self.sem[e] = nc.alloc_semaphore(f"s_{e}")
            self.cnt[e] = 0
            self.semid[e] = self._nid
        self.dsem = {}
        for q in ("sp", "act", "pool"):
            lst = []
            for i in range(n_dma_sems):
                self._nid += 1
                lst.append([nc.alloc_semaphore(f"d_{q}{i}"), 0, self._nid])
            self.dsem[q] = lst
        self.drr = {"sp": 0, "act": 0, "pool": 0}
        self.waited = {e: {} for e in self.ENGS}
        self.n_inst = 0
        self.n_wait = 0

    def _wait(self, E, tk, kind):
        if tk is None:
            return
        src, sid, sh, v = tk
        if src == E:
            if E == "pe" or kind != "raw":
                return
        wd = self.waited[E]
        if wd.get(sid, 0) >= v:
            return
        self.eng[E].wait_ge(sh, v)
        self.n_wait += 1
        wd[sid] = v

    def _deps(self, E, w, r):
        for x in r:
            t = _tile(x)
            if t is not None:
                self._wait(E, t.w, "raw")
                if t.psum:
                    for tk in t.r.values():
                        self._wait(E, tk, "war")
        for x in w:
            t = _tile(x)
            if t is not None:
                self._wait(E, t.w, "waw")
                for tk in t.r.values():
                    self._wait(E, tk, "war")

    def _mark(self, tk, w, r):
        for x in r:
            t = _tile(x)
            if t is not None:
                t.r[tk[1]] = tk
        for x in w:
            t = _tile(x)
            if t is not None:
                t.w = tk
                t.r = {}

    def op(self, E, fn, w=(), r=(), inc=True):
        self._deps(E, w, r)
        inst = fn(self.eng[E])
        self.n_inst += 1
        if inc:
            self.cnt[E] += 1
            inst.then_inc(self.sem[E], 1)
            tk = (E, self.semid[E], self.sem[E], self.cnt[E])
        else:
            tk = (E, self.semid[E], self.sem[E], self.cnt[E] + 1)
        self._mark(tk, w, r)
        return inst

    def dma(self, Q, out, in_, w=(), r=(), **kw):
        self._deps(Q, w, r)
        lst = self.dsem[Q]
        k = self.drr[Q]
        self.drr[Q] = (k + 1) % len(lst)
        ent = lst[k]
        if ent[1] > 0:
            self._wait(Q, ("dma", ent[2], ent[0], ent[1]), "raw")
        inst = self.eng[Q].dma_start(out=out, in_=in_, **kw)
        self.n_inst += 1
        ent[1] += 16
        inst.then_inc(ent[0], 16)
        tk = ("dma", ent[2], ent[0], ent[1])
        self._mark(tk, w, r)
        return tk

    def barrier(self):
        for E in self.ENGS:
            for E2 in self.ENGS:
                if E2 != E and self.cnt[E2] > 0:
                    self._wait(E, (E2, self.semid[E2], self.sem[E2], self.cnt[E2]), "raw")
            self.wait_all_dma(E)

    def wait_all_dma(self, E):
        for q in self.dsem:
            for ent in self.dsem[q]:
                if ent[1] > 0:
                    self._wait(E, ("dma", ent[2], ent[0], ent[1]), "raw")

    def mm(self, out, pairs, first=True, last=True):
        n = len(pairs)
        for i, (l, r_) in enumerate(pairs):
            st = first and i == 0
            sp = last and i == n - 1
            self.op("pe", lambda e, l=l, r_=r_, st=st, sp=sp: e.matmul(
                out.ap, l.ap, r_.ap, start=st, stop=sp), w=[out], r=[l, r_], inc=(i == n - 1))

    def act(self, out, in_, func, bias=None, scale=None, extra_r=(), accum=None):
        kw = {}
        rr = [in_] + list(extra_r)
        ww = [out]
        if bias is not None:
            if isinstance(bias, (V, T)):
                kw["bias"] = bias.ap
                rr.append(bias)
            else:
                kw["bias"] = bias
        if scale is not None:
            if isinstance(scale, (V, T)):
                kw["scale"] = scale.ap
                rr.append(scale)
            else:
                kw["scale"] = scale
        if accum is not None:
            kw["accum_out"] = accum.ap
            ww.append(accum)
        self.op("act", lambda e: e.activation(out.ap, in_.ap, func, **kw), w=ww, r=rr)

    def stt(self, out, in0, scalar, in1, op0, op1, E="dve"):
        rr = [in0, in1]
        sc = scalar
        if isinstance(scalar, (V, T)):
            sc = scalar.ap
            rr.append(scalar)
        self.op(E, lambda e: e.scalar_tensor_tensor(out.ap, in0.ap, sc, in1.ap, op0, op1), w=[out], r=rr)

    def tt(self, out, in0, in1, op, E="dve"):
        self.op(E, lambda e: e.tensor_tensor(out.ap, in0.ap, in1.ap, op), w=[out], r=[in0, in1])

    def ts(self, out, in0, s1, op0, s2=None, op1=None, E="dve"):
        rr = [in0]
        a1 = s1
        if isinstance(s1, (V, T)):
            a1 = s1.ap
            rr.append(s1)
        a2 = s2
        if isinstance(s2, (V, T)):
            a2 = s2.ap
            rr.append(s2)
        if op1 is None:
            self.op(E, lambda e: e.tensor_scalar(out.ap, in0.ap, a1, None, op0), w=[out], r=rr)
        else:
            self.op(E, lambda e: e.tensor_scalar(out.ap, in0.ap, a1, a2, op0, op1), w=[out], r=rr)

    def copy(self, out, in_, E="dve"):
        if E == "act":
            self.op("act", lambda e: e.activation(out.ap, in_.ap, AF.Copy), w=[out], r=[in_])
        else:
            self.op(E, lambda e: e.tensor_copy(out.ap, in_.ap), w=[out], r=[in_])


class Phase:
    _uid = 0

    def __init__(self, nc):
        from contextlib import ExitStack
        self.nc = nc
        self.st = ExitStack()
        self.k = 0

    def tile(self, shape, dt, name="t"):
        Phase._uid += 1
        h = self.st.enter_context(self.nc.sbuf_tensor(f"{name}_{Phase._uid}", list(shape), dt))
        return T(h.ap())

    def close(self):
        self.st.close()


WSPECS = {
    "ffn_w_gate": (D, DFF), "ffn_w_up": (D, DFF), "ffn_w_down": (DFF, D),
    "mem_w_kv": (D, 1024), "w_out": (2560, D),
    "a_w_in": (D, 6672), "b_w_in": (D, 1600), "b_w_uq": (512, 3072), "b_w_ukv": (512, 4096),
    "c_w_in": (D, 18944), "d_w_in": (D, 4688),
}


class K:
    def __init__(self, layers=(0, 1, 2, 3), stop=None, dbg=()):
        self.layers = layers
        self.stop = stop
        self.dbg = dbg
        nc = bass.Bass("TRN2", target_bir_lowering=False)
        self.nc = nc
        self.p = P(nc)
        self.inp = {}
        self.dbg_out = {}

    def din(self, name, shape, dt=F32):
        if self.stop == "xt" and (name in WSPECS):
            return None
        self.inp[name] = self.nc.dram_tensor(name, list(shape), dt, kind="ExternalInput").ap()
        return self.inp[name]

    def dscr(self, name, shape, dt):
        return self.nc.dram_tensor(name, list(shape), dt, kind="Internal").ap()

    def declare(self):
        nc = self.nc
        self.din("x", [S, D])
        self.din("mem", [256, D])
        self.din("positions", [1, S], I32)
        self.din("t5_table", [32, 16])
        self.din("ffn_norm", [DEPTH * 2 * 16, 128])
        for n in ("ffn_w_gate", "ffn_w_up", "ffn_w_down"):
            k, c = WSPECS[n]
            self.din(n, [DEPTH, 2, k, c])
        self.din("attn_norm", [DEPTH * 16, 128])
        self.din("mem_norm", [DEPTH * 16, 128])
        self.din("mem_w_kv", [DEPTH, D, 1024])
        self.din("mem_qk_g", [DEPTH * 2, 128])
        self.din("w_out", [DEPTH, 2560, D])
        self.din("a_w_in", [1, D, 6672])
        self.din("a_b_f", [16, 1])
        self.din("a_qk_g", [2, 128])
        self.din("b_w_in", [1, D, 1600])
        self.din("b_q_norm", [4, 128])
        self.din("b_w_uq", [1, 512, 3072])
        self.din("b_kv_norm", [4, 128])
        self.din("b_w_ukv", [1, 512, 4096])
        self.din("b_nope_g", [2, 128])
        self.din("b_rope_g", [2, 64])
        self.din("c_w_in", [1, D, 18944])
        self.din("c_qk_g", [6, 128])
        self.din("d_w_in", [1, D, 4688])
        self.din("d_qk_g", [2, 128])
        self.din("c_ident", [128, 128])
        self.din("c_identb", [128, 128], BF16)
        self.din("c_antib", [128, 128], BF16)
        self.din("c_onesb", [128, 128], BF16)
        self.din("c_cmask", [128, 4, 512], BF16)
        self.din("c_sel16", [16, 16, 128], BF16)
        self.din("c_ohd", [32, 2688])
        self.din("c_ohg", [3, 33, 384])
        self.din("c_rope", [64, 2])
        self.out = nc.dram_tensor("out", [S, D], F32, kind="ExternalOutput").ap()
        self.XT = self.dscr("XT", [16, 128, S], F32)
        self.PT = self.dscr("PTs", [112, 128, S], BF16)
        self.VTM = self.dscr("VTM", [S, 6144], BF16)
        self.AT = self.dscr("ATs", [20, 128, S], BF16)
        self.FG = self.dscr("FGs", [16, S], F32)
        self.ROPE = self.dscr("ROPEs", [2, 64, S], F32)
        self.BEXT_h = self.nc.dram_tensor("BEXTs", [16, 2688], F32, kind="Internal")
        self.BEXT = self.BEXT_h.ap()
        self.TZ = self.dscr("TZs", [16, 128, 2560], BF16)
        self.VEXT_h = self.nc.dram_tensor("VEXTs", [48, 512], F32, kind="Internal")
        self.VEXT = self.VEXT_h.ap()
        self.WB = {}

    def dbgout(self, name, src_ap, shape, dt):
        o = self.nc.dram_tensor("dbg_" + name, list(shape), dt, kind="ExternalOutput").ap()
        self.p.barrier()
        self.p.dma("sp", o, src_ap)
        self.p.barrier()

    def load_consts(self):
        p = self.p
        nc = self.nc
        self.cst = Phase(nc)
        c = self.cst
        self.ident = c.tile([128, 128], F32, "ident")
        self.identb = c.tile([128, 128], BF16, "identb")
        self.antib = c.tile([128, 128], BF16, "antib")
        self.onesb = c.tile([128, 128], BF16, "onesb")
        for t, n in ((self.ident, "c_ident"), (self.identb, "c_identb"), (self.antib, "c_antib"),
                     (self.onesb, "c_onesb")):
            p.dma("sp", t.ap, self.inp[n], w=[t])
        self.ps = [T(nc.alloc_psum_tensor(f"ps{i}", [128, 512], F32).ap(), psum=True) for i in range(8)]
        self.g_ffn = self.colvecs("ffn_norm", 128)
        self.g_attn = self.colvecs("attn_norm", 64)
        self.g_mem = self.colvecs("mem_norm", 64)
        self.g_memqk = self.colvecs("mem_qk_g", 8)

    def colvecs(self, name, n):
        p = self.p
        c = self.cst
        st = c.tile([128, 128], F32, "cvst")
        out = c.tile([128, n], F32, "cv")
        p.dma("sp", st.ap[0:n, :], self.inp[name], w=[st])
        ps = self.ps[7]
        p.op("pe", lambda e: e.transpose(ps.ap[:, 0:n], st.ap[0:n, :], self.ident.ap[0:n, 0:n]),
             w=[ps], r=[st, self.ident])
        p.copy(out, ps[:, 0:n])
        return out

    def convert(self, key, src, Kdim, C):
        p = self.p
        KC = Kdim // 128
        dst = self.dscr("WB_" + key, [128, KC, C], BF16)
        self.WB[key] = dst
        CW = 2048
        jobs = [(kc, c0, min(CW, C - c0)) for kc in range(KC) for c0 in range(0, C, CW)]
        ph = self.cvph
        engs = ("pool", "act", "dve")
        for i, (kc, c0, cw) in enumerate(jobs):
            st = self.cv_st[i % 3]
            bf = self.cv_bf[i % 3]
            p.dma("sp", st.ap[:, 0:cw], src[kc * 128:(kc + 1) * 128, c0:c0 + cw], w=[st])
            E = engs[i % 3]
            p.copy(bf[:, 0:cw], st[:, 0:cw], E=E)
            p.dma("pool" if False else "sp", dst[:, kc, c0:c0 + cw], bf.ap[:, 0:cw], r=[bf])
        return dst

    def convert_all(self):
        nc = self.nc
        self.cvph = Phase(nc)
        self.cv_st = [self.cvph.tile([128, 2048], F32, "cvs") for _ in range(3)]
        self.cv_bf = [self.cvph.tile([128, 2048], BF16, "cvb") for _ in range(3)]
        mixw = {0: [("a_w_in", 0)], 1: [("b_w_in", 0), ("b_w_uq", 0), ("b_w_ukv", 0)],
                2: [("c_w_in", 0)], 3: [("d_w_in", 0)]}
        for L in self.layers:
            for s in (0, 1):
                for n in ("ffn_w_gate", "ffn_w_up", "ffn_w_down"):
                    k, c = WSPECS[n]
                    self.convert(f"{n}_{L}_{s}", self.inp[n][L, s], k, c)
            if self.stop == "ffn0":
                break
            self.convert(f"mem_w_kv_{L}", self.inp["mem_w_kv"][L], D, 1024)
            self.convert(f"w_out_{L}", self.inp["w_out"][L], 2560, D)
            for n, j in mixw[L % 4]:
                k, c = WSPECS[n]
                self.convert(f"{n}_{L}", self.inp[n][j], k, c)
        self.p.barrier()
        self.cvph.close()

    def x_to_xt(self):
        p = self.p
        ph = Phase(self.nc)
        xt = [ph.tile([128, D], F32, "xin") for _ in range(8)]
        ob = [ph.tile([128, 512], F32, "xo") for _ in range(3)]
        k = 0
        for tb in range(NSUB):
            tiles = []
            for j in range(4):
                t = xt[(tb % 2) * 4 + j]
                r0 = tb * 512 + j * 128
                p.dma("sp", t.ap, self.inp["x"][r0:r0 + 128, :], w=[t])
                tiles.append(t)
            for dc in range(16):
                ps = self.ps[dc % 4]
                for j in range(4):
                    p.op("pe", lambda e, j=j, ps=ps, dc=dc: e.transpose(
                        ps.ap[:, j * 128:(j + 1) * 128], tiles[j].ap[:, dc * 128:(dc + 1) * 128], self.ident.ap),
                        w=[ps], r=[tiles[j], self.ident], inc=(j == 3))
                o = ob[k % 3]
                k += 1
                p.copy(o, ps, E=("dve" if dc % 2 == 0 else "act"))
                p.dma("sp", self.XT[dc][:, tb * 512:(tb + 1) * 512], o.ap, r=[o])
        p.barrier()
        ph.close()

    def xt_to_out(self):
        p = self.p
        ph = Phase(self.nc)
        xin = [ph.tile([128, 512], F32, "xi") for _ in range(6)]
        ot = [ph.tile([128, D], F32, "xo") for _ in range(8)]
        k = 0
        for tb in range(NSUB):
            outs = [ot[(tb % 2) * 4 + j] for j in range(4)]
            for dc in range(16):
                xi = xin[k % 6]
                k += 1
                p.dma("sp", xi.ap, self.XT[dc][:, tb * 512:(tb + 1) * 512], w=[xi])
                ps = self.ps[dc % 4]
                for j in range(4):
                    p.op("pe", lambda e, j=j, ps=ps, xi=xi: e.transpose(
                        ps.ap[:, j * 128:(j + 1) * 128], xi.ap[:, j * 128:(j + 1) * 128], self.ident.ap),
                        w=[ps], r=[xi, self.ident], inc=(j == 3))
                for j in range(4):
                    p.copy(outs[j][:, dc * 128:(dc + 1) * 128], ps[:, j * 128:(j + 1) * 128],
                           E=("dve" if (j % 2 == 0 or os.environ.get("XO_DVE")) else "act"))
            for j in range(4):
                r0 = tb * 512 + j * 128
                p.dma("sp", self.out[r0:r0 + 128, :], outs[j].ap, r=[outs[j]])
        p.barrier()
        ph.close()

    def rstd_from_ssq(self, ps_ssq, n_feat, lnv, rstd):
        p = self.p
        p.act(lnv, ps_ssq, AF.Ln, bias=self.eps_col[:, 0:1], scale=1.0 / n_feat)
        p.act(rstd, lnv, AF.Exp, scale=-0.5)

    def ffn(self, L, s):
        p = self.p
        nc = self.nc
        ph = Phase(nc)
        Wg = self.WB[f"ffn_w_gate_{L}_{s}"]
        Wu = self.WB[f"ffn_w_up_{L}_{s}"]
        Wd = self.WB[f"ffn_w_down_{L}_{s}"]
        gcol = self.g_ffn
        gbase = (L * 2 + s) * 16
        xnT = [[ph.tile([128, 512], BF16, "xn") for _ in range(16)] for _ in range(2)]
        hT = [[ph.tile([128, 512], BF16, "h") for _ in range(44)] for _ in range(2)]
        xs = [ph.tile([128, 512], F32, "xs") for _ in range(4)]
        sq = [ph.tile([128, 512], BF16, "sq") for _ in range(2)]
        lnv = ph.tile([128, 512], F32, "lnv")
        rstd = ph.tile([128, 512], F32, "rstd")
        wg = [ph.tile([128, 16, 128], BF16, "wg") for _ in range(2)]
        wu = [ph.tile([128, 16, 128], BF16, "wu") for _ in range(2)]
        wd = [ph.tile([128, 44, 128], BF16, "wd") for _ in range(2)]
        sg = [ph.tile([128, 512], F32, "sg") for _ in range(2)]
        xo = [ph.tile([128, 512], F32, "xo") for _ in range(2)]
        ps = self.ps
        kx = 0
        for tb in range(S // 1024):
            for sub in range(2):
                tok = slice((2 * tb + sub) * 512, (2 * tb + sub + 1) * 512)
                for kc in range(16):
                    xt = xs[kx % 4]
                    kx += 1
                    p.dma("sp", xt.ap, self.XT[kc][:, tok], w=[xt])
                    q = sq[kc % 2]
                    p.act(q, xt, AF.Square)
                    p.mm(ps[0], [(self.onesb, q)], first=(kc == 0), last=(kc == 15))
                self.rstd_from_ssq(ps[0], D, lnv, rstd)
                for kc in range(16):
                    xt = xs[kx % 4]
                    kx += 1
                    p.dma("sp", xt.ap, self.XT[kc][:, tok], w=[xt])
                    p.stt(xnT[sub][kc], xt, gcol[:, gbase + kc:gbase + kc + 1], rstd, ALU.mult, ALU.mult)
            for fc in range(44):
                a = wg[fc % 2]
                b = wu[fc % 2]
                p.dma("sp", a.ap, Wg[:, :, fc * 128:(fc + 1) * 128], w=[a])
                p.dma("sp", b.ap, Wu[:, :, fc * 128:(fc + 1) * 128], w=[b])
                for sub in range(2):
                    pg = ps[1 + sub * 2]
                    pu = ps[2 + sub * 2]
                    p.mm(pg, [(a[:, kc, :], xnT[sub][kc]) for kc in range(16)])
                    p.mm(pu, [(b[:, kc, :], xnT[sub][kc]) for kc in range(16)])
                    g_ = sg[sub]
                    p.act(g_, pg, AF.Silu)
                    p.tt(hT[sub][fc], g_, pu, ALU.mult)
            for dc in range(16):
                w_ = wd[dc % 2]
                p.dma("sp", w_.ap, Wd[:, :, dc * 128:(dc + 1) * 128], w=[w_])
                for sub in range(2):
                    tok = slice((2 * tb + sub) * 512, (2 * tb + sub + 1) * 512)
                    py = ps[5 + sub]
                    p.mm(py, [(w_[:, fc, :], hT[sub][fc]) for fc in range(44)])
                    xt = xs[kx % 4]
                    kx += 1
                    p.dma("sp", xt.ap, self.XT[dc][:, tok], w=[xt])
                    o = xo[sub]
                    p.stt(o, py, 0.5, xt, ALU.mult, ALU.add)
                    p.dma("sp", self.XT[dc][:, tok], o.ap, r=[o])
        p.barrier()
        ph.close()

    def headnorm(self, ph_tiles, ps_in, ps_ssq, gcol, out_bf, nfeat=128, npart=128):
        p = self.p
        sq, lnv, rstd = ph_tiles
        p.act(sq[0:npart, :], ps_in[0:npart, :], AF.Square)
        p.mm(ps_ssq[0:npart, :], [(self.onesb[0:npart, 0:npart], sq[0:npart, :])])
        p.act(lnv[0:npart, :], ps_ssq[0:npart, :], AF.Ln, bias=self.eps_col[0:npart, 0:1], scale=1.0 / nfeat)
        p.act(rstd[0:npart, :], lnv[0:npart, :], AF.Exp, scale=-0.5)
        p.stt(out_bf, ps_in[0:npart, :], gcol, rstd[0:npart, :], ALU.mult, ALU.mult)

    def mem_setup(self):
        p = self.p
        ph = Phase(self.nc)
        mt = [ph.tile([128, D], F32, "memin") for _ in range(2)]
        sq = [ph.tile([128, 256], BF16, "msq") for _ in range(2)]
        lnv = ph.tile([128, 256], F32, "mln")
        for j in range(2):
            p.dma("sp", mt[j].ap, self.inp["mem"][j * 128:(j + 1) * 128, :], w=[mt[j]])
        for kc in range(16):
            ps = self.ps[kc % 2]
            for j in range(2):
                p.op("pe", lambda e, j=j, ps=ps, kc=kc: e.transpose(
                    ps.ap[:, j * 128:(j + 1) * 128], mt[j].ap[:, kc * 128:(kc + 1) * 128], self.ident.ap),
                    w=[ps], r=[mt[j], self.ident], inc=(j == 1))
            p.copy(self.memT[kc], ps[:, 0:256])
            q = sq[kc % 2]
            p.act(q, self.memT[kc], AF.Square)
            p.mm(self.ps[2][:, 0:256], [(self.onesb, q)], first=(kc == 0), last=(kc == 15))
        p.act(lnv, self.ps[2][:, 0:256], AF.Ln, bias=self.eps_col[:, 0:1], scale=1.0 / D)
        p.act(self.mem_rstd, lnv, AF.Exp, scale=-0.5)
        p.barrier()
        ph.close()

    def mem_kv(self, L, ph):
        p = self.p
        W = self.WB[f"mem_w_kv_{L}"]
        memn = [ph.tile([128, 256], BF16, "memn") for _ in range(16)]
        for kc in range(16):
            p.stt(memn[kc], self.memT[kc], self.g_mem[:, L * 16 + kc:L * 16 + kc + 1], self.mem_rstd,
                  ALU.mult, ALU.mult)
        kmT = [ph.tile([128, 256], BF16, "kmT") for _ in range(4)]
        vm = [ph.tile([128, 512], BF16, "vm") for _ in range(2)]
        wk = [ph.tile([128, 16, 128], BF16, "wmk") for _ in range(2)]
        wv = ph.tile([128, 16, 512], BF16, "wmv")
        sq = ph.tile([128, 512], BF16, "hsq")
        lnv = ph.tile([128, 512], F32, "hln")
        rstd = ph.tile([128, 512], F32, "hrs")
        for j in range(4):
            w_ = wk[j % 2]
            p.dma("sp", w_.ap, W[:, :, j * 128:(j + 1) * 128], w=[w_])
            ps = self.ps[j % 2]
            p.mm(ps[:, 0:256], [(w_[:, kc, :], memn[kc]) for kc in range(16)])
            self.headnorm((sq[:, 0:256], lnv[:, 0:256], rstd[:, 0:256]), ps[:, 0:256], self.ps[2][:, 0:256],
                          self.g_memqk[:, L * 2 + 1:L * 2 + 2], kmT[j])
        p.dma("sp", wv.ap, W[:, :, 512:1024], w=[wv])
        for t in range(2):
            ps = self.ps[3 + t]
            p.mm(ps, [(memn[kc][:, t * 128:(t + 1) * 128], wv[:, kc, :]) for kc in range(16)])
            p.copy(vm[t], ps)
        return kmT, vm

    def attn_chunk(self, ktiles, vtiles, qpairs_fn, extra_fn, bias_fn, pt_tiles, ps_s, ps_o, ps_d, rec, out_bf,
                   dv=128, nq=512):
        p = self.p
        n = ktiles
        for j in range(n):
            pss = ps_s[j % 2]
            pairs = list(qpairs_fn(j)) + list(extra_fn(j))
            p.mm(pss[:, 0:nq], pairs)
            pt = pt_tiles[j % 2]
            p.act(pt[:, 0:nq], pss[:, 0:nq], AF.Exp, bias=bias_fn(j))
            p.mm(ps_o[0:dv, 0:nq], [(vtiles(j), pt[:, 0:nq])], first=(j == 0), last=(j == n - 1))
            p.mm(ps_d[0:dv, 0:nq], [(self.onesb[:, 0:dv], pt[:, 0:nq])], first=(j == 0), last=(j == n - 1))
        p.op("dve", lambda e: e.reciprocal(rec.ap[0:dv, 0:nq], ps_d.ap[0:dv, 0:nq]), w=[rec], r=[ps_d])
        p.tt(out_bf, ps_o[0:dv, 0:nq], rec[0:dv, 0:nq], ALU.mult)

    def attention(self, L):
        m = L % 4
        if m == 1:
            self.rope_tables()
        self.in_proj(L)
        if m == 0:
            self.fox_core(L)
        elif m == 1:
            self.mla_core(L)
        elif m == 2:
            self.dil_core(L)
        elif m == 3:
            self.dsa_core(L)
        self.out_proj(L)

    def in_proj(self, L):
        p = self.p
        nc = self.nc
        m = L % 4
        ph = Phase(nc)
        wkey = {0: "a_w_in", 1: "b_w_in", 2: "c_w_in", 3: "d_w_in"}[m]
        W = self.WB[f"{wkey}_{L}"]
        ncols = WSPECS[wkey][1]
        memq0 = ncols - 512
        kmT, vm = self.mem_kv(L, ph)
        hT = [ph.tile([128, 512], BF16, "hT") for _ in range(16)]
        xs = [ph.tile([128, 512], F32, "xs") for _ in range(3)]
        sqx = [ph.tile([128, 512], BF16, "sqx") for _ in range(2)]
        lnv = ph.tile([128, 512], F32, "lnv")
        rstd = ph.tile([128, 512], F32, "rstd")
        hsq = ph.tile([128, 512], BF16, "hsq2")
        hln = ph.tile([128, 512], F32, "hln2")
        hrs = ph.tile([128, 512], F32, "hrs2")
        wt = [ph.tile([128, 16, 128], BF16, "wt") for _ in range(2)]
        wvt = [ph.tile([128, 16, 512], BF16, "wvt") for _ in range(2)] if m in (0, 2, 3) else None
        ob = [ph.tile([128, 512], BF16, "ob") for _ in range(3)]
        qm = [ph.tile([128, 512], BF16, "qm") for _ in range(2)]
        ptt = [ph.tile([128, 512], BF16, "ptm") for _ in range(2)]
        rec = ph.tile([128, 512], F32, "rec")
        jobs = []
        vjobs = []
        if m == 0:
            self.gq_a = self.colv_small("a_qk_g", 2, ph)
            gq = ph.tile([128, 1], F32, "gqs")
            p.ts(gq, self.gq_a[:, 0:1], 128 ** -0.5, ALU.mult)
            for h in range(16):
                jobs.append((h * 128, 128, gq[:, 0:1], h))
            for h in range(16):
                jobs.append((2048 + h * 128, 128, self.gq_a[:, 1:2], 16 + h))
            for g in range(4):
                vjobs.append((4096 + g * 512, g * 512))
        if m == 2:
            self.gq_c = self.colv_small("c_qk_g", 6, ph)
            gqc = ph.tile([128, 3], F32, "gqc")
            for g in range(3):
                p.ts(gqc[:, g:g + 1], self.gq_c[:, 2 * g:2 * g + 1], 128 ** -0.5, ALU.mult)
            for g in range(3):
                for h in range(16):
                    jobs.append((g * 6144 + h * 128, 128, gqc[:, g:g + 1], g * 32 + h))
                    jobs.append((g * 6144 + 2048 + h * 128, 128, self.gq_c[:, 2 * g + 1:2 * g + 2], g * 32 + 16 + h))
                for v4 in range(4):
                    vjobs.append((g * 6144 + 4096 + v4 * 512, g * 2048 + v4 * 512))
        if m == 3:
            self.gq_d = self.colv_small("d_qk_g", 2, ph)
            gqd = ph.tile([128, 1], F32, "gqd")
            p.ts(gqd, self.gq_d[:, 0:1], 128 ** -0.5, ALU.mult)
            for h in range(16):
                jobs.append((h * 128, 128, gqd[:, 0:1], h))
            for g in range(4):
                jobs.append((2048 + g * 128, 128, self.gq_d[:, 1:2], 16 + g))
            vjobs.append((2560, 0))
            for h in range(16):
                jobs.append((3072 + h * 64, 64, 64 ** -0.5, 20 + h))
            jobs.append((4096, 64, 1.0, 36))
            jobs.append((4160, 16, 16 ** -0.5, 37))
        mla = None
        if m == 1:
            mla = self.mla_setup(L, ph)
            mla["wt"] = wt
        gmq = ph.tile([128, 1], F32, "gmq")
        p.ts(gmq, self.g_memqk[:, L * 2:L * 2 + 1], 128 ** -0.5, ALU.mult)
        kx = 0
        ko = 0
        for tb in range(NSUB):
            tok = slice(tb * 512, (tb + 1) * 512)
            for kc in range(16):
                xt = xs[kx % 3]
                kx += 1
                p.dma("sp", xt.ap, self.XT[kc][:, tok], w=[xt])
                q = sqx[kc % 2]
                p.act(q, xt, AF.Square)
                p.mm(self.ps[0], [(self.onesb, q)], first=(kc == 0), last=(kc == 15))
            self.rstd_from_ssq(self.ps[0], D, lnv, rstd)
            for kc in range(16):
                xt = xs[kx % 3]
                kx += 1
                p.dma("sp", xt.ap, self.XT[kc][:, tok], w=[xt])
                p.stt(hT[kc], xt, self.g_attn[:, L * 16 + kc:L * 16 + kc + 1], rstd, ALU.mult, ALU.mult)
            for ji, (c0, cw, gcol, chunk) in enumerate(jobs):
                w_ = wt[ji % 2]
                p.dma("sp", w_.ap[:, :, 0:cw], W[:, :, c0:c0 + cw], w=[w_])
                ps = self.ps[1 + ji % 2]
                p.mm(ps[0:cw, :], [(w_[:, kc, 0:cw], hT[kc]) for kc in range(16)])
                o = ob[ko % 3]
                ko += 1
                if isinstance(gcol, float):
                    p.ts(o[0:cw, :], ps[0:cw, :], gcol, ALU.mult)
                else:
                    self.headnorm((hsq, hln, hrs), ps, self.ps[3], gcol, o[0:cw, :], nfeat=cw, npart=cw)
                p.dma("sp", self.PT[chunk][0:cw, tok], o.ap[0:cw, :], r=[o])
            if m == 1:
                def ob_next():
                    nonlocal ko
                    o_ = ob[ko % 3]
                    ko += 1
                    return o_
                self.mla_block(L, mla, tb, hT, W, (hsq, hln, hrs), ob_next)
            if m == 0:
                w_ = wt[0]
                p.dma("sp", w_.ap[:, :, 0:16], W[:, :, 6144:6160], w=[w_])
                ps = self.ps[1]
                p.mm(ps[0:16, :], [(w_[:, kc, 0:16], hT[kc]) for kc in range(16)])
                o32 = xs[kx % 3]
                kx += 1
                p.copy(o32[0:16, :], ps[0:16, :])
                p.dma("sp", self.FG[:, tok], o32.ap[0:16, :], r=[o32])
            for vi, (c0, v0) in enumerate(vjobs):
                w_ = wvt[vi % 2]
                p.dma("sp", w_.ap, W[:, :, c0:c0 + 512], w=[w_])
                for t in range(4):
                    ps = self.ps[4 + t % 2]
                    p.mm(ps, [(hT[kc][:, t * 128:(t + 1) * 128], w_[:, kc, :]) for kc in range(16)])
                    o = ob[ko % 3]
                    ko += 1
                    p.copy(o, ps, E=("act" if t % 2 else "dve"))
                    r0 = tb * 512 + t * 128
                    p.dma("sp", self.VTM[r0:r0 + 128, v0:v0 + 512], o.ap, r=[o])
            for j in range(4):
                w_ = wt[j % 2]
                c0 = memq0 + j * 128
                p.dma("sp", w_.ap, W[:, :, c0:c0 + 128], w=[w_])
                ps = self.ps[1 + j % 2]
                p.mm(ps, [(w_[:, kc, :], hT[kc]) for kc in range(16)])
                q_ = qm[j % 2]
                self.headnorm((hsq, hln, hrs), ps, self.ps[3], gmq[:, 0:1], q_)
                o = ob[ko % 3]
                ko += 1
                self.attn_chunk(
                    2, lambda t, j=j: vm[t][:, j * 128:(j + 1) * 128],
                    lambda t, j=j, q_=q_: [(kmT[j][:, t * 128:(t + 1) * 128], q_)],
                    lambda t: [], lambda t: None, ptt, (self.ps[4], self.ps[5]), self.ps[6], self.ps[7], rec, o)
                p.dma("sp", self.AT[16 + j][:, tok], o.ap, r=[o])
        p.barrier()
        ph.close()

    def colv_small(self, name, n, ph):
        p = self.p
        w = self.inp[name].shape[1]
        st = ph.tile([128, 128], F32, "cvs")
        out = ph.tile([128, n], F32, "cvo")
        p.dma("sp", st.ap[0:n, 0:w], self.inp[name], w=[st])
        ps = self.ps[7]
        p.op("pe", lambda e: e.transpose(ps.ap[0:w, 0:n], st.ap[0:n, 0:w], self.ident.ap[0:n, 0:n]),
             w=[ps], r=[st, self.ident])
        p.copy(out[0:w, :], ps[0:w, 0:n])
        return out

    def fox_core(self, L):
        p = self.p
        nc = self.nc
        ph = Phase(nc)
        fg = ph.tile([16, S], F32, "fg")
        p.dma("sp", fg.ap, self.FG, w=[fg])
        negbf = ph.tile([16, 1], F32, "negbf")
        p.dma("sp", negbf.ap, self.inp["a_b_f"], w=[negbf])
        p.ts(negbf, negbf, -1.0, ALU.mult)
        ones16 = ph.tile([16, S], F32, "ones16")
        p.op("pool", lambda e: e.memset(ones16.ap, 1.0), w=[ones16])
        lf = ph.tile([16, S], F32, "lf")
        ncum = ph.tile([16, S], F32, "ncum")
        p.act(lf, fg, AF.Exp, bias=negbf[:, 0:1], scale=-1.0)
        p.act(lf, lf, AF.Ln, bias=1.0)
        p.op("dve", lambda e: e.tensor_tensor_scan(ncum.ap, ones16.ap, lf.ap, 0.0, ALU.mult, ALU.add),
             w=[ncum], r=[ones16, lf])
        c_hi = ph.tile([16, S], BF16, "chi")
        c_mid = ph.tile([16, S], BF16, "cmid")
        c_lo = ph.tile([16, S], BF16, "clo")
        r1 = lf
        r2 = ones16
        p.ts(c_hi, ncum, -1.0, ALU.mult)
        p.stt(r1, ncum, -1.0, c_hi, ALU.mult, ALU.subtract)
        p.copy(c_mid, r1)
        p.tt(r2, r1, c_mid, ALU.subtract)
        p.copy(c_lo, r2)
        nct = ph.tile([128, 32, 16], F32, "nct")
        ps = self.ps[7]
        for j in range(32):
            p.op("pe", lambda e, j=j: e.transpose(ps.ap[:, j * 16:(j + 1) * 16], ncum.ap[:, j * 128:(j + 1) * 128],
                                                  self.ident.ap[0:16, 0:16]),
                 w=[ps], r=[ncum, self.ident], inc=(j == 31))
        p.copy(nct, ps.ap.rearrange("p (j h) -> p j h", h=16) if False else ps)
        sel = ph.tile([16, 16, 128], BF16, "sel16")
        p.dma("sp", sel.ap, self.inp["c_sel16"], w=[sel])
        cm = ph.tile([128, 4, 512], BF16, "cmask")
        p.dma("sp", cm.ap, self.inp["c_cmask"], w=[cm])
        qT = [ph.tile([128, S], BF16, "qT") for _ in range(2)]
        kT = [ph.tile([128, S], BF16, "kT") for _ in range(2)]
        vt = [ph.tile([128, 32, 128], BF16, "vt") for _ in range(2)]
        ptt = [ph.tile([128, 512], BF16, "pt") for _ in range(2)]
        rec = ph.tile([128, 512], F32, "rec")
        ob = [ph.tile([128, 512], BF16, "ob") for _ in range(2)]
        nctv = nct.ap
        ko = 0
        for h in range(16):
            q_ = qT[h % 2]
            k_ = kT[h % 2]
            v_ = vt[h % 2]
            p.dma("sp", q_.ap, self.PT[h], w=[q_])
            p.dma("sp", k_.ap, self.PT[16 + h], w=[k_])
            p.dma("sp", v_.ap, self.VTM[:, h * 128:(h + 1) * 128].rearrange("(j p) d -> p j d", p=128), w=[v_])
            for c in range(NSUB):
                qs = slice(c * 512, (c + 1) * 512)

                def qpairs(j, k_=k_, q_=q_, qs=qs):
                    return [(k_[:, j * 128:(j + 1) * 128], q_[:, qs])]

                def extra(j, c=c, h=h, qs=qs):
                    e = [(sel[:, h, :], c_hi[:, qs]), (sel[:, h, :], c_mid[:, qs]), (sel[:, h, :], c_lo[:, qs])]
                    if j >= 4 * c:
                        e.append((self.identb, cm[:, j - 4 * c, :]))
                    return e

                def bias(j, h=h):
                    return V(nct, nctv[:, j, h:h + 1])

                o = ob[ko % 2]
                ko += 1
                self.attn_chunk(4 * c + 4, lambda j, v_=v_: v_[:, j, :], qpairs, extra, bias, ptt,
                                (self.ps[0], self.ps[1]), self.ps[2], self.ps[3], rec, o)
                p.dma("sp", self.AT[h][:, qs], o.ap, r=[o])
        p.barrier()
        ph.close()

    def dil_core(self, L):
        p = self.p
        nc = self.nc
        ph = Phase(nc)
        tab = ph.tile([33, 16], F32, "tab33")
        p.op("dve", lambda e: e.memset(tab.ap[32:33, :], NEG), w=[tab])
        p.dma("sp", tab.ap[0:32, :], self.inp["t5_table"], w=[tab])
        ohg = ph.tile([33, 3, 384], F32, "ohg")
        p.dma("sp", ohg.ap, self.inp["c_ohg"].rearrange("g v i -> v g i"), w=[ohg])
        vx = ph.tile([16, 384], F32, "vx")
        for g in range(3):
            ps = self.ps[g]
            p.mm(ps[0:16, 0:384], [(tab, ohg[:, g, :])])
            p.copy(vx, ps[0:16, 0:384])
            p.dma("sp", self.VEXT[g * 16:(g + 1) * 16, 0:384], vx.ap, r=[vx])
        p.barrier()
        bz = [[ph.tile([128, 256], BF16, "bz") for _ in range(16)] for _ in range(3)]
        hk = [ph.tile([128, 256], F32, "hk") for _ in range(2)]
        hkb = [ph.tile([128, 256], BF16, "hkb") for _ in range(2)]
        for g in range(3):
            for h in range(16):
                i = g * 16 + h
                a = hk[i % 2]
                b = hkb[i % 2]
                src = bass.AP(tensor=self.VEXT_h, offset=i * 512, ap=[[1, 128], [1, 256]])
                p.dma("sp", a.ap, src, w=[a])
                p.copy(b, a)
                ps = self.ps[i % 2]
                p.mm(ps[:, 0:256], [(self.antib, b)])
                p.copy(bz[g][h], ps[:, 0:256], E=("act" if i % 2 else "dve"))
        num = ph.tile([128, S], F32, "num")
        den = ph.tile([128, S], F32, "den")
        qT = [ph.tile([128, S], BF16, "qT") for _ in range(2)]
        kT = [ph.tile([128, S], BF16, "kT") for _ in range(2)]
        vt = [ph.tile([128, 32, 128], BF16, "vt") for _ in range(2)]
        ptt = [ph.tile([128, 128], BF16, "pt") for _ in range(2)]
        ob = ph.tile([128, S], BF16, "ob")
        kk = 0
        kb = 0
        for h in range(16):
            for g, dil in enumerate((1, 4, 16)):
                q_ = qT[kk % 2]
                k_ = kT[kk % 2]
                v_ = vt[kk % 2]
                kk += 1
                nb = S // dil // 128
                p.dma("sp", q_.ap, self.PT[g * 32 + h], w=[q_])
                p.dma("sp", k_.ap, self.PT[g * 32 + 16 + h], w=[k_])
                c0 = g * 2048 + h * 128
                vsrc = self.VTM[:, c0:c0 + 128].rearrange("(jj p r) d -> p r jj d", p=128, r=dil)
                vdst = v_.ap.rearrange("p (r jj) d -> p r jj d", r=dil)
                for r in range(dil):
                    p.dma("sp", vdst[:, r], vsrc[:, r], w=[v_])
                for r in range(dil):
                    for i in range(nb):
                        q0 = r + dil * 128 * i
                        qsl = slice(q0, q0 + dil * 127 + 1, dil)
                        tiles = [i] if i == 0 else [i - 1, i]
                        ps_o = self.ps[2 + kb % 2]
                        ps_d = self.ps[4 + kb % 2]
                        kb += 1
                        for ti, jj in enumerate(tiles):
                            k0 = r + dil * 128 * jj
                            ksl = slice(k0, k0 + dil * 127 + 1, dil)
                            bsl = slice(0, 128) if jj == i else slice(128, 256)
                            pss = self.ps[ti]
                            p.mm(pss[:, 0:128], [(k_[:, ksl], q_[:, qsl]), (self.identb, bz[g][h][:, bsl])])
                            pt = ptt[ti]
                            p.act(pt, pss[:, 0:128], AF.Exp)
                            first = ti == 0
                            last = ti == len(tiles) - 1
                            p.mm(ps_o[:, 0:128], [(v_[:, r * nb + jj, :], pt)], first=first, last=last)
                            p.mm(ps_d[:, 0:128], [(self.onesb, pt)], first=first, last=last)
                        if g == 0:
                            p.copy(num[:, qsl], ps_o[:, 0:128], E="act")
                            p.copy(den[:, qsl], ps_d[:, 0:128], E="dve")
                        else:
                            p.tt(num[:, qsl], ps_o[:, 0:128], num[:, qsl], ALU.add)
                            p.tt(den[:, qsl], ps_d[:, 0:128], den[:, qsl], ALU.add)
            p.op("dve", lambda e: e.reciprocal(den.ap, den.ap), w=[den], r=[den])
            p.tt(ob, num, den, ALU.mult)
            p.dma("sp", self.AT[h], ob.ap, r=[ob])
        p.barrier()
        ph.close()

    def rope_tables(self):
        p = self.p
        ph = Phase(self.nc)
        TWO_PI = 2.0 * math.pi
        cr = ph.tile([64, 2], F32, "crope")
        p.dma("sp", cr.ap, self.inp["c_rope"], w=[cr])
        posi = ph.tile([64, S], I32, "posi")
        pa = self.inp["positions"]
        p.dma("sp", posi.ap, bass.AP(tensor=pa.tensor, offset=0, ap=[[0, 64], [1, S]]), w=[posi])
        ang = ph.tile([64, S], F32, "ang")
        t1 = ph.tile([64, S], F32, "t1")
        ki = ph.tile([64, S], I32, "ki")
        p.copy(ang, posi)
        p.ts(ang, ang, cr[:, 0:1], ALU.mult)
        p.ts(t1, ang, 1.0 / TWO_PI, ALU.mult)
        p.copy(ki, t1)
        p.copy(t1, ki)
        r = ph.tile([64, S], F32, "r")
        p.stt(r, t1, -TWO_PI, ang, ALU.mult, ALU.add)
        p.ts(t1, r, math.pi, ALU.is_gt)
        p.stt(r, t1, -TWO_PI, r, ALU.mult, ALU.add)
        p.ts(t1, r, -1.0, ALU.mult, math.pi, ALU.is_gt)
        p.stt(r, t1, TWO_PI, r, ALU.mult, ALU.add)
        p.ts(r, r, 3.14159, ALU.min, -3.14159, ALU.max)
        p.act(t1, r, AF.Sin)
        p.ts(t1, t1, cr[:, 1:2], ALU.mult)
        p.dma("sp", self.ROPE[1], t1.ap, r=[t1])
        p.stt(ang, r, -1.0, r, ALU.mult, ALU.max)
        hp = ph.tile([64, 1], F32, "halfpi")
        p.op("dve", lambda e: e.memset(hp.ap, math.pi / 2), w=[hp])
        p.act(ang, ang, AF.Sin, bias=hp[:, 0:1], scale=-1.0)
        p.dma("sp", self.ROPE[0], ang.ap, r=[ang])
        p.barrier()
        ph.close()

    def mla_setup(self, L, ph):
        p = self.p
        st = {}
        st["gq"] = self.colv_small("b_q_norm", 4, ph)
        st["gkv"] = self.colv_small("b_kv_norm", 4, ph)
        gn = self.colv_small("b_nope_g", 2, ph)
        gr = self.colv_small("b_rope_g", 2, ph)
        sc = 192 ** -0.5
        gqn = ph.tile([128, 1], F32, "gqn")
        p.ts(gqn, gn[:, 0:1], sc, ALU.mult)
        st["gqn"] = gqn
        st["gkn"] = gn
        grs = ph.tile([64, 4], F32, "grs")
        p.ts(grs[:, 0:1], gr[0:64, 0:1], sc, ALU.mult)
        p.copy(grs[:, 2:3], gr[0:64, 1:2])
        stg = ph.tile([128, 128], F32, "grst")
        src = self.inp["b_rope_g"]
        p.dma("sp", stg.ap[0:2, 0:32], src[:, 32:64], w=[stg])
        p.dma("sp", stg.ap[0:2, 32:64], src[:, 0:32], w=[stg])
        ps = self.ps[7]
        p.op("pe", lambda e: e.transpose(ps.ap[0:64, 0:2], stg.ap[0:2, 0:64], self.ident.ap[0:2, 0:2]),
             w=[ps], r=[stg, self.ident])
        p.ts(grs[:, 1:2], ps[0:64, 0:1], sc, ALU.mult)
        p.copy(grs[:, 3:4], ps[0:64, 1:2])
        st["grs"] = grs
        st["cqf"] = [ph.tile([128, 512], F32, "cqf") for _ in range(4)]
        st["cqn"] = [ph.tile([128, 512], BF16, "cqn") for _ in range(4)]
        st["ckvn"] = [ph.tile([128, 512], BF16, "ckvn") for _ in range(4)]
        st["cs"] = [ph.tile([64, 512], F32, "cs") for _ in range(2)]
        st["ce"] = [ph.tile([64, 512], F32, "ce") for _ in range(2)]
        st["tt"] = [ph.tile([64, 512], F32, "ropet") for _ in range(2)]
        st["wq"] = [ph.tile([128, 4, 128], BF16, "wuq") for _ in range(2)]
        st["wq2"] = [ph.tile([128, 4, 64], BF16, "wuq2") for _ in range(2)]
        st["wv"] = [ph.tile([128, 4, 512], BF16, "wukvv") for _ in range(2)]
        return st

    def rope_apply(self, st, ps_a, ps_b, rstd64, g_a, g_b, out_bf):
        p = self.p
        ce = st["ce"]
        cs = st["cs"]
        tt = st["tt"]
        p.tt(ce[0], cs[0], rstd64, ALU.mult)
        p.tt(ce[1], cs[1], rstd64, ALU.mult)
        p.stt(tt[0], ps_a, g_a, ce[0], ALU.mult, ALU.mult)
        p.stt(tt[1], ps_b, g_b, ce[1], ALU.mult, ALU.mult)
        p.tt(out_bf, tt[0], tt[1], ALU.add)

    def mla_block(self, L, st, tb, hT, W, tmp, ob_next):
        p = self.p
        tok = slice(tb * 512, (tb + 1) * 512)
        hsq, hln, hrs = tmp
        Wuq = self.WB[f"b_w_uq_{L}"]
        Wukv = self.WB[f"b_w_ukv_{L}"]
        wt = st["wt"]
        p.dma("sp", st["cs"][0].ap, self.ROPE[0][:, tok], w=[st["cs"][0]])
        p.dma("sp", st["cs"][1].ap, self.ROPE[1][:, tok], w=[st["cs"][1]])
        for which, c_base, gcols, dst in (("q", 0, st["gq"], st["cqn"]), ("kv", 512, st["gkv"], st["ckvn"])):
            for kc in range(4):
                w_ = wt[kc % 2]
                p.dma("sp", w_.ap, W[:, :, c_base + kc * 128:c_base + (kc + 1) * 128], w=[w_])
                ps = self.ps[1 + kc % 2]
                p.mm(ps, [(w_[:, k2, :], hT[k2]) for k2 in range(16)])
                p.copy(st["cqf"][kc], ps)
                p.act(hsq, ps, AF.Square)
                p.mm(self.ps[3], [(self.onesb, hsq)], first=(kc == 0), last=(kc == 3))
            p.act(hln, self.ps[3], AF.Ln, bias=self.eps_col[:, 0:1], scale=1.0 / 512)
            p.act(hrs, hln, AF.Exp, scale=-0.5)
            for kc in range(4):
                p.stt(dst[kc], st["cqf"][kc], gcols[:, kc:kc + 1], hrs, ALU.mult, ALU.mult)
        w_ = wt[0]
        w2 = wt[1]
        p.dma("sp", w_.ap[:, :, 0:64], W[:, :, 1024:1088], w=[w_])
        p.dma("sp", w2.ap[:, :, 0:32], W[:, :, 1056:1088], w=[w2])
        p.dma("sp", w2.ap[:, :, 32:64], W[:, :, 1024:1056], w=[w2])
        pa = self.ps[1]
        pb = self.ps[2]
        p.mm(pa[0:64, :], [(w_[:, k2, 0:64], hT[k2]) for k2 in range(16)])
        p.mm(pb[0:64, :], [(w2[:, k2, 0:64], hT[k2]) for k2 in range(16)])
        self.rstd_part(pa, 64, hsq, hln, hrs)
        o = ob_next()
        self.rope_apply(st, pa[0:64, :], pb[0:64, :], hrs[0:64, :], st["grs"][:, 2:3], st["grs"][:, 3:4], o[0:64, :])
        p.dma("sp", self.PT[48][0:64, tok], o.ap[0:64, :], r=[o])
        for h in range(16):
            wq = st["wq"][h % 2]
            p.dma("sp", wq.ap, Wuq[:, :, h * 192:h * 192 + 128], w=[wq])
            ps = self.ps[1 + h % 2]
            p.mm(ps, [(wq[:, kc, :], st["cqn"][kc]) for kc in range(4)])
            o = ob_next()
            self.headnorm((hsq, hln, hrs), ps, self.ps[3], st["gqn"][:, 0:1], o)
            p.dma("sp", self.PT[h][:, tok], o.ap, r=[o])
            wa = st["wq2"][0]
            wb = st["wq2"][1]
            c0 = h * 192 + 128
            p.dma("sp", wa.ap, Wuq[:, :, c0:c0 + 64], w=[wa])
            p.dma("sp", wb.ap[:, :, 0:32], Wuq[:, :, c0 + 32:c0 + 64], w=[wb])
            p.dma("sp", wb.ap[:, :, 32:64], Wuq[:, :, c0:c0 + 32], w=[wb])
            pa = self.ps[4]
            pb = self.ps[5]
            p.mm(pa[0:64, :], [(wa[:, kc, :], st["cqn"][kc]) for kc in range(4)])
            p.mm(pb[0:64, :], [(wb[:, kc, :], st["cqn"][kc]) for kc in range(4)])
            self.rstd_part(pa, 64, hsq, hln, hrs)
            o = ob_next()
            self.rope_apply(st, pa[0:64, :], pb[0:64, :], hrs[0:64, :], st["grs"][:, 0:1], st["grs"][:, 1:2],
                            o[0:64, :])
            p.dma("sp", self.PT[16 + h][0:64, tok], o.ap[0:64, :], r=[o])
            wq = st["wq"][(h + 1) % 2]
            p.dma("sp", wq.ap, Wukv[:, :, h * 256:h * 256 + 128], w=[wq])
            ps = self.ps[1 + (h + 1) % 2]
            p.mm(ps, [(wq[:, kc, :], st["ckvn"][kc]) for kc in range(4)])
            o = ob_next()
            self.headnorm((hsq, hln, hrs), ps, self.ps[3], st["gkn"][:, 1:2], o)
            p.dma("sp", self.PT[32 + h][:, tok], o.ap, r=[o])
        wv5 = Wukv.rearrange("p kc (h two d) -> p kc h two d", two=2, d=128)
        for g4 in range(4):
            wv = st["wv"][g4 % 2]
            wdst = wv.ap.rearrange("p kc (h d) -> p kc h d", d=128)
            for kc in range(4):
                p.dma("sp", wdst[:, kc], wv5[:, kc, g4 * 4:(g4 + 1) * 4, 1, :], w=[wv])
            for t in range(4):
                ps = self.ps[4 + t % 2]
                p.mm(ps, [(st["ckvn"][kc][:, t * 128:(t + 1) * 128], wv[:, kc, :]) for kc in range(4)])
                o = ob_next()
                p.copy(o, ps, E=("act" if t % 2 else "dve"))
                r0 = tb * 512 + t * 128
                p.dma("sp", self.VTM[r0:r0 + 128, g4 * 512:(g4 + 1) * 512], o.ap, r=[o])

    def rstd_part(self, ps_in, npart, sq, lnv, rstd):
        p = self.p
        p.act(sq[0:npart, :], ps_in[0:npart, :], AF.Square)
        p.mm(self.ps[3][0:npart, :], [(self.onesb[0:npart, 0:npart], sq[0:npart, :])])
        p.act(lnv[0:npart, :], self.ps[3][0:npart, :], AF.Ln, bias=self.eps_col[0:npart, 0:1], scale=1.0 / npart)
        p.act(rstd[0:npart, :], lnv[0:npart, :], AF.Exp, scale=-0.5)

    def mla_core(self, L):
        p = self.p
        ph = Phase(self.nc)
        cm = ph.tile([128, 4, 512], BF16, "cmask")
        p.dma("sp", cm.ap, self.inp["c_cmask"], w=[cm])
        kr = ph.tile([64, S], BF16, "krT")
        p.dma("sp", kr.ap, self.PT[48][0:64, :], w=[kr])
        qT = [ph.tile([128, S], BF16, "qT") for _ in range(2)]
        qR = [ph.tile([64, S], BF16, "qR") for _ in range(2)]
        kT = [ph.tile([128, S], BF16, "kT") for _ in range(2)]
        vt = [ph.tile([128, 32, 128], BF16, "vt") for _ in range(2)]
        ptt = [ph.tile([128, 512], BF16, "pt") for _ in range(2)]
        rec = ph.tile([128, 512], F32, "rec")
        ob = [ph.tile([128, 512], BF16, "ob") for _ in range(2)]
        ko = 0
        for h in range(16):
            q_ = qT[h % 2]
            r_ = qR[h % 2]
            k_ = kT[h % 2]
            v_ = vt[h % 2]
            p.dma("sp", q_.ap, self.PT[h], w=[q_])
            p.dma("sp", r_.ap, self.PT[16 + h][0:64, :], w=[r_])
            p.dma("sp", k_.ap, self.PT[32 + h], w=[k_])
            p.dma("sp", v_.ap, self.VTM[:, h * 128:(h + 1) * 128].rearrange("(j p) d -> p j d", p=128), w=[v_])
            for c in range(NSUB):
                qs = slice(c * 512, (c + 1) * 512)

                def qpairs(j, k_=k_, q_=q_, r_=r_, qs=qs):
                    ks = slice(j * 128, (j + 1) * 128)
                    return [(k_[:, ks], q_[:, qs]), (kr[:, ks], r_[:, qs])]

                def extra(j, c=c):
                    if j >= 4 * c:
                        return [(self.identb, cm[:, j - 4 * c, :])]
                    return []

                o = ob[ko % 2]
                ko += 1
                self.attn_chunk(4 * c + 4, lambda j, v_=v_: v_[:, j, :], qpairs, extra, lambda j: None, ptt,
                                (self.ps[0], self.ps[1]), self.ps[2], self.ps[3], rec, o)
                p.dma("sp", self.AT[h][:, qs], o.ap, r=[o])
        p.barrier()
        ph.close()

    def dsa_core(self, L):
        p = self.p
        nc = self.nc
        ph = Phase(nc)
        tab = ph.tile([32, 16], F32, "tab")
        p.dma("sp", tab.ap, self.inp["t5_table"], w=[tab])
        ohd = ph.tile([32, 2688], F32, "ohd")
        p.dma("sp", ohd.ap, self.inp["c_ohd"], w=[ohd])
        bv = ph.tile([16, 2688], F32, "bv")
        for i in range(6):
            w_ = min(512, 2688 - i * 512)
            ps = self.ps[i % 2]
            p.mm(ps[0:16, 0:w_], [(tab, ohd[:, i * 512:i * 512 + w_])])
            p.copy(bv[:, i * 512:i * 512 + w_], ps[0:16, 0:w_])
        p.dma("sp", self.BEXT, bv.ap, r=[bv])
        p.barrier()
        hk = [ph.tile([128, 2560], F32, "hk") for _ in range(2)]
        hkb = [ph.tile([128, 2560], BF16, "hkb") for _ in range(2)]
        tzb = [ph.tile([128, 2560], BF16, "tzb") for _ in range(2)]
        for h in range(16):
            a = hk[h % 2]
            b = hkb[h % 2]
            t = tzb[h % 2]
            p.dma("sp", a.ap, bass.AP(tensor=self.BEXT_h, offset=h * 2688, ap=[[1, 128], [1, 2560]]), w=[a])
            p.copy(b, a, E=("act" if h % 2 else "dve"))
            for i in range(5):
                ps = self.ps[2 + i % 2]
                p.mm(ps, [(self.antib, b[:, i * 512:(i + 1) * 512])])
                p.copy(t[:, i * 512:(i + 1) * 512], ps, E=("dve" if i % 2 else "act"))
            p.dma("sp", self.TZ[h], t.ap, r=[t])
        p.barrier()
        ph.close()
        ph = Phase(nc)
        cm = ph.tile([128, 4, 512], BF16, "cmask")
        p.dma("sp", cm.ap, self.inp["c_cmask"], w=[cm])
        sel = ph.tile([16, 16, 128], BF16, "sel16")
        p.dma("sp", sel.ap, self.inp["c_sel16"], w=[sel])
        kiT = ph.tile([64, S], BF16, "kiT")
        p.dma("sp", kiT.ap, self.PT[36][0:64, :], w=[kiT])
        wiT = ph.tile([16, 512], BF16, "wiT")
        idx = [ph.tile([128, 512], F32, "idx") for _ in range(32)]
        selb = [ph.tile([128, 512], BF16, "selb") for _ in range(32)]
        wrep = [ph.tile([128, 512], F32, "wrep") for _ in range(2)]
        qi = [ph.tile([64, 512], BF16, "qi") for _ in range(2)]
        tmp = [ph.tile([128, 512], F32, "itmp") for _ in range(2)]
        cmp = [ph.tile([128, 512], BF16, "cmp") for _ in range(2)]
        lo = ph.tile([128, 512], F32, "lo")
        mid = ph.tile([128, 512], F32, "mid")
        tsel = ph.tile([128, 512], F32, "tsel")
        tz = [ph.tile([128, 2560], BF16, "tz") for _ in range(2)]
        qT = [ph.tile([128, 512], BF16, "qT") for _ in range(2)]
        kT = [ph.tile([128, S], BF16, "kT") for _ in range(2)]
        vt = [ph.tile([128, 32, 128], BF16, "vt") for _ in range(2)]
        ptt = [ph.tile([128, 512], BF16, "pt") for _ in range(2)]
        rec = ph.tile([128, 512], F32, "rec")
        ob = [ph.tile([128, 512], BF16, "ob") for _ in range(2)]
        NIT = 24
        ko = 0
        kq = 0
        kg = 0
        for c in range(NSUB):
            qs = slice(c * 512, (c + 1) * 512)
            nk = 4 * c + 4
            p.dma("sp", wiT.ap, self.PT[37][0:16, qs], w=[wiT])
            for h in range(16):
                wr = wrep[h % 2]
                ps = self.ps[0]
                p.mm(ps, [(sel[:, h, :], wiT)])
                p.copy(wr, ps, E="act")
                q_ = qi[h % 2]
                p.dma("sp", q_.ap, self.PT[20 + h][0:64, qs], w=[q_])
                for j in range(nk):
                    ps = self.ps[1 + j % 2]
                    p.mm(ps, [(kiT[:, j * 128:(j + 1) * 128], q_)])
                    if h == 0:
                        p.stt(idx[j], ps, 0.0, wr, ALU.max, ALU.mult)
                    else:
                        t_ = tmp[j % 2]
                        p.stt(t_, ps, 0.0, wr, ALU.max, ALU.mult)
                        p.tt(idx[j], idx[j], t_, ALU.add, E="pool")
            for j in range(4 * c, nk):
                p.tt(idx[j], idx[j], cm[:, j - 4 * c, :], ALU.add, E="pool")
            p.op("dve", lambda e: e.memset(lo.ap, -64.0), w=[lo])
            for it in range(NIT):
                ck = 64.0 / (2 ** it)
                p.ts(mid, lo, ck, ALU.add)
                pc = self.ps[3 + it % 2]
                for j in range(nk):
                    cp = cmp[j % 2]
                    p.tt(cp, idx[j], mid, ALU.is_ge)
                    p.mm(pc, [(self.onesb, cp)], first=(j == 0), last=(j == nk - 1))
                p.ts(tsel, pc, 255.5, ALU.is_ge, ck, ALU.mult)
                p.tt(lo, lo, tsel, ALU.add)
            for j in range(nk):
                cp = tmp[j % 2]
                p.tt(cp, idx[j], lo, ALU.is_ge)
                p.ts(selb[j], cp, -1.0, ALU.add, -NEG, ALU.mult, E="pool")
            tzw = min(512 * c, 1664) + 896
            for g in range(4):
                k_ = kT[kg % 2]
                v_ = vt[kg % 2]
                kg += 1
                p.dma("sp", k_.ap[:, 0:nk * 128], self.PT[16 + g][:, 0:nk * 128], w=[k_])
                p.dma("sp", v_.ap[:, 0:nk, :],
                      self.VTM[0:nk * 128, g * 128:(g + 1) * 128].rearrange("(j p) d -> p j d", p=128), w=[v_])
                for r in range(4):
                    h = g * 4 + r
                    q_ = qT[kq % 2]
                    z_ = tz[kq % 2]
                    kq += 1
                    p.dma("sp", q_.ap, self.PT[h][:, qs], w=[q_])
                    p.dma("sp", z_.ap[:, 0:tzw], self.TZ[h][:, 0:tzw], w=[z_])

                    def qpairs(j, k_=k_, q_=q_):
                        return [(k_[:, j * 128:(j + 1) * 128], q_)]

                    def extra(j, c=c, z_=z_):
                        d0 = min(512 * c - 128 * j, 1664)
                        m0 = d0 + 384
                        return [(self.identb, z_[:, m0:m0 + 512]), (self.identb, selb[j])]

                    o = ob[ko % 2]
                    ko += 1
                    self.attn_chunk(nk, lambda j, v_=v_: v_[:, j, :], qpairs, extra, lambda j: None, ptt,
                                    (self.ps[5], self.ps[6]), self.ps[7], self.ps[0], rec, o)
                    p.dma("sp", self.AT[h][:, qs], o.ap, r=[o])
        p.barrier()
        ph.close()

    def out_proj(self, L):
        p = self.p
        ph = Phase(self.nc)
        W = self.WB[f"w_out_{L}"]
        m = L % 4
        nch = 20
        at = [[ph.tile([128, 512], BF16, "at") for _ in range(nch)] for _ in range(2)]
        wt = [ph.tile([128, 20, 128], BF16, "wo") for _ in range(2)]
        xs = [ph.tile([128, 512], F32, "xs") for _ in range(2)]
        xo = [ph.tile([128, 512], F32, "xo") for _ in range(2)]
        c_start = 0
        for tb in range(NSUB):
            tok = slice(tb * 512, (tb + 1) * 512)
            a = at[tb % 2]
            for c in range(c_start, nch):
                p.dma("sp", a[c].ap, self.AT[c][:, tok], w=[a[c]])
            for dc in range(16):
                w_ = wt[dc % 2]
                p.dma("sp", w_.ap, W[:, :, dc * 128:(dc + 1) * 128], w=[w_])
                ps = self.ps[dc % 2]
                p.mm(ps, [(w_[:, c, :], a[c]) for c in range(c_start, nch)])
                xt = xs[dc % 2]
                p.dma("sp", xt.ap, self.XT[dc][:, tok], w=[xt])
                o = xo[dc % 2]
                p.tt(o, ps, xt, ALU.add)
                p.dma("sp", self.XT[dc][:, tok], o.ap, r=[o])
        p.barrier()
        ph.close()

    def build(self):
        p = self.p
        self.declare()
        self.load_consts()
        self.eps_col = self.cst.tile([128, 1], F32, "eps")
        self.memT = [self.cst.tile([128, 256], F32, "memT") for _ in range(16)]
        self.mem_rstd = self.cst.tile([128, 256], F32, "memrstd")

        p.op("dve", lambda e: e.memset(self.eps_col.ap, EPS), w=[self.eps_col])
        if self.stop != "xt":
            self.convert_all()
        if self.stop not in ("xt", "cv", "ffn0"):
            self.mem_setup()
        import os
        if not os.environ.get("SKIP_XT"):
            self.x_to_xt()
        for L in (() if self.stop in ("xt", "cv") else self.layers):
            self.ffn(L, 0)
            if self.stop == "ffn0":
                break
            self.attention(L)
            if self.stop == f"att{L}":
                break
            self.ffn(L, 1)
            if self.stop == f"ffn1_{L}":
                break
        if not os.environ.get("SKIP_OUT"):
            self.xt_to_out()
        p.wait_all_dma("sp")
        return self.nc


def t5_bucket_np(dist):
    n = np.maximum(dist, 0)
    nf = np.maximum(n, 1).astype(np.float32)
    large = 16 + (np.log(nf / np.float32(16)) / np.float32(math.log(2048 / 16)) * np.float32(16)).astype(np.int32)
    large = np.minimum(large, 31)
    return np.where(n < 16, n, large)


def host_consts():
    bf = ml_dtypes.bfloat16
    c = {}
    c["c_ident"] = np.eye(128, dtype=np.float32)
    c["c_identb"] = np.eye(128, dtype=np.float32).astype(bf)
    c["c_antib"] = np.eye(128, dtype=np.float32)[::-1].copy().astype(bf)
    c["c_onesb"] = np.ones((128, 128), dtype=np.float32).astype(bf)
    k = np.arange(128)[:, None, None]
    o = np.arange(4)[None, :, None]
    q = np.arange(512)[None, None, :]
    c["c_cmask"] = np.where(128 * o + k <= q, 0.0, NEG).astype(np.float32).astype(bf)
    sel = np.zeros((16, 16, 128), dtype=np.float32)
    for h in range(16):
        sel[h, h, :] = 1.0
    c["c_sel16"] = sel.astype(bf)
    i = np.arange(2688)
    dist = i - 511
    oh = np.zeros((32, 2688), dtype=np.float32)
    b = t5_bucket_np(dist)
    valid = dist >= 0
    oh[b[valid], i[valid]] = 1.0
    c["c_ohd"] = oh
    ohg = np.zeros((3, 33, 384), dtype=np.float32)
    for g, dil in enumerate((1, 4, 16)):
        for ii in range(384):
            rel = ii - 127
            if 0 <= rel <= 128:
                ohg[g, int(t5_bucket_np(np.array(rel * dil))), ii] = 1.0
            else:
                ohg[g, 32, ii] = 1.0
    c["c_ohg"] = ohg
    half = 32
    inv = (np.float32(10000.0) ** (-np.arange(half, dtype=np.float32) / np.float32(half))).astype(np.float32)
    rope = np.zeros((64, 2), dtype=np.float32)
    rope[:, 0] = np.concatenate([inv, inv])
    rope[:, 1] = np.concatenate([-np.ones(32), np.ones(32)])
    c["c_rope"] = rope
    return c


def prep_inputs(inputs, b):
    f = lambda a: np.ascontiguousarray(a)
    m = {}
    m["x"] = f(inputs["x"][b])
    m["mem"] = f(inputs["mem"][b])
    m["positions"] = f(inputs["positions"][b].reshape(1, S).astype(np.int32))
    m["t5_table"] = f(inputs["t5_table"])
    m["ffn_norm"] = f(inputs["ffn_norm"].reshape(DEPTH * 2 * 16, 128))
    for n in ("ffn_w_gate", "ffn_w_up", "ffn_w_down", "mem_w_kv", "w_out", "a_w_in", "b_w_in", "b_w_uq",
              "b_w_ukv", "c_w_in", "d_w_in"):
        m[n] = inputs[n]
    m["attn_norm"] = f(inputs["attn_norm"].reshape(DEPTH * 16, 128))
    m["mem_norm"] = f(inputs["mem_norm"].reshape(DEPTH * 16, 128))
    m["mem_qk_g"] = f(inputs["mem_qk_g"].reshape(DEPTH * 2, 128))
    m["a_b_f"] = f(inputs["a_b_f"].reshape(16, 1))
    m["a_qk_g"] = f(inputs["a_qk_g"].reshape(2, 128))
    m["b_q_norm"] = f(inputs["b_q_norm"].reshape(4, 128))
    m["b_kv_norm"] = f(inputs["b_kv_norm"].reshape(4, 128))
    m["b_nope_g"] = f(inputs["b_nope_g"].reshape(2, 128))
    m["b_rope_g"] = f(inputs["b_rope_g"].reshape(2, 64))
    m["c_qk_g"] = f(inputs["c_qk_g"].reshape(6, 128))
    m["d_qk_g"] = f(inputs["d_qk_g"].reshape(2, 128))
    return m


_CACHE = {}


def kernel(**inputs):
    inputs = {k: np.asarray(v) for k, v in inputs.items()}
    if "nc" not in _CACHE:
        _CACHE["nc"] = K().build()
    nc = _CACHE["nc"]
    consts = host_consts()
    in_maps = []
    for b in range(8):
        m = prep_inputs(inputs, b)
        m.update(consts)
        in_maps.append(m)
    res = run_bass_kernel_spmd(nc, in_maps, core_ids=list(range(8)))
    out = np.stack([np.asarray(r["out"]) for r in res.results], axis=0)
    return out.astype(np.float32)
```

```python
import math
import os
import numpy as np
import ml_dtypes
import concourse.bass as bass
import concourse.mybir as mybir
from concourse.bass_utils import run_bass_kernel_spmd

F32 = mybir.dt.float32
BF16 = mybir.dt.bfloat16
I32 = mybir.dt.int32
AF = mybir.ActivationFunctionType
ALU = mybir.AluOpType
AX = mybir.AxisListType

S = 4096
D = 2048
DFF = 5632
NSUB = S // 512
DEPTH = 4
EPS = 1e-6
NEG = -30000.0


class T:
    __slots__ = ("ap", "w", "r", "psum")

    def __init__(self, ap, psum=False):
        self.ap = ap
        self.w = None
        self.r = {}
        self.psum = psum

    def __getitem__(self, idx):
        return V(self, self.ap[idx])


class V:
    __slots__ = ("t", "ap")

    def __init__(self, t, ap):
        self.t = t
        self.ap = ap

    def __getitem__(self, idx):
        return V(self.t, self.ap[idx])


def _tile(x):
    return x.t if isinstance(x, V) else x


class P:
    ENGS = ("pe", "act", "dve", "pool", "sp")

    def __init__(self, nc, n_dma_sems=8):
        self.nc = nc
        self.eng = {"pe": nc.tensor, "act": nc.scalar, "dve": nc.vector,
                    "pool": nc.gpsimd, "sp": nc.sync}
        self.sem = {}
        self.cnt = {}
        self.semid = {}
        self._nid = 0
        for e in self.ENGS:
            self._nid += 1
            self.sem[e] = nc.alloc_semaphore(f"s_{e}")
            self.cnt[e] = 0
            self.semid[e] = self._nid
        self.dsem = {}
        for q in ("sp", "act", "pool"):
            lst = []
            for i in range(n_dma_sems):
                self._nid += 1
                lst.append([nc.alloc_semaphore(f"d_{q}{i}"), 0, self._nid])
            self.dsem[q] = lst
        self.drr = {"sp": 0, "act": 0, "pool": 0}
        self.waited = {e: {} for e in self.ENGS}
        self.n_inst = 0
        self.n_wait = 0

    def _wait(self, E, tk, kind):
        if tk is None:
            return
        src, sid, sh, v = tk
        if src == E:
            if E == "pe" or kind != "raw":
                return
        wd = self.waited[E]
        if wd.get(sid, 0) >= v:
            return
        self.eng[E].wait_ge(sh, v)
        self.n_wait += 1
        wd[sid] = v

    def _deps(self, E, w, r):
        for x in r:
            t = _tile(x)
            if t is not None:
                self._wait(E, t.w, "raw")
                if t.psum:
                    for tk in t.r.values():
                        self._wait(E, tk, "war")
        for x in w:
            t = _tile(x)
            if t is not None:
                self._wait(E, t.w, "waw")
                for tk in t.r.values():
                    self._wait(E, tk, "war")

    def _mark(self, tk, w, r):
        for x in r:
            t = _tile(x)
            if t is not None:
                t.r[tk[1]] = tk
        for x in w:
            t = _tile(x)
            if t is not None:
                t.w = tk
                t.r = {}

    def op(self, E, fn, w=(), r=(), inc=True):
        self._deps(E, w, r)
        inst = fn(self.eng[E])
        self.n_inst += 1
        if inc:
            self.cnt[E] += 1
            inst.then_inc(self.sem[E], 1)
            tk = (E, self.semid[E], self.sem[E], self.cnt[E])
        else:
            tk = (E, self.semid[E], self.sem[E], self.cnt[E] + 1)
        self._mark(tk, w, r)
        return inst

    def dma(self, Q, out, in_, w=(), r=(), **kw):
        self._deps(Q, w, r)
        lst = self.dsem[Q]
        k = self.drr[Q]
        self.drr[Q] = (k + 1) % len(lst)
        ent = lst[k]
        if ent[1] > 0:
            self._wait(Q, ("dma", ent[2], ent[0], ent[1]), "raw")
        inst = self.eng[Q].dma_start(out=out, in_=in_, **kw)
        self.n_inst += 1
        ent[1] += 16
        inst.then_inc(ent[0], 16)
        tk = ("dma", ent[2], ent[0], ent[1])
        self._mark(tk, w, r)
        return tk

    def barrier(self):
        for E in self.ENGS:
            for E2 in self.ENGS:
                if E2 != E and self.cnt[E2] > 0:
                    self._wait(E, (E2, self.semid[E2], self.sem[E2], self.cnt[E2]), "raw")
            self.wait_all_dma(E)

    def wait_all_dma(self, E):
        for q in self.dsem:
            for ent in self.dsem[q]:
                if ent[1] > 0:
                    self._wait(E, ("dma", ent[2], ent[0], ent[1]), "raw")

    def mm(self, out, pairs, first=True, last=True):
        n = len(pairs)
        for i, (l, r_) in enumerate(pairs):
            st = first and i == 0
            sp = last and i == n - 1
            self.op("pe", lambda e, l=l, r_=r_, st=st, sp=sp: e.matmul(
                out.ap, l.ap, r_.ap, start=st, stop=sp), w=[out], r=[l, r_], inc=(i == n - 1))

    def act(self, out, in_, func, bias=None, scale=None, extra_r=(), accum=None):
        kw = {}
        rr = [in_] + list(extra_r)
        ww = [out]
        if bias is not None:
            if isinstance(bias, (V, T)):
                kw["bias"] = bias.ap
                rr.append(bias)
            else:
                kw["bias"] = bias
        if scale is not None:
            if isinstance(scale, (V, T)):
                kw["scale"] = scale.ap
                rr.append(scale)
            else:
                kw["scale"] = scale
        if accum is not None:
            kw["accum_out"] = accum.ap
            ww.append(accum)
        self.op("act", lambda e: e.activation(out.ap, in_.ap, func, **kw), w=ww, r=rr)

    def stt(self, out, in0, scalar, in1, op0, op1, E="dve"):
        rr = [in0, in1]
        sc = scalar
        if isinstance(scalar, (V, T)):
            sc = scalar.ap
            rr.append(scalar)
        self.op(E, lambda e: e.scalar_tensor_tensor(out.ap, in0.ap, sc, in1.ap, op0, op1), w=[out], r=rr)

    def tt(self, out, in0, in1, op, E="dve"):
        self.op(E, lambda e: e.tensor_tensor(out.ap, in0.ap, in1.ap, op), w=[out], r=[in0, in1])

    def ts(self, out, in0, s1, op0, s2=None, op1=None, E="dve"):
        rr = [in0]
        a1 = s1
        if isinstance(s1, (V, T)):
            a1 = s1.ap
            rr.append(s1)
        a2 = s2
        if isinstance(s2, (V, T)):
            a2 = s2.ap
            rr.append(s2)
        if op1 is None:
            self.op(E, lambda e: e.tensor_scalar(out.ap, in0.ap, a1, None, op0), w=[out], r=rr)
        else:
            self.op(E, lambda e: e.tensor_scalar(out.ap, in0.ap, a1, a2, op0, op1), w=[out], r=rr)

    def copy(self, out, in_, E="dve"):
        if E == "act":
            self.op("act", lambda e: e.activation(out.ap, in_.ap, AF.Copy), w=[out], r=[in_])
        else:
            self.op(E, lambda e: e.tensor_copy(out.ap, in_.ap), w=[out], r=[in_])


class Phase:
    _uid = 0

    def __init__(self, nc):
        from contextlib import ExitStack
        self.nc = nc
        self.st = ExitStack()
        self.k = 0

    def tile(self, shape, dt, name="t"):
        Phase._uid += 1
        h = self.st.enter_context(self.nc.sbuf_tensor(f"{name}_{Phase._uid}", list(shape), dt))
        return T(h.ap())

    def close(self):
        self.st.close()


WSPECS = {
    "ffn_w_gate": (D, DFF), "ffn_w_up": (D, DFF), "ffn_w_down": (DFF, D),
    "mem_w_kv": (D, 1024), "w_out": (2560, D),
    "a_w_in": (D, 6672), "b_w_in": (D, 1600), "b_w_uq": (512, 3072), "b_w_ukv": (512, 4096),
    "c_w_in": (D, 18944), "d_w_in": (D, 4688),
}


class K:
    def __init__(self, layers=(0, 1, 2, 3), stop=None, dbg=()):
        self.layers = layers
        self.stop = stop
        self.dbg = dbg
        nc = bass.Bass("TRN2", target_bir_lowering=False)
        self.nc = nc
        self.p = P(nc)
        self.inp = {}
        self.dbg_out = {}

    def din(self, name, shape, dt=F32):
        if self.stop == "xt" and (name in WSPECS):
            return None
        self.inp[name] = self.nc.dram_tensor(name, list(shape), dt, kind="ExternalInput").ap()
        return self.inp[name]

    def dscr(self, name, shape, dt):
        return self.nc.dram_tensor(name, list(shape), dt, kind="Internal").ap()

    def declare(self):
        nc = self.nc
        self.din("x", [S, D])
        self.din("mem", [256, D])
        self.din("positions", [1, S], I32)
        self.din("t5_table", [32, 16])
        self.din("ffn_norm", [DEPTH * 2 * 16, 128])
        for n in ("ffn_w_gate", "ffn_w_up", "ffn_w_down"):
            k, c = WSPECS[n]
            self.din(n, [DEPTH, 2, k, c])
        self.din("attn_norm", [DEPTH * 16, 128])
        self.din("mem_norm", [DEPTH * 16, 128])
        self.din("mem_w_kv", [DEPTH, D, 1024])
        self.din("mem_qk_g", [DEPTH * 2, 128])
        self.din("w_out", [DEPTH, 2560, D])
        self.din("a_w_in", [1, D, 6672])
        self.din("a_b_f", [16, 1])
        self.din("a_qk_g", [2, 128])
        self.din("b_w_in", [1, D, 1600])
        self.din("b_q_norm", [4, 128])
        self.din("b_w_uq", [1, 512, 3072])
        self.din("b_kv_norm", [4, 128])
        self.din("b_w_ukv", [1, 512, 4096])
        self.din("b_nope_g", [2, 128])
        self.din("b_rope_g", [2, 64])
        self.din("c_w_in", [1, D, 18944])
        self.din("c_qk_g", [6, 128])
        self.din("d_w_in", [1, D, 4688])
        self.din("d_qk_g", [2, 128])
        self.din("c_ident", [128, 128])
        self.din("c_identb", [128, 128], BF16)
        self.din("c_antib", [128, 128], BF16)
        self.din("c_onesb", [128, 128], BF16)
        self.din("c_cmask", [128, 4, 512], BF16)
        self.din("c_sel16", [16, 16, 128], BF16)
        self.din("c_sel80", [80, 16, 128], BF16)
        self.din("c_ohd", [32, 2688])
        self.din("c_ohg", [3, 33, 384])
        self.din("c_rope", [64, 2])
        self.out = nc.dram_tensor("out", [S, D], F32, kind="ExternalOutput").ap()
        self.XT = self.dscr("XT", [16, 128, S], F32)
        self.PT = self.dscr("PTs", [112, 128, S], BF16)
        self.VTM = self.dscr("VTM", [S, 6144], BF16)
        self.AT = self.dscr("ATs", [20, 128, S], BF16)
        self.FG = self.dscr("FGs", [16, S], F32)
        self.ROPE = self.dscr("ROPEs", [2, 64, S], F32)
        self.BEXT_h = self.nc.dram_tensor("BEXTs", [16, 2688], F32, kind="Internal")
        self.BEXT = self.BEXT_h.ap()
        self.TZ = self.dscr("TZs", [16, 128, 2560], BF16)
        self.VEXT_h = self.nc.dram_tensor("VEXTs", [48, 512], F32, kind="Internal")
        self.VEXT = self.VEXT_h.ap()
        self.WB = {}

    def dbgout(self, name, src_ap, shape, dt):
        o = self.nc.dram_tensor("dbg_" + name, list(shape), dt, kind="ExternalOutput").ap()
        self.p.barrier()
        self.p.dma("sp", o, src_ap)
        self.p.barrier()

    def load_consts(self):
        p = self.p
        nc = self.nc
        self.cst = Phase(nc)
        c = self.cst
        self.ident = c.tile([128, 128], F32, "ident")
        self.identb = c.tile([128, 128], BF16, "identb")
        self.antib = c.tile([128, 128], BF16, "antib")
        self.onesb = c.tile([128, 128], BF16, "onesb")
        for t, n in ((self.ident, "c_ident"), (self.identb, "c_identb"), (self.antib, "c_antib"),
                     (self.onesb, "c_onesb")):
            p.dma("sp", t.ap, self.inp[n], w=[t])
        self.ps = [T(nc.alloc_psum_tensor(f"ps{i}", [128, 512], F32).ap(), psum=True) for i in range(8)]
        self.g_ffn = self.colvecs("ffn_norm", 128)
        self.g_attn = self.colvecs("attn_norm", 64)
        self.g_mem = self.colvecs("mem_norm", 64)
        self.g_memqk = self.colvecs("mem_qk_g", 8)

    def colvecs(self, name, n):
        p = self.p
        c = self.cst
        st = c.tile([128, 128], F32, "cvst")
        out = c.tile([128, n], F32, "cv")
        p.dma("sp", st.ap[0:n, :], self.inp[name], w=[st])
        ps = self.ps[7]
        p.op("pe", lambda e: e.transpose(ps.ap[:, 0:n], st.ap[0:n, :], self.ident.ap[0:n, 0:n]),
             w=[ps], r=[st, self.ident])
        p.copy(out, ps[:, 0:n])
        return out

    def convert(self, key, src, Kdim, C):
        p = self.p
        KC = Kdim // 128
        dst = self.dscr("WB_" + key, [128, KC, C], BF16)
        self.WB[key] = dst
        CW = 2048
        jobs = [(kc, c0, min(CW, C - c0)) for kc in range(KC) for c0 in range(0, C, CW)]
        ph = self.cvph
        engs = ("pool", "act", "dve")
        for i, (kc, c0, cw) in enumerate(jobs):
            st = self.cv_st[i % 3]
            bf = self.cv_bf[i % 3]
            p.dma("sp", st.ap[:, 0:cw], src[kc * 128:(kc + 1) * 128, c0:c0 + cw], w=[st])
            E = engs[i % 3]
            p.copy(bf[:, 0:cw], st[:, 0:cw], E=E)
            p.dma("pool" if False else "sp", dst[:, kc, c0:c0 + cw], bf.ap[:, 0:cw], r=[bf])
        return dst

    def convert_all(self):
        nc = self.nc
        self.cvph = Phase(nc)
        self.cv_st = [self.cvph.tile([128, 2048], F32, "cvs") for _ in range(3)]
        self.cv_bf = [self.cvph.tile([128, 2048], BF16, "cvb") for _ in range(3)]
        mixw = {0: [("a_w_in", 0)], 1: [("b_w_in", 0), ("b_w_uq", 0), ("b_w_ukv", 0)],
                2: [("c_w_in", 0)], 3: [("d_w_in", 0)]}
        for L in self.layers:
            for s in (0, 1):
                for n in ("ffn_w_gate", "ffn_w_up", "ffn_w_down"):
                    k, c = WSPECS[n]
                    self.convert(f"{n}_{L}_{s}", self.inp[n][L, s], k, c)
            if self.stop == "ffn0":
                break
            self.convert(f"mem_w_kv_{L}", self.inp["mem_w_kv"][L], D, 1024)
            self.convert(f"w_out_{L}", self.inp["w_out"][L], 2560, D)
            for n, j in mixw[L % 4]:
                k, c = WSPECS[n]
                self.convert(f"{n}_{L}", self.inp[n][j], k, c)
        self.p.barrier()
        self.cvph.close()

    def x_to_xt(self):
        p = self.p
        ph = Phase(self.nc)
        xt = [ph.tile([128, D], F32, "xin") for _ in range(8)]
        ob = [ph.tile([128, 512], F32, "xo") for _ in range(3)]
        k = 0
        for tb in range(NSUB):
            tiles = []
            for j in range(4):
                t = xt[(tb % 2) * 4 + j]
                r0 = tb * 512 + j * 128
                p.dma("sp", t.ap, self.inp["x"][r0:r0 + 128, :], w=[t])
                tiles.append(t)
            for dc in range(16):
                ps = self.ps[dc % 4]
                for j in range(4):
                    p.op("pe", lambda e, j=j, ps=ps, dc=dc: e.transpose(
                        ps.ap[:, j * 128:(j + 1) * 128], tiles[j].ap[:, dc * 128:(dc + 1) * 128], self.ident.ap),
                        w=[ps], r=[tiles[j], self.ident], inc=(j == 3))
                o = ob[k % 3]
                k += 1
                p.copy(o, ps, E=("dve" if dc % 2 == 0 else "act"))
                p.dma("sp", self.XT[dc][:, tb * 512:(tb + 1) * 512], o.ap, r=[o])
        p.barrier()
        ph.close()

    def xt_to_out(self):
        p = self.p
        ph = Phase(self.nc)
        xin = [ph.tile([128, 512], F32, "xi") for _ in range(6)]
        ot = [ph.tile([128, D], F32, "xo") for _ in range(8)]
        k = 0
        for tb in range(NSUB):
            outs = [ot[(tb % 2) * 4 + j] for j in range(4)]
            for dc in range(16):
                xi = xin[k % 6]
                k += 1
                p.dma("sp", xi.ap, self.XT[dc][:, tb * 512:(tb + 1) * 512], w=[xi])
                ps = self.ps[dc % 4]
                for j in range(4):
                    p.op("pe", lambda e, j=j, ps=ps, xi=xi: e.transpose(
                        ps.ap[:, j * 128:(j + 1) * 128], xi.ap[:, j * 128:(j + 1) * 128], self.ident.ap),
                        w=[ps], r=[xi, self.ident], inc=(j == 3))
                for j in range(4):
                    p.copy(outs[j][:, dc * 128:(dc + 1) * 128], ps[:, j * 128:(j + 1) * 128],
                           E=("dve" if (j % 2 == 0 or os.environ.get("XO_DVE")) else "act"))
            for j in range(4):
                r0 = tb * 512 + j * 128
                p.dma("sp", self.out[r0:r0 + 128, :], outs[j].ap, r=[outs[j]])
        p.barrier()
        ph.close()

    def rstd_from_ssq(self, ps_ssq, n_feat, lnv, rstd):
        p = self.p
        p.act(lnv, ps_ssq, AF.Ln, bias=self.eps_col[:, 0:1], scale=1.0 / n_feat)
        p.act(rstd, lnv, AF.Exp, scale=-0.5)

    def ffn(self, L, s):
        p = self.p
        nc = self.nc
        ph = Phase(nc)
        Wg = self.WB[f"ffn_w_gate_{L}_{s}"]
        Wu = self.WB[f"ffn_w_up_{L}_{s}"]
        Wd = self.WB[f"ffn_w_down_{L}_{s}"]
        gcol = self.g_ffn
        gbase = (L * 2 + s) * 16
        xnT = [[ph.tile([128, 512], BF16, "xn") for _ in range(16)] for _ in range(2)]
        hT = [[ph.tile([128, 512], BF16, "h") for _ in range(44)] for _ in range(2)]
        xs = [ph.tile([128, 512], F32, "xs") for _ in range(4)]
        sq = [ph.tile([128, 512], BF16, "sq") for _ in range(2)]
        lnv = ph.tile([128, 512], F32, "lnv")
        rstd = ph.tile([128, 512], F32, "rstd")
        wg = [ph.tile([128, 16, 128], BF16, "wg") for _ in range(2)]
        wu = [ph.tile([128, 16, 128], BF16, "wu") for _ in range(2)]
        wd = [ph.tile([128, 44, 128], BF16, "wd") for _ in range(2)]
        sg = [ph.tile([128, 512], F32, "sg") for _ in range(2)]
        xo = [ph.tile([128, 512], F32, "xo") for _ in range(2)]
        ps = self.ps
        kx = 0
        for tb in range(S // 1024):
            for sub in range(2):
                tok = slice((2 * tb + sub) * 512, (2 * tb + sub + 1) * 512)
                for kc in range(16):
                    xt = xs[kx % 4]
                    kx += 1
                    p.dma("sp", xt.ap, self.XT[kc][:, tok], w=[xt])
                    q = sq[kc % 2]
                    p.act(q, xt, AF.Square)
                    p.mm(ps[0], [(self.onesb, q)], first=(kc == 0), last=(kc == 15))
                self.rstd_from_ssq(ps[0], D, lnv, rstd)
                for kc in range(16):
                    xt = xs[kx % 4]
                    kx += 1
                    p.dma("sp", xt.ap, self.XT[kc][:, tok], w=[xt])
                    p.stt(xnT[sub][kc], xt, gcol[:, gbase + kc:gbase + kc + 1], rstd, ALU.mult, ALU.mult)
            for fc in range(44):
                a = wg[fc % 2]
                b = wu[fc % 2]
                p.dma("sp", a.ap, Wg[:, :, fc * 128:(fc + 1) * 128], w=[a])
                p.dma("sp", b.ap, Wu[:, :, fc * 128:(fc + 1) * 128], w=[b])
                for sub in range(2):
                    pg = ps[1 + sub * 2]
                    pu = ps[2 + sub * 2]
                    p.mm(pg, [(a[:, kc, :], xnT[sub][kc]) for kc in range(16)])
                    p.mm(pu, [(b[:, kc, :], xnT[sub][kc]) for kc in range(16)])
                    g_ = sg[sub]
                    p.act(g_, pg, AF.Silu)
                    p.tt(hT[sub][fc], g_, pu, ALU.mult)
            for dc in range(16):
                w_ = wd[dc % 2]
                p.dma("sp", w_.ap, Wd[:, :, dc * 128:(dc + 1) * 128], w=[w_])
                for sub in range(2):
                    tok = slice((2 * tb + sub) * 512, (2 * tb + sub + 1) * 512)
                    py = ps[5 + sub]
                    p.mm(py, [(w_[:, fc, :], hT[sub][fc]) for fc in range(44)])
                    xt = xs[kx % 4]
                    kx += 1
                    p.dma("sp", xt.ap, self.XT[dc][:, tok], w=[xt])
                    o = xo[sub]
                    p.stt(o, py, 0.5, xt, ALU.mult, ALU.add)
                    p.dma("sp", self.XT[dc][:, tok], o.ap, r=[o])
        p.barrier()
        ph.close()

    def headnorm(self, ph_tiles, ps_in, ps_ssq, gcol, out_bf, nfeat=128, npart=128):
        p = self.p
        sq, lnv, rstd = ph_tiles
        p.act(sq[0:npart, :], ps_in[0:npart, :], AF.Square)
        p.mm(ps_ssq[0:npart, :], [(self.onesb[0:npart, 0:npart], sq[0:npart, :])])
        p.act(lnv[0:npart, :], ps_ssq[0:npart, :], AF.Ln, bias=self.eps_col[0:npart, 0:1], scale=1.0 / nfeat)
        p.act(rstd[0:npart, :], lnv[0:npart, :], AF.Exp, scale=-0.5)
        p.stt(out_bf, ps_in[0:npart, :], gcol, rstd[0:npart, :], ALU.mult, ALU.mult)

    def mem_setup(self):
        p = self.p
        ph = Phase(self.nc)
        mt = [ph.tile([128, D], F32, "memin") for _ in range(2)]
        sq = [ph.tile([128, 256], BF16, "msq") for _ in range(2)]
        lnv = ph.tile([128, 256], F32, "mln")
        for j in range(2):
            p.dma("sp", mt[j].ap, self.inp["mem"][j * 128:(j + 1) * 128, :], w=[mt[j]])
        for kc in range(16):
            ps = self.ps[kc % 2]
            for j in range(2):
                p.op("pe", lambda e, j=j, ps=ps, kc=kc: e.transpose(
                    ps.ap[:, j * 128:(j + 1) * 128], mt[j].ap[:, kc * 128:(kc + 1) * 128], self.ident.ap),
                    w=[ps], r=[mt[j], self.ident], inc=(j == 1))
            p.copy(self.memT[kc], ps[:, 0:256])
            q = sq[kc % 2]
            p.act(q, self.memT[kc], AF.Square)
            p.mm(self.ps[2][:, 0:256], [(self.onesb, q)], first=(kc == 0), last=(kc == 15))
        p.act(lnv, self.ps[2][:, 0:256], AF.Ln, bias=self.eps_col[:, 0:1], scale=1.0 / D)
        p.act(self.mem_rstd, lnv, AF.Exp, scale=-0.5)
        p.barrier()
        ph.close()

    def mem_kv(self, L, ph):
        p = self.p
        W = self.WB[f"mem_w_kv_{L}"]
        memn = [ph.tile([128, 256], BF16, "memn") for _ in range(16)]
        for kc in range(16):
            p.stt(memn[kc], self.memT[kc], self.g_mem[:, L * 16 + kc:L * 16 + kc + 1], self.mem_rstd,
                  ALU.mult, ALU.mult)
        kmT = [ph.tile([128, 256], BF16, "kmT") for _ in range(4)]
        vm = [ph.tile([128, 512], BF16, "vm") for _ in range(2)]
        wk = [ph.tile([128, 16, 128], BF16, "wmk") for _ in range(2)]
        wv = ph.tile([128, 16, 512], BF16, "wmv")
        sq = ph.tile([128, 512], BF16, "hsq")
        lnv = ph.tile([128, 512], F32, "hln")
        rstd = ph.tile([128, 512], F32, "hrs")
        for j in range(4):
            w_ = wk[j % 2]
            p.dma("sp", w_.ap, W[:, :, j * 128:(j + 1) * 128], w=[w_])
            ps = self.ps[j % 2]
            p.mm(ps[:, 0:256], [(w_[:, kc, :], memn[kc]) for kc in range(16)])
            self.headnorm((sq[:, 0:256], lnv[:, 0:256], rstd[:, 0:256]), ps[:, 0:256], self.ps[2][:, 0:256],
                          self.g_memqk[:, L * 2 + 1:L * 2 + 2], kmT[j])
        p.dma("sp", wv.ap, W[:, :, 512:1024], w=[wv])
        for t in range(2):
            ps = self.ps[3 + t]
            p.mm(ps, [(memn[kc][:, t * 128:(t + 1) * 128], wv[:, kc, :]) for kc in range(16)])
            p.copy(vm[t], ps)
        return kmT, vm

    def attn_chunk(self, ktiles, vtiles, qpairs_fn, extra_fn, bias_fn, pt_tiles, ps_s, ps_o, ps_d, rec, out_bf,
                   dv=128, nq=512):
        p = self.p
        n = ktiles

        def scores(j):
            p.mm(ps_s[j % 2][:, 0:nq], list(qpairs_fn(j)) + list(extra_fn(j)))

        scores(0)
        for j in range(n):
            pt = pt_tiles[j % 2]
            p.act(pt[:, 0:nq], ps_s[j % 2][:, 0:nq], AF.Exp, bias=bias_fn(j))
            if j + 1 < n:
                scores(j + 1)
            p.mm(ps_o[0:dv, 0:nq], [(vtiles(j), pt[:, 0:nq])], first=(j == 0), last=(j == n - 1))
            p.mm(ps_d[0:dv, 0:nq], [(self.onesb[:, 0:dv], pt[:, 0:nq])], first=(j == 0), last=(j == n - 1))
        p.op("dve", lambda e: e.reciprocal(rec.ap[0:dv, 0:nq], ps_d.ap[0:dv, 0:nq]), w=[rec], r=[ps_d])
        p.tt(out_bf, ps_o[0:dv, 0:nq], rec[0:dv, 0:nq], ALU.mult)

    def attention(self, L):
        m = L % 4
        if m == 1:
            self.rope_tables()
        self.in_proj(L)
        if m == 0:
            self.fox_core(L)
        elif m == 1:
            self.mla_core(L)
        elif m == 2:
            self.dil_core(L)
        elif m == 3:
            self.dsa_core(L)
        self.out_proj(L)

    def in_proj(self, L):
        p = self.p
        nc = self.nc
        m = L % 4
        ph = Phase(nc)
        wkey = {0: "a_w_in", 1: "b_w_in", 2: "c_w_in", 3: "d_w_in"}[m]
        W = self.WB[f"{wkey}_{L}"]
        ncols = WSPECS[wkey][1]
        memq0 = ncols - 512
        kmT, vm = self.mem_kv(L, ph)
        hT = [ph.tile([128, 512], BF16, "hT") for _ in range(16)]
        xs = [ph.tile([128, 512], F32, "xs") for _ in range(3)]
        sqx = [ph.tile([128, 512], BF16, "sqx") for _ in range(2)]
        lnv = ph.tile([128, 512], F32, "lnv")
        rstd = ph.tile([128, 512], F32, "rstd")
        hsq = ph.tile([128, 512], BF16, "hsq2")
        hsq2 = [hsq, ph.tile([128, 512], BF16, "hsq3")]
        hln = ph.tile([128, 512], F32, "hln2")
        hrs = ph.tile([128, 512], F32, "hrs2")
        wt = [ph.tile([128, 16, 128], BF16, "wt") for _ in range(2)]
        wvt = [ph.tile([128, 16, 512], BF16, "wvt") for _ in range(2)] if m in (0, 2, 3) else None
        ob = [ph.tile([128, 512], BF16, "ob") for _ in range(3)]
        qm = [ph.tile([128, 512], BF16, "qm") for _ in range(2)]
        ptt = [ph.tile([128, 512], BF16, "ptm") for _ in range(2)]
        rec = ph.tile([128, 512], F32, "rec")
        jobs = []
        vjobs = []
        if m == 0:
            self.gq_a = self.colv_small("a_qk_g", 2, ph)
            gq = ph.tile([128, 1], F32, "gqs")
            p.ts(gq, self.gq_a[:, 0:1], 128 ** -0.5, ALU.mult)
            for h in range(16):
                jobs.append((h * 128, 128, gq[:, 0:1], h))
            for h in range(16):
                jobs.append((2048 + h * 128, 128, self.gq_a[:, 1:2], 16 + h))
            for g in range(4):
                vjobs.append((4096 + g * 512, g * 512))
        if m == 2:
            self.gq_c = self.colv_small("c_qk_g", 6, ph)
            gqc = ph.tile([128, 3], F32, "gqc")
            for g in range(3):
                p.ts(gqc[:, g:g + 1], self.gq_c[:, 2 * g:2 * g + 1], 128 ** -0.5, ALU.mult)
            for g in range(3):
                for h in range(16):
                    jobs.append((g * 6144 + h * 128, 128, gqc[:, g:g + 1], g * 32 + h))
                    jobs.append((g * 6144 + 2048 + h * 128, 128, self.gq_c[:, 2 * g + 1:2 * g + 2], g * 32 + 16 + h))
                for v4 in range(4):
                    vjobs.append((g * 6144 + 4096 + v4 * 512, g * 2048 + v4 * 512))
        if m == 3:
            self.gq_d = self.colv_small("d_qk_g", 2, ph)
            gqd = ph.tile([128, 1], F32, "gqd")
            p.ts(gqd, self.gq_d[:, 0:1], 128 ** -0.5, ALU.mult)
            for h in range(16):
                jobs.append((h * 128, 128, gqd[:, 0:1], h))
            for g in range(4):
                jobs.append((2048 + g * 128, 128, self.gq_d[:, 1:2], 16 + g))
            vjobs.append((2560, 0))
            for h in range(16):
                jobs.append((3072 + h * 64, 64, 64 ** -0.5, 20 + h))
            jobs.append((4096, 64, 1.0, 36))
            jobs.append((4160, 16, 16 ** -0.5, 37))
        mla = None
        if m == 1:
            mla = self.mla_setup(L, ph)
            mla["wt"] = wt
        gmq = ph.tile([128, 1], F32, "gmq")
        p.ts(gmq, self.g_memqk[:, L * 2:L * 2 + 1], 128 ** -0.5, ALU.mult)
        kx = 0
        ko = 0
        for tb in range(NSUB):
            tok = slice(tb * 512, (tb + 1) * 512)
            for kc in range(16):
                xt = xs[kx % 3]
                kx += 1
                p.dma("sp", xt.ap, self.XT[kc][:, tok], w=[xt])
                q = sqx[kc % 2]
                p.act(q, xt, AF.Square)
                p.mm(self.ps[0], [(self.onesb, q)], first=(kc == 0), last=(kc == 15))
            self.rstd_from_ssq(self.ps[0], D, lnv, rstd)
            for kc in range(16):
                xt = xs[kx % 3]
                kx += 1
                p.dma("sp", xt.ap, self.XT[kc][:, tok], w=[xt])
                p.stt(hT[kc], xt, self.g_attn[:, L * 16 + kc:L * 16 + kc + 1], rstd, ALU.mult, ALU.mult)
            pend = None

            def stage_b(st):
                nonlocal ko
                ji, cw, gcol, chunk, ps = st
                o = ob[ko % 3]
                ko += 1
                if isinstance(gcol, float):
                    p.ts(o[0:cw, :], ps[0:cw, :], gcol, ALU.mult)
                else:
                    q = hsq2[ji % 2]
                    p.mm(self.ps[3][0:cw, :], [(self.onesb[0:cw, 0:cw], q[0:cw, :])])
                    p.act(hln[0:cw, :], self.ps[3][0:cw, :], AF.Ln, bias=self.eps_col[0:cw, 0:1], scale=1.0 / cw)
                    p.act(hrs[0:cw, :], hln[0:cw, :], AF.Exp, scale=-0.5)
                    p.stt(o[0:cw, :], ps[0:cw, :], gcol, hrs[0:cw, :], ALU.mult, ALU.mult)
                p.dma("sp", self.PT[chunk][0:cw, tok], o.ap[0:cw, :], r=[o])

            for ji, (c0, cw, gcol, chunk) in enumerate(jobs):
                w_ = wt[ji % 2]
                p.dma("sp", w_.ap[:, :, 0:cw], W[:, :, c0:c0 + cw], w=[w_])
                ps = self.ps[1 + ji % 2]
                p.mm(ps[0:cw, :], [(w_[:, kc, 0:cw], hT[kc]) for kc in range(16)])
                if not isinstance(gcol, float):
                    p.act(hsq2[ji % 2][0:cw, :], ps[0:cw, :], AF.Square)
                if pend is not None:
                    stage_b(pend)
                pend = (ji, cw, gcol, chunk, ps)
            if pend is not None:
                stage_b(pend)
            if m == 1:
                def ob_next():
                    nonlocal ko
                    o_ = ob[ko % 3]
                    ko += 1
                    return o_
                self.mla_block(L, mla, tb, hT, W, (hsq, hln, hrs), ob_next)
            if m == 0:
                w_ = wt[0]
                p.dma("sp", w_.ap[:, :, 0:16], W[:, :, 6144:6160], w=[w_])
                ps = self.ps[1]
                p.mm(ps[0:16, :], [(w_[:, kc, 0:16], hT[kc]) for kc in range(16)])
                o32 = xs[kx % 3]
                kx += 1
                p.copy(o32[0:16, :], ps[0:16, :])
                p.dma("sp", self.FG[:, tok], o32.ap[0:16, :], r=[o32])
            for vi, (c0, v0) in enumerate(vjobs):
                w_ = wvt[vi % 2]
                p.dma("sp", w_.ap, W[:, :, c0:c0 + 512], w=[w_])
                for t in range(4):
                    ps = self.ps[4 + t % 2]
                    p.mm(ps, [(hT[kc][:, t * 128:(t + 1) * 128], w_[:, kc, :]) for kc in range(16)])
                    o = ob[ko % 3]
                    ko += 1
                    p.copy(o, ps, E=("act" if t % 2 else "dve"))
                    r0 = tb * 512 + t * 128
                    p.dma("sp", self.VTM[r0:r0 + 128, v0:v0 + 512], o.ap, r=[o])
            for j in range(4):
                w_ = wt[j % 2]
                c0 = memq0 + j * 128
                p.dma("sp", w_.ap, W[:, :, c0:c0 + 128], w=[w_])
                ps = self.ps[1 + j % 2]
                p.mm(ps, [(w_[:, kc, :], hT[kc]) for kc in range(16)])
                q_ = qm[j % 2]
                self.headnorm((hsq, hln, hrs), ps, self.ps[3], gmq[:, 0:1], q_)
                o = ob[ko % 3]
                ko += 1
                self.attn_chunk(
                    2, lambda t, j=j: vm[t][:, j * 128:(j + 1) * 128],
                    lambda t, j=j, q_=q_: [(kmT[j][:, t * 128:(t + 1) * 128], q_)],
                    lambda t: [], lambda t: None, ptt, (self.ps[4], self.ps[5]), self.ps[6], self.ps[7], rec, o)
                p.dma("sp", self.AT[16 + j][:, tok], o.ap, r=[o])
        p.barrier()
        ph.close()

    def colv_small(self, name, n, ph):
        p = self.p
        w = self.inp[name].shape[1]
        st = ph.tile([128, 128], F32, "cvs")
        out = ph.tile([128, n], F32, "cvo")
        p.dma("sp", st.ap[0:n, 0:w], self.inp[name], w=[st])
        ps = self.ps[7]
        p.op("pe", lambda e: e.transpose(ps.ap[0:w, 0:n], st.ap[0:n, 0:w], self.ident.ap[0:n, 0:n]),
             w=[ps], r=[st, self.ident])
        p.copy(out[0:w, :], ps[0:w, 0:n])
        return out

    def fox_core(self, L):
        p = self.p
        nc = self.nc
        ph = Phase(nc)
        fg = ph.tile([80, S], F32, "fg")
        negbf = ph.tile([80, 1], F32, "negbf")
        p.op("pool", lambda e: e.memset(fg.ap, 0.0), w=[fg])
        p.op("pool", lambda e: e.memset(negbf.ap, 0.0), w=[negbf])
        for i in range(3):
            p.dma("sp", fg.ap[32 * i:32 * i + 16, :], self.FG, w=[fg])
            p.dma("sp", negbf.ap[32 * i:32 * i + 16, :], self.inp["a_b_f"], w=[negbf])
        p.ts(negbf, negbf, -1.0, ALU.mult)
        ones16 = ph.tile([80, S], F32, "ones16")
        p.op("pool", lambda e: e.memset(ones16.ap, 1.0), w=[ones16])
        lf = ph.tile([80, S], F32, "lf")
        ncum = ph.tile([80, S], F32, "ncum")
        p.act(lf, fg, AF.Exp, bias=negbf[:, 0:1], scale=-1.0)
        p.act(lf, lf, AF.Ln, bias=1.0)
        p.op("dve", lambda e: e.tensor_tensor_scan(ncum.ap, ones16.ap, lf.ap, 0.0, ALU.mult, ALU.add),
             w=[ncum], r=[ones16, lf])
        c_hi = ph.tile([80, S], BF16, "chi")
        c_mid = ph.tile([80, S], BF16, "cmid")
        c_lo = ph.tile([80, S], BF16, "clo")
        r1 = lf
        r2 = ones16
        p.ts(c_hi, ncum, -1.0, ALU.mult)
        p.stt(r1, ncum, -1.0, c_hi, ALU.mult, ALU.subtract)
        p.copy(c_mid, r1)
        p.tt(r2, r1, c_mid, ALU.subtract)
        p.copy(c_lo, r2)
        c_all = fg_b = ph.tile([80, S], BF16, "call")
        p.op("pool", lambda e: e.memset(c_all.ap, 0.0), w=[c_all])
        p.copy(c_all[0:16, :], c_hi[0:16, :])
        p.copy(c_all[32:48, :], c_mid[32:48, :])
        p.copy(c_all[64:80, :], c_lo[64:80, :])
        nct = ph.tile([128, 32, 16], F32, "nct")
        ps = self.ps[7]
        for j in range(32):
            p.op("pe", lambda e, j=j: e.transpose(ps.ap[:, j * 16:(j + 1) * 16], ncum.ap[0:16, j * 128:(j + 1) * 128],
                                                  self.ident.ap[0:16, 0:16]),
                 w=[ps], r=[ncum, self.ident], inc=(j == 31))
        p.copy(nct, ps.ap.rearrange("p (j h) -> p j h", h=16) if False else ps)
        sel = ph.tile([80, 16, 128], BF16, "sel80")
        p.dma("sp", sel.ap, self.inp["c_sel80"], w=[sel])
        cm = ph.tile([128, 4, 512], BF16, "cmask")
        p.dma("sp", cm.ap, self.inp["c_cmask"], w=[cm])
        qT = [ph.tile([128, S], BF16, "qT") for _ in range(2)]
        kT = [ph.tile([128, S], BF16, "kT") for _ in range(2)]
        vt = [ph.tile([128, 32, 128], BF16, "vt") for _ in range(2)]
        ptt = [ph.tile([128, 512], BF16, "pt") for _ in range(2)]
        rec = ph.tile([128, 512], F32, "rec")
        ob = [ph.tile([128, 512], BF16, "ob") for _ in range(2)]
        nctv = nct.ap
        ko = 0
        for h in range(16):
            q_ = qT[h % 2]
            k_ = kT[h % 2]
            v_ = vt[h % 2]
            p.dma("sp", q_.ap, self.PT[h], w=[q_])
            p.dma("sp", k_.ap, self.PT[16 + h], w=[k_])
            p.dma("sp", v_.ap, self.VTM[:, h * 128:(h + 1) * 128].rearrange("(j p) d -> p j d", p=128), w=[v_])
            for c in range(NSUB):
                qs = slice(c * 512, (c + 1) * 512)

                def qpairs(j, k_=k_, q_=q_, qs=qs):
                    return [(k_[:, j * 128:(j + 1) * 128], q_[:, qs])]

                def extra(j, c=c, h=h, qs=qs):
                    e = [(sel[:, h, :], c_all[:, qs])]
                    if j >= 4 * c:
                        e.append((self.identb, cm[:, j - 4 * c, :]))
                    return e

                def bias(j, h=h):
                    return V(nct, nctv[:, j, h:h + 1])

                o = ob[ko % 2]
                ko += 1
                self.attn_chunk(4 * c + 4, lambda j, v_=v_: v_[:, j, :], qpairs, extra, bias, ptt,
                                (self.ps[0], self.ps[1]), self.ps[2], self.ps[3], rec, o)
                p.dma("sp", self.AT[h][:, qs], o.ap, r=[o])
        p.barrier()
        ph.close()

    def dil_core(self, L):
        p = self.p
        nc = self.nc
        ph = Phase(nc)
        tab = ph.tile([33, 16], F32, "tab33")
        p.op("dve", lambda e: e.memset(tab.ap[32:33, :], NEG), w=[tab])
        p.dma("sp", tab.ap[0:32, :], self.inp["t5_table"], w=[tab])
        ohg = ph.tile([33, 3, 384], F32, "ohg")
        p.dma("sp", ohg.ap, self.inp["c_ohg"].rearrange("g v i -> v g i"), w=[ohg])
        vx = ph.tile([16, 384], F32, "vx")
        for g in range(3):
            ps = self.ps[g]
            p.mm(ps[0:16, 0:384], [(tab, ohg[:, g, :])])
            p.copy(vx, ps[0:16, 0:384])
            p.dma("sp", self.VEXT[g * 16:(g + 1) * 16, 0:384], vx.ap, r=[vx])
        p.barrier()
        bz = [[ph.tile([128, 256], BF16, "bz") for _ in range(16)] for _ in range(3)]
        hk = [ph.tile([128, 256], F32, "hk") for _ in range(2)]
        hkb = [ph.tile([128, 256], BF16, "hkb") for _ in range(2)]
        for g in range(3):
            for h in range(16):
                i = g * 16 + h
                a = hk[i % 2]
                b = hkb[i % 2]
                src = bass.AP(tensor=self.VEXT_h, offset=i * 512, ap=[[1, 128], [1, 256]])
                p.dma("sp", a.ap, src, w=[a])
                p.copy(b, a)
                ps = self.ps[i % 2]
                p.mm(ps[:, 0:256], [(self.antib, b)])
                p.copy(bz[g][h], ps[:, 0:256], E=("act" if i % 2 else "dve"))
        num = ph.tile([128, S], F32, "num")
        den = ph.tile([128, S], F32, "den")
        qT = [ph.tile([128, S], BF16, "qT") for _ in range(2)]
        kT = [ph.tile([128, S], BF16, "kT") for _ in range(2)]
        vt = [ph.tile([128, 32, 128], BF16, "vt") for _ in range(2)]
        ptt = [ph.tile([128, 128], BF16, "pt") for _ in range(2)]
        ob = ph.tile([128, S], BF16, "ob")
        kk = 0
        kb = 0
        for h in range(16):
            for g, dil in enumerate((1, 4, 16)):
                q_ = qT[kk % 2]
                k_ = kT[kk % 2]
                v_ = vt[kk % 2]
                kk += 1
                nb = S // dil // 128
                p.dma("sp", q_.ap, self.PT[g * 32 + h], w=[q_])
                p.dma("sp", k_.ap, self.PT[g * 32 + 16 + h], w=[k_])
                c0 = g * 2048 + h * 128
                vsrc = self.VTM[:, c0:c0 + 128].rearrange("(jj p r) d -> p r jj d", p=128, r=dil)
                vdst = v_.ap.rearrange("p (r jj) d -> p r jj d", r=dil)
                for r in range(dil):
                    p.dma("sp", vdst[:, r], vsrc[:, r], w=[v_])
                for r in range(dil):
                    for i in range(nb):
                        q0 = r + dil * 128 * i
                        qsl = slice(q0, q0 + dil * 127 + 1, dil)
                        tiles = [i] if i == 0 else [i - 1, i]
                        ps_o = self.ps[2 + kb % 2]
                        ps_d = self.ps[4 + kb % 2]
                        kb += 1
                        for ti, jj in enumerate(tiles):
                            k0 = r + dil * 128 * jj
                            ksl = slice(k0, k0 + dil * 127 + 1, dil)
                            bsl = slice(0, 128) if jj == i else slice(128, 256)
                            pss = self.ps[ti]
                            p.mm(pss[:, 0:128], [(k_[:, ksl], q_[:, qsl]), (self.identb, bz[g][h][:, bsl])])
                            pt = ptt[ti]
                            p.act(pt, pss[:, 0:128], AF.Exp)
                            first = ti == 0
                            last = ti == len(tiles) - 1
                            p.mm(ps_o[:, 0:128], [(v_[:, r * nb + jj, :], pt)], first=first, last=last)
                            p.mm(ps_d[:, 0:128], [(self.onesb, pt)], first=first, last=last)
                        if g == 0:
                            p.copy(num[:, qsl], ps_o[:, 0:128], E="act")
                            p.copy(den[:, qsl], ps_d[:, 0:128], E="dve")
                        else:
                            p.tt(num[:, qsl], ps_o[:, 0:128], num[:, qsl], ALU.add)
                            p.tt(den[:, qsl], ps_d[:, 0:128], den[:, qsl], ALU.add)
            p.op("dve", lambda e: e.reciprocal(den.ap, den.ap), w=[den], r=[den])
            p.tt(ob, num, den, ALU.mult)
            p.dma("sp", self.AT[h], ob.ap, r=[ob])
        p.barrier()
        ph.close()

    def rope_tables(self):
        p = self.p
        ph = Phase(self.nc)
        TWO_PI = 2.0 * math.pi
        cr = ph.tile([64, 2], F32, "crope")
        p.dma("sp", cr.ap, self.inp["c_rope"], w=[cr])
        posi = ph.tile([64, S], I32, "posi")
        pa = self.inp["positions"]
        p.dma("sp", posi.ap, bass.AP(tensor=pa.tensor, offset=0, ap=[[0, 64], [1, S]]), w=[posi])
        ang = ph.tile([64, S], F32, "ang")
        t1 = ph.tile([64, S], F32, "t1")
        ki = ph.tile([64, S], I32, "ki")
        p.copy(ang, posi)
        p.ts(ang, ang, cr[:, 0:1], ALU.mult)
        p.ts(t1, ang, 1.0 / TWO_PI, ALU.mult)
        p.copy(ki, t1)
        p.copy(t1, ki)
        r = ph.tile([64, S], F32, "r")
        p.stt(r, t1, -TWO_PI, ang, ALU.mult, ALU.add)
        p.ts(t1, r, math.pi, ALU.is_gt)
        p.stt(r, t1, -TWO_PI, r, ALU.mult, ALU.add)
        p.ts(t1, r, -1.0, ALU.mult, math.pi, ALU.is_gt)
        p.stt(r, t1, TWO_PI, r, ALU.mult, ALU.add)
        p.ts(r, r, 3.14159, ALU.min, -3.14159, ALU.max)
        p.act(t1, r, AF.Sin)
        p.ts(t1, t1, cr[:, 1:2], ALU.mult)
        p.dma("sp", self.ROPE[1], t1.ap, r=[t1])
        p.stt(ang, r, -1.0, r, ALU.mult, ALU.max)
        hp = ph.tile([64, 1], F32, "halfpi")
        p.op("dve", lambda e: e.memset(hp.ap, math.pi / 2), w=[hp])
        p.act(ang, ang, AF.Sin, bias=hp[:, 0:1], scale=-1.0)
        p.dma("sp", self.ROPE[0], ang.ap, r=[ang])
        p.barrier()
        ph.close()

    def mla_setup(self, L, ph):
        p = self.p
        st = {}
        st["gq"] = self.colv_small("b_q_norm", 4, ph)
        st["gkv"] = self.colv_small("b_kv_norm", 4, ph)
        gn = self.colv_small("b_nope_g", 2, ph)
        gr = self.colv_small("b_rope_g", 2, ph)
        sc = 192 ** -0.5
        gqn = ph.tile([128, 1], F32, "gqn")
        p.ts(gqn, gn[:, 0:1], sc, ALU.mult)
        st["gqn"] = gqn
        st["gkn"] = gn
        grs = ph.tile([64, 4], F32, "grs")
        p.ts(grs[:, 0:1], gr[0:64, 0:1], sc, ALU.mult)
        p.copy(grs[:, 2:3], gr[0:64, 1:2])
        stg = ph.tile([128, 128], F32, "grst")
        src = self.inp["b_rope_g"]
        p.dma("sp", stg.ap[0:2, 0:32], src[:, 32:64], w=[stg])
        p.dma("sp", stg.ap[0:2, 32:64], src[:, 0:32], w=[stg])
        ps = self.ps[7]
        p.op("pe", lambda e: e.transpose(ps.ap[0:64, 0:2], stg.ap[0:2, 0:64], self.ident.ap[0:2, 0:2]),
             w=[ps], r=[stg, self.ident])
        p.ts(grs[:, 1:2], ps[0:64, 0:1], sc, ALU.mult)
        p.copy(grs[:, 3:4], ps[0:64, 1:2])
        st["grs"] = grs
        st["cqf"] = [ph.tile([128, 512], F32, "cqf") for _ in range(4)]
        st["cqn"] = [ph.tile([128, 512], BF16, "cqn") for _ in range(4)]
        st["ckvn"] = [ph.tile([128, 512], BF16, "ckvn") for _ in range(4)]
        st["cs"] = [ph.tile([64, 512], F32, "cs") for _ in range(2)]
        st["ce"] = [ph.tile([64, 512], F32, "ce") for _ in range(2)]
        st["tt"] = [ph.tile([64, 512], F32, "ropet") for _ in range(2)]
        st["wq"] = [ph.tile([128, 4, 128], BF16, "wuq") for _ in range(2)]
        st["wq2"] = [ph.tile([128, 4, 64], BF16, "wuq2") for _ in range(2)]
        st["wv"] = [ph.tile([128, 4, 512], BF16, "wukvv") for _ in range(2)]
        return st

    def rope_apply(self, st, ps_a, ps_b, rstd64, g_a, g_b, out_bf):
        p = self.p
        ce = st["ce"]
        cs = st["cs"]
        tt = st["tt"]
        p.tt(ce[0], cs[0], rstd64, ALU.mult)
        p.tt(ce[1], cs[1], rstd64, ALU.mult)
        p.stt(tt[0], ps_a, g_a, ce[0], ALU.mult, ALU.mult)
        p.stt(tt[1], ps_b, g_b, ce[1], ALU.mult, ALU.mult)
        p.tt(out_bf, tt[0], tt[1], ALU.add)

    def mla_block(self, L, st, tb, hT, W, tmp, ob_next):
        p = self.p
        tok = slice(tb * 512, (tb + 1) * 512)
        hsq, hln, hrs = tmp
        Wuq = self.WB[f"b_w_uq_{L}"]
        Wukv = self.WB[f"b_w_ukv_{L}"]
        wt = st["wt"]
        p.dma("sp", st["cs"][0].ap, self.ROPE[0][:, tok], w=[st["cs"][0]])
        p.dma("sp", st["cs"][1].ap, self.ROPE[1][:, tok], w=[st["cs"][1]])
        for which, c_base, gcols, dst in (("q", 0, st["gq"], st["cqn"]), ("kv", 512, st["gkv"], st["ckvn"])):
            for kc in range(4):
                w_ = wt[kc % 2]
                p.dma("sp", w_.ap, W[:, :, c_base + kc * 128:c_base + (kc + 1) * 128], w=[w_])
                ps = self.ps[1 + kc % 2]
                p.mm(ps, [(w_[:, k2, :], hT[k2]) for k2 in range(16)])
                p.copy(st["cqf"][kc], ps)
                p.act(hsq, ps, AF.Square)
                p.mm(self.ps[3], [(self.onesb, hsq)], first=(kc == 0), last=(kc == 3))
            p.act(hln, self.ps[3], AF.Ln, bias=self.eps_col[:, 0:1], scale=1.0 / 512)
            p.act(hrs, hln, AF.Exp, scale=-0.5)
            for kc in range(4):
                p.stt(dst[kc], st["cqf"][kc], gcols[:, kc:kc + 1], hrs, ALU.mult, ALU.mult)
        w_ = wt[0]
        w2 = wt[1]
        p.dma("sp", w_.ap[:, :, 0:64], W[:, :, 1024:1088], w=[w_])
        p.dma("sp", w2.ap[:, :, 0:32], W[:, :, 1056:1088], w=[w2])
        p.dma("sp", w2.ap[:, :, 32:64], W[:, :, 1024:1056], w=[w2])
        pa = self.ps[1]
        pb = self.ps[2]
        p.mm(pa[0:64, :], [(w_[:, k2, 0:64], hT[k2]) for k2 in range(16)])
        p.mm(pb[0:64, :], [(w2[:, k2, 0:64], hT[k2]) for k2 in range(16)])
        self.rstd_part(pa, 64, hsq, hln, hrs)
        o = ob_next()
        self.rope_apply(st, pa[0:64, :], pb[0:64, :], hrs[0:64, :], st["grs"][:, 2:3], st["grs"][:, 3:4], o[0:64, :])
        p.dma("sp", self.PT[48][0:64, tok], o.ap[0:64, :], r=[o])
        for h in range(16):
            wq = st["wq"][h % 2]
            p.dma("sp", wq.ap, Wuq[:, :, h * 192:h * 192 + 128], w=[wq])
            ps = self.ps[1 + h % 2]
            p.mm(ps, [(wq[:, kc, :], st["cqn"][kc]) for kc in range(4)])
            o = ob_next()
            self.headnorm((hsq, hln, hrs), ps, self.ps[3], st["gqn"][:, 0:1], o)
            p.dma("sp", self.PT[h][:, tok], o.ap, r=[o])
            wa = st["wq2"][0]
            wb = st["wq2"][1]
            c0 = h * 192 + 128
            p.dma("sp", wa.ap, Wuq[:, :, c0:c0 + 64], w=[wa])
            p.dma("sp", wb.ap[:, :, 0:32], Wuq[:, :, c0 + 32:c0 + 64], w=[wb])
            p.dma("sp", wb.ap[:, :, 32:64], Wuq[:, :, c0:c0 + 32], w=[wb])
            pa = self.ps[4]
            pb = self.ps[5]
            p.mm(pa[0:64, :], [(wa[:, kc, :], st["cqn"][kc]) for kc in range(4)])
            p.mm(pb[0:64, :], [(wb[:, kc, :], st["cqn"][kc]) for kc in range(4)])
            self.rstd_part(pa, 64, hsq, hln, hrs)
            o = ob_next()
            self.rope_apply(st, pa[0:64, :], pb[0:64, :], hrs[0:64, :], st["grs"][:, 0:1], st["grs"][:, 1:2],
                            o[0:64, :])
            p.dma("sp", self.PT[16 + h][0:64, tok], o.ap[0:64, :], r=[o])
            wq = st["wq"][(h + 1) % 2]
            p.dma("sp", wq.ap, Wukv[:, :, h * 256:h * 256 + 128], w=[wq])
            ps = self.ps[1 + (h + 1) % 2]
            p.mm(ps, [(wq[:, kc, :], st["ckvn"][kc]) for kc in range(4)])
            o = ob_next()
            self.headnorm((hsq, hln, hrs), ps, self.ps[3], st["gkn"][:, 1:2], o)
            p.dma("sp", self.PT[32 + h][:, tok], o.ap, r=[o])
        wv5 = Wukv.rearrange("p kc (h two d) -> p kc h two d", two=2, d=128)
        for g4 in range(4):
            wv = st["wv"][g4 % 2]
            wdst = wv.ap.rearrange("p kc (h d) -> p kc h d", d=128)
            for kc in range(4):
                p.dma("sp", wdst[:, kc], wv5[:, kc, g4 * 4:(g4 + 1) * 4, 1, :], w=[wv])
            for t in range(4):
                ps = self.ps[4 + t % 2]
                p.mm(ps, [(st["ckvn"][kc][:, t * 128:(t + 1) * 128], wv[:, kc, :]) for kc in range(4)])
                o = ob_next()
                p.copy(o, ps, E=("act" if t % 2 else "dve"))
                r0 = tb * 512 + t * 128
                p.dma("sp", self.VTM[r0:r0 + 128, g4 * 512:(g4 + 1) * 512], o.ap, r=[o])

    def rstd_part(self, ps_in, npart, sq, lnv, rstd):
        p = self.p
        p.act(sq[0:npart, :], ps_in[0:npart, :], AF.Square)
        p.mm(self.ps[3][0:npart, :], [(self.onesb[0:npart, 0:npart], sq[0:npart, :])])
        p.act(lnv[0:npart, :], self.ps[3][0:npart, :], AF.Ln, bias=self.eps_col[0:npart, 0:1], scale=1.0 / npart)
        p.act(rstd[0:npart, :], lnv[0:npart, :], AF.Exp, scale=-0.5)

    def mla_core(self, L):
        p = self.p
        ph = Phase(self.nc)
        cm = ph.tile([128, 4, 512], BF16, "cmask")
        p.dma("sp", cm.ap, self.inp["c_cmask"], w=[cm])
        kr = ph.tile([64, S], BF16, "krT")
        p.dma("sp", kr.ap, self.PT[48][0:64, :], w=[kr])
        qT = [ph.tile([128, S], BF16, "qT") for _ in range(2)]
        qR = [ph.tile([64, S], BF16, "qR") for _ in range(2)]
        kT = [ph.tile([128, S], BF16, "kT") for _ in range(2)]
        vt = [ph.tile([128, 32, 128], BF16, "vt") for _ in range(2)]
        ptt = [ph.tile([128, 512], BF16, "pt") for _ in range(2)]
        rec = ph.tile([128, 512], F32, "rec")
        ob = [ph.tile([128, 512], BF16, "ob") for _ in range(2)]
        ko = 0
        for h in range(16):
            q_ = qT[h % 2]
            r_ = qR[h % 2]
            k_ = kT[h % 2]
            v_ = vt[h % 2]
            p.dma("sp", q_.ap, self.PT[h], w=[q_])
            p.dma("sp", r_.ap, self.PT[16 + h][0:64, :], w=[r_])
            p.dma("sp", k_.ap, self.PT[32 + h], w=[k_])
            p.dma("sp", v_.ap, self.VTM[:, h * 128:(h + 1) * 128].rearrange("(j p) d -> p j d", p=128), w=[v_])
            for c in range(NSUB):
                qs = slice(c * 512, (c + 1) * 512)

                def qpairs(j, k_=k_, q_=q_, r_=r_, qs=qs):
                    ks = slice(j * 128, (j + 1) * 128)
                    return [(k_[:, ks], q_[:, qs]), (kr[:, ks], r_[:, qs])]

                def extra(j, c=c):
                    if j >= 4 * c:
                        return [(self.identb, cm[:, j - 4 * c, :])]
                    return []

                o = ob[ko % 2]
                ko += 1
                self.attn_chunk(4 * c + 4, lambda j, v_=v_: v_[:, j, :], qpairs, extra, lambda j: None, ptt,
                                (self.ps[0], self.ps[1]), self.ps[2], self.ps[3], rec, o)
                p.dma("sp", self.AT[h][:, qs], o.ap, r=[o])
        p.barrier()
        ph.close()

    def dsa_core(self, L):
        p = self.p
        nc = self.nc
        ph = Phase(nc)
        tab = ph.tile([32, 16], F32, "tab")
        p.dma("sp", tab.ap, self.inp["t5_table"], w=[tab])
        ohd = ph.tile([32, 2688], F32, "ohd")
        p.dma("sp", ohd.ap, self.inp["c_ohd"], w=[ohd])
        bv = ph.tile([16, 2688], F32, "bv")
        for i in range(6):
            w_ = min(512, 2688 - i * 512)
            ps = self.ps[i % 2]
            p.mm(ps[0:16, 0:w_], [(tab, ohd[:, i * 512:i * 512 + w_])])
            p.copy(bv[:, i * 512:i * 512 + w_], ps[0:16, 0:w_])
        p.dma("sp", self.BEXT, bv.ap, r=[bv])
        p.barrier()
        hk = [ph.tile([128, 2560], F32, "hk") for _ in range(2)]
        hkb = [ph.tile([128, 2560], BF16, "hkb") for _ in range(2)]
        tzb = [ph.tile([128, 2560], BF16, "tzb") for _ in range(2)]
        for h in range(16):
            a = hk[h % 2]
            b = hkb[h % 2]
            t = tzb[h % 2]
            p.dma("sp", a.ap, bass.AP(tensor=self.BEXT_h, offset=h * 2688, ap=[[1, 128], [1, 2560]]), w=[a])
            p.copy(b, a, E=("act" if h % 2 else "dve"))
            for i in range(5):
                ps = self.ps[2 + i % 2]
                p.mm(ps, [(self.antib, b[:, i * 512:(i + 1) * 512])])
                p.copy(t[:, i * 512:(i + 1) * 512], ps, E=("dve" if i % 2 else "act"))
            p.dma("sp", self.TZ[h], t.ap, r=[t])
        p.barrier()
        ph.close()
        ph = Phase(nc)
        cm = ph.tile([128, 4, 512], BF16, "cmask")
        p.dma("sp", cm.ap, self.inp["c_cmask"], w=[cm])
        sel = ph.tile([16, 16, 128], BF16, "sel16")
        p.dma("sp", sel.ap, self.inp["c_sel16"], w=[sel])
        kiT = ph.tile([64, S], BF16, "kiT")
        p.dma("sp", kiT.ap, self.PT[36][0:64, :], w=[kiT])
        wiT = ph.tile([16, 512], BF16, "wiT")
        idx = [ph.tile([128, 512], F32, "idx") for _ in range(32)]
        selb = [ph.tile([128, 512], BF16, "selb") for _ in range(32)]
        wrep = [ph.tile([128, 512], F32, "wrep") for _ in range(2)]
        qi = [ph.tile([64, 512], BF16, "qi") for _ in range(2)]
        tmp = [ph.tile([128, 512], F32, "itmp") for _ in range(2)]
        cmp = [ph.tile([128, 512], BF16, "cmp") for _ in range(2)]
        lo = ph.tile([128, 512], F32, "lo")
        mid = ph.tile([128, 512], F32, "mid")
        tsel = ph.tile([128, 512], F32, "tsel")
        tz = [ph.tile([128, 2560], BF16, "tz") for _ in range(2)]
        qT = [ph.tile([128, 512], BF16, "qT") for _ in range(2)]
        kT = [ph.tile([128, S], BF16, "kT") for _ in range(2)]
        vt = [ph.tile([128, 32, 128], BF16, "vt") for _ in range(2)]
        ptt = [ph.tile([128, 512], BF16, "pt") for _ in range(2)]
        rec = ph.tile([128, 512], F32, "rec")
        ob = [ph.tile([128, 512], BF16, "ob") for _ in range(2)]
        NIT = 24
        ko = 0
        kq = 0
        kg = 0
        for c in range(NSUB):
            qs = slice(c * 512, (c + 1) * 512)
            nk = 4 * c + 4
            p.dma("sp", wiT.ap, self.PT[37][0:16, qs], w=[wiT])
            for h in range(16):
                wr = wrep[h % 2]
                ps = self.ps[0]
                p.mm(ps, [(sel[:, h, :], wiT)])
                p.copy(wr, ps, E="act")
                q_ = qi[h % 2]
                p.dma("sp", q_.ap, self.PT[20 + h][0:64, qs], w=[q_])
                for j in range(nk):
                    ps = self.ps[1 + j % 2]
                    p.mm(ps, [(kiT[:, j * 128:(j + 1) * 128], q_)])
                    if h == 0:
                        p.stt(idx[j], ps, 0.0, wr, ALU.max, ALU.mult)
                    else:
                        t_ = tmp[j % 2]
                        p.stt(t_, ps, 0.0, wr, ALU.max, ALU.mult)
                        p.tt(idx[j], idx[j], t_, ALU.add, E="pool")
            for j in range(4 * c, nk):
                p.tt(idx[j], idx[j], cm[:, j - 4 * c, :], ALU.add, E="pool")
            p.op("dve", lambda e: e.memset(lo.ap, -64.0), w=[lo])
            for it in range(NIT):
                ck = 64.0 / (2 ** it)
                p.ts(mid, lo, ck, ALU.add)
                pc = self.ps[3 + it % 2]
                for j in range(nk):
                    cp = cmp[j % 2]
                    p.tt(cp, idx[j], mid, ALU.is_ge)
                    p.mm(pc, [(self.onesb, cp)], first=(j == 0), last=(j == nk - 1))
                p.ts(tsel, pc, 255.5, ALU.is_ge, ck, ALU.mult)
                p.tt(lo, lo, tsel, ALU.add)
            for j in range(nk):
                cp = tmp[j % 2]
                p.tt(cp, idx[j], lo, ALU.is_ge)
                p.ts(selb[j], cp, -1.0, ALU.add, -NEG, ALU.mult, E="pool")
            tzw = min(512 * c, 1664) + 896
            for g in range(4):
                k_ = kT[kg % 2]
                v_ = vt[kg % 2]
                kg += 1
                p.dma("sp", k_.ap[:, 0:nk * 128], self.PT[16 + g][:, 0:nk * 128], w=[k_])
                p.dma("sp", v_.ap[:, 0:nk, :],
                      self.VTM[0:nk * 128, g * 128:(g + 1) * 128].rearrange("(j p) d -> p j d", p=128), w=[v_])
                for r in range(4):
                    h = g * 4 + r
                    q_ = qT[kq % 2]
                    z_ = tz[kq % 2]
                    kq += 1
                    p.dma("sp", q_.ap, self.PT[h][:, qs], w=[q_])
                    p.dma("sp", z_.ap[:, 0:tzw], self.TZ[h][:, 0:tzw], w=[z_])

                    def qpairs(j, k_=k_, q_=q_):
                        return [(k_[:, j * 128:(j + 1) * 128], q_)]

                    def extra(j, c=c, z_=z_):
                        d0 = min(512 * c - 128 * j, 1664)
                        m0 = d0 + 384
                        return [(self.identb, z_[:, m0:m0 + 512]), (self.identb, selb[j])]

                    o = ob[ko % 2]
                    ko += 1
                    self.attn_chunk(nk, lambda j, v_=v_: v_[:, j, :], qpairs, extra, lambda j: None, ptt,
                                    (self.ps[5], self.ps[6]), self.ps[7], self.ps[0], rec, o)
                    p.dma("sp", self.AT[h][:, qs], o.ap, r=[o])
        p.barrier()
        ph.close()

    def out_proj(self, L):
        p = self.p
        ph = Phase(self.nc)
        W = self.WB[f"w_out_{L}"]
        m = L % 4
        nch = 20
        at = [[ph.tile([128, 512], BF16, "at") for _ in range(nch)] for _ in range(2)]
        wt = [ph.tile([128, 20, 128], BF16, "wo") for _ in range(2)]
        xs = [ph.tile([128, 512], F32, "xs") for _ in range(2)]
        xo = [ph.tile([128, 512], F32, "xo") for _ in range(2)]
        c_start = 0
        for tb in range(NSUB):
            tok = slice(tb * 512, (tb + 1) * 512)
            a = at[tb % 2]
            for c in range(c_start, nch):
                p.dma("sp", a[c].ap, self.AT[c][:, tok], w=[a[c]])
            for dc in range(16):
                w_ = wt[dc % 2]
                p.dma("sp", w_.ap, W[:, :, dc * 128:(dc + 1) * 128], w=[w_])
                ps = self.ps[dc % 2]
                p.mm(ps, [(w_[:, c, :], a[c]) for c in range(c_start, nch)])
                xt = xs[dc % 2]
                p.dma("sp", xt.ap, self.XT[dc][:, tok], w=[xt])
                o = xo[dc % 2]
                p.tt(o, ps, xt, ALU.add)
                p.dma("sp", self.XT[dc][:, tok], o.ap, r=[o])
        p.barrier()
        ph.close()

    def build(self):
        p = self.p
        self.declare()
        self.load_consts()
        self.eps_col = self.cst.tile([128, 1], F32, "eps")
        self.memT = [self.cst.tile([128, 256], F32, "memT") for _ in range(16)]
        self.mem_rstd = self.cst.tile([128, 256], F32, "memrstd")

        p.op("dve", lambda e: e.memset(self.eps_col.ap, EPS), w=[self.eps_col])
        if self.stop != "xt":
            self.convert_all()
        if self.stop not in ("xt", "cv", "ffn0"):
            self.mem_setup()
        import os
        if not os.environ.get("SKIP_XT"):
            self.x_to_xt()
        for L in (() if self.stop in ("xt", "cv") else self.layers):
            self.ffn(L, 0)
            if self.stop == "ffn0":
                break
            self.attention(L)
            if self.stop == f"att{L}":
                break
            self.ffn(L, 1)
            if self.stop == f"ffn1_{L}":
                break
        if not os.environ.get("SKIP_OUT"):
            self.xt_to_out()
        p.wait_all_dma("sp")
        return self.nc


def t5_bucket_np(dist):
    n = np.maximum(dist, 0)
    nf = np.maximum(n, 1).astype(np.float32)
    large = 16 + (np.log(nf / np.float32(16)) / np.float32(math.log(2048 / 16)) * np.float32(16)).astype(np.int32)
    large = np.minimum(large, 31)
    return np.where(n < 16, n, large)


def host_consts():
    bf = ml_dtypes.bfloat16
    c = {}
    c["c_ident"] = np.eye(128, dtype=np.float32)
    c["c_identb"] = np.eye(128, dtype=np.float32).astype(bf)
    c["c_antib"] = np.eye(128, dtype=np.float32)[::-1].copy().astype(bf)
    c["c_onesb"] = np.ones((128, 128), dtype=np.float32).astype(bf)
    k = np.arange(128)[:, None, None]
    o = np.arange(4)[None, :, None]
    q = np.arange(512)[None, None, :]
    c["c_cmask"] = np.where(128 * o + k <= q, 0.0, NEG).astype(np.float32).astype(bf)
    sel = np.zeros((16, 16, 128), dtype=np.float32)
    for h in range(16):
        sel[h, h, :] = 1.0
    c["c_sel16"] = sel.astype(bf)
    sel80 = np.zeros((80, 16, 128), dtype=np.float32)
    for i in range(3):
        sel80[32 * i:32 * i + 16] = sel
    c["c_sel80"] = sel80.astype(bf)
    i = np.arange(2688)
    dist = i - 511
    oh = np.zeros((32, 2688), dtype=np.float32)
    b = t5_bucket_np(dist)
    valid = dist >= 0
    oh[b[valid], i[valid]] = 1.0
    c["c_ohd"] = oh
    ohg = np.zeros((3, 33, 384), dtype=np.float32)
    for g, dil in enumerate((1, 4, 16)):
        for ii in range(384):
            rel = ii - 127
            if 0 <= rel <= 128:
                ohg[g, int(t5_bucket_np(np.array(rel * dil))), ii] = 1.0
            else:
                ohg[g, 32, ii] = 1.0
    c["c_ohg"] = ohg
    half = 32
    inv = (np.float32(10000.0) ** (-np.arange(half, dtype=np.float32) / np.float32(half))).astype(np.float32)
    rope = np.zeros((64, 2), dtype=np.float32)
    rope[:, 0] = np.concatenate([inv, inv])
    rope[:, 1] = np.concatenate([-np.ones(32), np.ones(32)])
    c["c_rope"] = rope
    return c


def prep_inputs(inputs, b):
    f = lambda a: np.ascontiguousarray(a)
    m = {}
    m["x"] = f(inputs["x"][b])
    m["mem"] = f(inputs["mem"][b])
    m["positions"] = f(inputs["positions"][b].reshape(1, S).astype(np.int32))
    m["t5_table"] = f(inputs["t5_table"])
    m["ffn_norm"] = f(inputs["ffn_norm"].reshape(DEPTH * 2 * 16, 128))
    for n in ("ffn_w_gate", "ffn_w_up", "ffn_w_down", "mem_w_kv", "w_out", "a_w_in", "b_w_in", "b_w_uq",
              "b_w_ukv", "c_w_in", "d_w_in"):
        m[n] = inputs[n]
    m["attn_norm"] = f(inputs["attn_norm"].reshape(DEPTH * 16, 128))
    m["mem_norm"] = f(inputs["mem_norm"].reshape(DEPTH * 16, 128))
    m["mem_qk_g"] = f(inputs["mem_qk_g"].reshape(DEPTH * 2, 128))
    m["a_b_f"] = f(inputs["a_b_f"].reshape(16, 1))
    m["a_qk_g"] = f(inputs["a_qk_g"].reshape(2, 128))
    m["b_q_norm"] = f(inputs["b_q_norm"].reshape(4, 128))
    m["b_kv_norm"] = f(inputs["b_kv_norm"].reshape(4, 128))
    m["b_nope_g"] = f(inputs["b_nope_g"].reshape(2, 128))
    m["b_rope_g"] = f(inputs["b_rope_g"].reshape(2, 64))
    m["c_qk_g"] = f(inputs["c_qk_g"].reshape(6, 128))
    m["d_qk_g"] = f(inputs["d_qk_g"].reshape(2, 128))
    return m


_CACHE = {}


def kernel(**inputs):
    inputs = {k: np.asarray(v) for k, v in inputs.items()}
    if "nc" not in _CACHE:
        _CACHE["nc"] = K().build()
    nc = _CACHE["nc"]
    consts = host_consts()
    in_maps = []
    for b in range(8):
        m = prep_inputs(inputs, b)
        m.update(consts)
        in_maps.append(m)
    res = run_bass_kernel_spmd(nc, in_maps, core_ids=list(range(8)))
    out = np.stack([np.asarray(r["out"]) for r in res.results], axis=0)
    return out.astype(np.float32)
```

```python
import math
import os
import numpy as np
import ml_dtypes
import concourse.bass as bass
import concourse.mybir as mybir
from concourse.bass_utils import run_bass_kernel_spmd

F32 = mybir.dt.float32
BF16 = mybir.dt.bfloat16
I32 = mybir.dt.int32
AF = mybir.ActivationFunctionType
ALU = mybir.AluOpType
AX = mybir.AxisListType

S = 4096
D = 2048
DFF = 5632
NSUB = S // 512
DEPTH = 4
EPS = 1e-6
NEG = -30000.0


class T:
    __slots__ = ("ap", "w", "r", "psum")

    def __init__(self, ap, psum=False):
        self.ap = ap
        self.w = None
        self.r = {}
        self.psum = psum

    def __getitem__(self, idx):
        return V(self, self.ap[idx])


class V:
    __slots__ = ("t", "ap")

    def __init__(self, t, ap):
        self.t = t
        self.ap = ap

    def __getitem__(self, idx):
        return V(self.t, self.ap[idx])


def _tile(x):
    return x.t if isinstance(x, V) else x


class P:
    ENGS = ("pe", "act", "dve", "pool", "sp")

    def __init__(self, nc, n_dma_sems=8):
        self.nc = nc
        self.eng = {"pe": nc.tensor, "act": nc.scalar, "dve": nc.vector,
                    "pool": nc.gpsimd, "sp": nc.sync}
        self.sem = {}
        self.cnt = {}
        self.semid = {}
        self._nid = 0
        for e in self.ENGS:
            self._nid += 1
            self.sem[e] = nc.alloc_semaphore(f"s_{e}")
            self.cnt[e] = 0
            self.semid[e] = self._nid
        self.dsem = {}
        for q in ("sp", "act", "pool"):
            lst = []
            for i in range(n_dma_sems):
                self._nid += 1
                lst.append([nc.alloc_semaphore(f"d_{q}{i}"), 0, self._nid])
            self.dsem[q] = lst
        self.drr = {"sp": 0, "act": 0, "pool": 0}
        self.waited = {e: {} for e in self.ENGS}
        self.n_inst = 0
        self.n_wait = 0

    def _wait(self, E, tk, kind):
        if tk is None:
            return
        src, sid, sh, v = tk
        if src == E:
            if E == "pe" or kind != "raw":
                return
        wd = self.waited[E]
        if wd.get(sid, 0) >= v:
            return
        self.eng[E].wait_ge(sh, v)
        self.n_wait += 1
        wd[sid] = v

    def _deps(self, E, w, r):
        for x in r:
            t = _tile(x)
            if t is not None:
                self._wait(E, t.w, "raw")
                if t.psum:
                    for tk in t.r.values():
                        self._wait(E, tk, "war")
        for x in w:
            t = _tile(x)
            if t is not None:
                self._wait(E, t.w, "waw")
                for tk in t.r.values():
                    self._wait(E, tk, "war")

    def _mark(self, tk, w, r):
        for x in r:
            t = _tile(x)
            if t is not None:
                t.r[tk[1]] = tk
        for x in w:
            t = _tile(x)
            if t is not None:
                t.w = tk
                t.r = {}

    def op(self, E, fn, w=(), r=(), inc=True):
        self._deps(E, w, r)
        inst = fn(self.eng[E])
        self.n_inst += 1
        if inc:
            self.cnt[E] += 1
            inst.then_inc(self.sem[E], 1)
            tk = (E, self.semid[E], self.sem[E], self.cnt[E])
        else:
            tk = (E, self.semid[E], self.sem[E], self.cnt[E] + 1)
        self._mark(tk, w, r)
        return inst

    def dma(self, Q, out, in_, w=(), r=(), **kw):
        self._deps(Q, w, r)
        lst = self.dsem[Q]
        k = self.drr[Q]
        self.drr[Q] = (k + 1) % len(lst)
        ent = lst[k]
        if ent[1] > 0:
            self._wait(Q, ("dma", ent[2], ent[0], ent[1]), "raw")
        inst = self.eng[Q].dma_start(out=out, in_=in_, **kw)
        self.n_inst += 1
        ent[1] += 16
        inst.then_inc(ent[0], 16)
        tk = ("dma", ent[2], ent[0], ent[1])
        self._mark(tk, w, r)
        return tk

    def barrier(self):
        for E in self.ENGS:
            for E2 in self.ENGS:
                if E2 != E and self.cnt[E2] > 0:
                    self._wait(E, (E2, self.semid[E2], self.sem[E2], self.cnt[E2]), "raw")
            self.wait_all_dma(E)

    def wait_all_dma(self, E):
        for q in self.dsem:
            for ent in self.dsem[q]:
                if ent[1] > 0:
                    self._wait(E, ("dma", ent[2], ent[0], ent[1]), "raw")

    def mm(self, out, pairs, first=True, last=True):
        n = len(pairs)
        for i, (l, r_) in enumerate(pairs):
            st = first and i == 0
            sp = last and i == n - 1
            self.op("pe", lambda e, l=l, r_=r_, st=st, sp=sp: e.matmul(
                out.ap, l.ap, r_.ap, start=st, stop=sp), w=[out], r=[l, r_], inc=(i == n - 1))

    def act(self, out, in_, func, bias=None, scale=None, extra_r=(), accum=None):
        kw = {}
        rr = [in_] + list(extra_r)
        ww = [out]
        if bias is not None:
            if isinstance(bias, (V, T)):
                kw["bias"] = bias.ap
                rr.append(bias)
            else:
                kw["bias"] = bias
        if scale is not None:
            if isinstance(scale, (V, T)):
                kw["scale"] = scale.ap
                rr.append(scale)
            else:
                kw["scale"] = scale
        if accum is not None:
            kw["accum_out"] = accum.ap
            ww.append(accum)
        self.op("act", lambda e: e.activation(out.ap, in_.ap, func, **kw), w=ww, r=rr)

    def stt(self, out, in0, scalar, in1, op0, op1, E="dve"):
        rr = [in0, in1]
        sc = scalar
        if isinstance(scalar, (V, T)):
            sc = scalar.ap
            rr.append(scalar)
        self.op(E, lambda e: e.scalar_tensor_tensor(out.ap, in0.ap, sc, in1.ap, op0, op1), w=[out], r=rr)

    def tt(self, out, in0, in1, op, E="dve"):
        self.op(E, lambda e: e.tensor_tensor(out.ap, in0.ap, in1.ap, op), w=[out], r=[in0, in1])

    def ts(self, out, in0, s1, op0, s2=None, op1=None, E="dve"):
        rr = [in0]
        a1 = s1
        if isinstance(s1, (V, T)):
            a1 = s1.ap
            rr.append(s1)
        a2 = s2
        if isinstance(s2, (V, T)):
            a2 = s2.ap
            rr.append(s2)
        if op1 is None:
            self.op(E, lambda e: e.tensor_scalar(out.ap, in0.ap, a1, None, op0), w=[out], r=rr)
        else:
            self.op(E, lambda e: e.tensor_scalar(out.ap, in0.ap, a1, a2, op0, op1), w=[out], r=rr)

    def copy(self, out, in_, E="dve"):
        if E == "act":
            self.op("act", lambda e: e.activation(out.ap, in_.ap, AF.Copy), w=[out], r=[in_])
        else:
            self.op(E, lambda e: e.tensor_copy(out.ap, in_.ap), w=[out], r=[in_])


class Phase:
    _uid = 0

    def __init__(self, nc):
        from contextlib import ExitStack
        self.nc = nc
        self.st = ExitStack()
        self.k = 0

    def tile(self, shape, dt, name="t"):
        Phase._uid += 1
        h = self.st.enter_context(self.nc.sbuf_tensor(f"{name}_{Phase._uid}", list(shape), dt))
        return T(h.ap())

    def close(self):
        self.st.close()


WSPECS = {
    "ffn_w_gate": (D, DFF), "ffn_w_up": (D, DFF), "ffn_w_down": (DFF, D),
    "mem_w_kv": (D, 1024), "w_out": (2560, D),
    "a_w_in": (D, 6672), "b_w_in": (D, 1600), "b_w_uq": (512, 3072), "b_w_ukv": (512, 4096),
    "c_w_in": (D, 18944), "d_w_in": (D, 4688),
}


class K:
    def __init__(self, layers=(0, 1, 2, 3), stop=None, dbg=()):
        self.layers = layers
        self.stop = stop
        self.dbg = dbg
        nc = bass.Bass("TRN2", target_bir_lowering=False)
        self.nc = nc
        self.p = P(nc)
        self.inp = {}
        self.dbg_out = {}

    def din(self, name, shape, dt=F32):
        if self.stop == "xt" and (name in WSPECS):
            return None
        self.inp[name] = self.nc.dram_tensor(name, list(shape), dt, kind="ExternalInput").ap()
        return self.inp[name]

    def dscr(self, name, shape, dt):
        return self.nc.dram_tensor(name, list(shape), dt, kind="Internal").ap()

    def declare(self):
        nc = self.nc
        self.din("x", [S, D])
        self.din("mem", [256, D])
        self.din("positions", [1, S], I32)
        self.din("t5_table", [32, 16])
        self.din("ffn_norm", [DEPTH * 2 * 16, 128])
        for n in ("ffn_w_gate", "ffn_w_up", "ffn_w_down"):
            k, c = WSPECS[n]
            self.din(n, [DEPTH, 2, k, c])
        self.din("attn_norm", [DEPTH * 16, 128])
        self.din("mem_norm", [DEPTH * 16, 128])
        self.din("mem_w_kv", [DEPTH, D, 1024])
        self.din("mem_qk_g", [DEPTH * 2, 128])
        self.din("w_out", [DEPTH, 2560, D])
        self.din("a_w_in", [1, D, 6672])
        self.din("a_b_f", [16, 1])
        self.din("a_qk_g", [2, 128])
        self.din("b_w_in", [1, D, 1600])
        self.din("b_q_norm", [4, 128])
        self.din("b_w_uq", [1, 512, 3072])
        self.din("b_kv_norm", [4, 128])
        self.din("b_w_ukv", [1, 512, 4096])
        self.din("b_nope_g", [2, 128])
        self.din("b_rope_g", [2, 64])
        self.din("c_w_in", [1, D, 18944])
        self.din("c_qk_g", [6, 128])
        self.din("d_w_in", [1, D, 4688])
        self.din("d_qk_g", [2, 128])
        self.din("c_ident", [128, 128])
        self.din("c_identb", [128, 128], BF16)
        self.din("c_antib", [128, 128], BF16)
        self.din("c_onesb", [128, 128], BF16)
        self.din("c_cmask", [128, 4, 512], BF16)
        self.din("c_sel16", [16, 16, 128], BF16)
        self.din("c_sel80", [80, 16, 128], BF16)
        self.din("c_ohd", [32, 2688])
        self.din("c_ohg", [3, 33, 384])
        self.din("c_rope", [64, 2])
        self.out = nc.dram_tensor("out", [S, D], F32, kind="ExternalOutput").ap()
        self.XT = self.dscr("XT", [16, 128, S], F32)
        self.PT = self.dscr("PTs", [112, 128, S], BF16)
        self.VTM = self.dscr("VTM", [S, 6144], BF16)
        self.AT = self.dscr("ATs", [20, 128, S], BF16)
        self.FG = self.dscr("FGs", [16, S], F32)
        self.ROPE = self.dscr("ROPEs", [2, 64, S], F32)
        self.BEXT_h = self.nc.dram_tensor("BEXTs", [16, 2688], F32, kind="Internal")
        self.BEXT = self.BEXT_h.ap()
        self.TZ = self.dscr("TZs", [16, 128, 2560], BF16)
        self.VEXT_h = self.nc.dram_tensor("VEXTs", [48, 512], F32, kind="Internal")
        self.VEXT = self.VEXT_h.ap()
        self.WB = {}

    def dbgout(self, name, src_ap, shape, dt):
        o = self.nc.dram_tensor("dbg_" + name, list(shape), dt, kind="ExternalOutput").ap()
        self.p.barrier()
        self.p.dma("sp", o, src_ap)
        self.p.barrier()

    def load_consts(self):
        p = self.p
        nc = self.nc
        self.cst = Phase(nc)
        c = self.cst
        self.ident = c.tile([128, 128], F32, "ident")
        self.identb = c.tile([128, 128], BF16, "identb")
        self.antib = c.tile([128, 128], BF16, "antib")
        self.onesb = c.tile([128, 128], BF16, "onesb")
        for t, n in ((self.ident, "c_ident"), (self.identb, "c_identb"), (self.antib, "c_antib"),
                     (self.onesb, "c_onesb")):
            p.dma("sp", t.ap, self.inp[n], w=[t])
        self.ps = [T(nc.alloc_psum_tensor(f"ps{i}", [128, 512], F32).ap(), psum=True) for i in range(8)]
        self.g_ffn = self.colvecs("ffn_norm", 128)
        self.g_attn = self.colvecs("attn_norm", 64)
        self.g_mem = self.colvecs("mem_norm", 64)
        self.g_memqk = self.colvecs("mem_qk_g", 8)

    def colvecs(self, name, n):
        p = self.p
        c = self.cst
        st = c.tile([128, 128], F32, "cvst")
        out = c.tile([128, n], F32, "cv")
        p.dma("sp", st.ap[0:n, :], self.inp[name], w=[st])
        ps = self.ps[7]
        p.op("pe", lambda e: e.transpose(ps.ap[:, 0:n], st.ap[0:n, :], self.ident.ap[0:n, 0:n]),
             w=[ps], r=[st, self.ident])
        p.copy(out, ps[:, 0:n])
        return out

    def convert(self, key, src, Kdim, C):
        p = self.p
        KC = Kdim // 128
        dst = self.dscr("WB_" + key, [128, KC, C], BF16)
        self.WB[key] = dst
        CW = 2048
        jobs = [(kc, c0, min(CW, C - c0)) for kc in range(KC) for c0 in range(0, C, CW)]
        ph = self.cvph
        engs = ("pool", "act", "dve")
        for i, (kc, c0, cw) in enumerate(jobs):
            st = self.cv_st[i % 3]
            bf = self.cv_bf[i % 3]
            p.dma("sp", st.ap[:, 0:cw], src[kc * 128:(kc + 1) * 128, c0:c0 + cw], w=[st])
            E = engs[i % 3]
            p.copy(bf[:, 0:cw], st[:, 0:cw], E=E)
            p.dma("pool" if False else "sp", dst[:, kc, c0:c0 + cw], bf.ap[:, 0:cw], r=[bf])
        return dst

    def convert_all(self):
        nc = self.nc
        self.cvph = Phase(nc)
        self.cv_st = [self.cvph.tile([128, 2048], F32, "cvs") for _ in range(3)]
        self.cv_bf = [self.cvph.tile([128, 2048], BF16, "cvb") for _ in range(3)]
        mixw = {0: [("a_w_in", 0)], 1: [("b_w_in", 0), ("b_w_uq", 0), ("b_w_ukv", 0)],
                2: [("c_w_in", 0)], 3: [("d_w_in", 0)]}
        for L in self.layers:
            for s in (0, 1):
                for n in ("ffn_w_gate", "ffn_w_up", "ffn_w_down"):
                    k, c = WSPECS[n]
                    self.convert(f"{n}_{L}_{s}", self.inp[n][L, s], k, c)
            if self.stop == "ffn0":
                break
            self.convert(f"mem_w_kv_{L}", self.inp["mem_w_kv"][L], D, 1024)
            self.convert(f"w_out_{L}", self.inp["w_out"][L], 2560, D)
            for n, j in mixw[L % 4]:
                k, c = WSPECS[n]
                self.convert(f"{n}_{L}", self.inp[n][j], k, c)
        self.p.barrier()
        self.cvph.close()

    def x_to_xt(self):
        p = self.p
        ph = Phase(self.nc)
        xt = [ph.tile([128, D], F32, "xin") for _ in range(8)]
        ob = [ph.tile([128, 512], F32, "xo") for _ in range(3)]
        k = 0
        for tb in range(NSUB):
            tiles = []
            for j in range(4):
                t = xt[(tb % 2) * 4 + j]
                r0 = tb * 512 + j * 128
                p.dma("sp", t.ap, self.inp["x"][r0:r0 + 128, :], w=[t])
                tiles.append(t)
            for dc in range(16):
                ps = self.ps[dc % 4]
                for j in range(4):
                    p.op("pe", lambda e, j=j, ps=ps, dc=dc: e.transpose(
                        ps.ap[:, j * 128:(j + 1) * 128], tiles[j].ap[:, dc * 128:(dc + 1) * 128], self.ident.ap),
                        w=[ps], r=[tiles[j], self.ident], inc=(j == 3))
                o = ob[k % 3]
                k += 1
                p.copy(o, ps, E=("dve" if dc % 2 == 0 else "act"))
                p.dma("sp", self.XT[dc][:, tb * 512:(tb + 1) * 512], o.ap, r=[o])
        p.barrier()
        ph.close()

    def xt_to_out(self):
        p = self.p
        ph = Phase(self.nc)
        xin = [ph.tile([128, 512], F32, "xi") for _ in range(6)]
        ot = [ph.tile([128, D], F32, "xo") for _ in range(8)]
        k = 0
        for tb in range(NSUB):
            outs = [ot[(tb % 2) * 4 + j] for j in range(4)]
            for dc in range(16):
                xi = xin[k % 6]
                k += 1
                p.dma("sp", xi.ap, self.XT[dc][:, tb * 512:(tb + 1) * 512], w=[xi])
                ps = self.ps[dc % 4]
                for j in range(4):
                    p.op("pe", lambda e, j=j, ps=ps, xi=xi: e.transpose(
                        ps.ap[:, j * 128:(j + 1) * 128], xi.ap[:, j * 128:(j + 1) * 128], self.ident.ap),
                        w=[ps], r=[xi, self.ident], inc=(j == 3))
                for j in range(4):
                    p.copy(outs[j][:, dc * 128:(dc + 1) * 128], ps[:, j * 128:(j + 1) * 128],
                           E=("dve" if (j % 2 == 0 or os.environ.get("XO_DVE")) else "act"))
            for j in range(4):
                r0 = tb * 512 + j * 128
                p.dma("sp", self.out[r0:r0 + 128, :], outs[j].ap, r=[outs[j]])
        p.barrier()
        ph.close()

    def rstd_from_ssq(self, ps_ssq, n_feat, lnv, rstd):
        p = self.p
        p.act(lnv, ps_ssq, AF.Ln, bias=self.eps_col[:, 0:1], scale=1.0 / n_feat)
        p.act(rstd, lnv, AF.Exp, scale=-0.5)

    def ffn(self, L, s):
        p = self.p
        nc = self.nc
        ph = Phase(nc)
        Wg = self.WB[f"ffn_w_gate_{L}_{s}"]
        Wu = self.WB[f"ffn_w_up_{L}_{s}"]
        Wd = self.WB[f"ffn_w_down_{L}_{s}"]
        gcol = self.g_ffn
        gbase = (L * 2 + s) * 16
        xnT = [[ph.tile([128, 512], BF16, "xn") for _ in range(16)] for _ in range(2)]
        hT = [[ph.tile([128, 512], BF16, "h") for _ in range(44)] for _ in range(2)]
        xs = [ph.tile([128, 512], F32, "xs") for _ in range(4)]
        sq = [ph.tile([128, 512], BF16, "sq") for _ in range(2)]
        lnv = ph.tile([128, 512], F32, "lnv")
        rstd = ph.tile([128, 512], F32, "rstd")
        wg = [ph.tile([128, 16, 128], BF16, "wg") for _ in range(2)]
        wu = [ph.tile([128, 16, 128], BF16, "wu") for _ in range(2)]
        wd = [ph.tile([128, 44, 128], BF16, "wd") for _ in range(2)]
        sg = [ph.tile([128, 512], F32, "sg") for _ in range(2)]
        xo = [ph.tile([128, 512], F32, "xo") for _ in range(2)]
        ps = self.ps
        kx = 0
        for tb in range(S // 1024):
            for sub in range(2):
                tok = slice((2 * tb + sub) * 512, (2 * tb + sub + 1) * 512)
                for kc in range(16):
                    xt = xs[kx % 4]
                    kx += 1
                    p.dma("sp", xt.ap, self.XT[kc][:, tok], w=[xt])
                    q = sq[kc % 2]
                    p.act(q, xt, AF.Square)
                    p.mm(ps[0], [(self.onesb, q)], first=(kc == 0), last=(kc == 15))
                self.rstd_from_ssq(ps[0], D, lnv, rstd)
                for kc in range(16):
                    xt = xs[kx % 4]
                    kx += 1
                    p.dma("sp", xt.ap, self.XT[kc][:, tok], w=[xt])
                    p.stt(xnT[sub][kc], xt, gcol[:, gbase + kc:gbase + kc + 1], rstd, ALU.mult, ALU.mult)
            for fc in range(44):
                a = wg[fc % 2]
                b = wu[fc % 2]
                p.dma("sp", a.ap, Wg[:, :, fc * 128:(fc + 1) * 128], w=[a])
                p.dma("sp", b.ap, Wu[:, :, fc * 128:(fc + 1) * 128], w=[b])
                for sub in range(2):
                    pg = ps[1 + sub * 2]
                    pu = ps[2 + sub * 2]
                    p.mm(pg, [(a[:, kc, :], xnT[sub][kc]) for kc in range(16)])
                    p.mm(pu, [(b[:, kc, :], xnT[sub][kc]) for kc in range(16)])
                    g_ = sg[sub]
                    p.act(g_, pg, AF.Silu)
                    p.tt(hT[sub][fc], g_, pu, ALU.mult)
            for dc in range(16):
                w_ = wd[dc % 2]
                p.dma("sp", w_.ap, Wd[:, :, dc * 128:(dc + 1) * 128], w=[w_])
                for sub in range(2):
                    tok = slice((2 * tb + sub) * 512, (2 * tb + sub + 1) * 512)
                    py = ps[5 + sub]
                    p.mm(py, [(w_[:, fc, :], hT[sub][fc]) for fc in range(44)])
                    xt = xs[kx % 4]
                    kx += 1
                    p.dma("sp", xt.ap, self.XT[dc][:, tok], w=[xt])
                    o = xo[sub]
                    p.stt(o, py, 0.5, xt, ALU.mult, ALU.add)
                    p.dma("sp", self.XT[dc][:, tok], o.ap, r=[o])
        p.barrier()
        ph.close()

    def headnorm(self, ph_tiles, ps_in, ps_ssq, gcol, out_bf, nfeat=128, npart=128):
        p = self.p
        sq, lnv, rstd = ph_tiles
        p.act(sq[0:npart, :], ps_in[0:npart, :], AF.Square)
        p.mm(ps_ssq[0:npart, :], [(self.onesb[0:npart, 0:npart], sq[0:npart, :])])
        p.act(lnv[0:npart, :], ps_ssq[0:npart, :], AF.Ln, bias=self.eps_col[0:npart, 0:1], scale=1.0 / nfeat)
        p.act(rstd[0:npart, :], lnv[0:npart, :], AF.Exp, scale=-0.5)
        p.stt(out_bf, ps_in[0:npart, :], gcol, rstd[0:npart, :], ALU.mult, ALU.mult)

    def mem_setup(self):
        p = self.p
        ph = Phase(self.nc)
        mt = [ph.tile([128, D], F32, "memin") for _ in range(2)]
        sq = [ph.tile([128, 256], BF16, "msq") for _ in range(2)]
        lnv = ph.tile([128, 256], F32, "mln")
        for j in range(2):
            p.dma("sp", mt[j].ap, self.inp["mem"][j * 128:(j + 1) * 128, :], w=[mt[j]])
        for kc in range(16):
            ps = self.ps[kc % 2]
            for j in range(2):
                p.op("pe", lambda e, j=j, ps=ps, kc=kc: e.transpose(
                    ps.ap[:, j * 128:(j + 1) * 128], mt[j].ap[:, kc * 128:(kc + 1) * 128], self.ident.ap),
                    w=[ps], r=[mt[j], self.ident], inc=(j == 1))
            p.copy(self.memT[kc], ps[:, 0:256])
            q = sq[kc % 2]
            p.act(q, self.memT[kc], AF.Square)
            p.mm(self.ps[2][:, 0:256], [(self.onesb, q)], first=(kc == 0), last=(kc == 15))
        p.act(lnv, self.ps[2][:, 0:256], AF.Ln, bias=self.eps_col[:, 0:1], scale=1.0 / D)
        p.act(self.mem_rstd, lnv, AF.Exp, scale=-0.5)
        p.barrier()
        ph.close()

    def mem_kv(self, L, ph):
        p = self.p
        W = self.WB[f"mem_w_kv_{L}"]
        memn = [ph.tile([128, 256], BF16, "memn") for _ in range(16)]
        for kc in range(16):
            p.stt(memn[kc], self.memT[kc], self.g_mem[:, L * 16 + kc:L * 16 + kc + 1], self.mem_rstd,
                  ALU.mult, ALU.mult)
        kmT = [ph.tile([128, 256], BF16, "kmT") for _ in range(4)]
        vm = [ph.tile([128, 512], BF16, "vm") for _ in range(2)]
        wk = [ph.tile([128, 16, 128], BF16, "wmk") for _ in range(2)]
        wv = ph.tile([128, 16, 512], BF16, "wmv")
        sq = ph.tile([128, 512], BF16, "hsq")
        lnv = ph.tile([128, 512], F32, "hln")
        rstd = ph.tile([128, 512], F32, "hrs")
        for j in range(4):
            w_ = wk[j % 2]
            p.dma("sp", w_.ap, W[:, :, j * 128:(j + 1) * 128], w=[w_])
            ps = self.ps[j % 2]
            p.mm(ps[:, 0:256], [(w_[:, kc, :], memn[kc]) for kc in range(16)])
            self.headnorm((sq[:, 0:256], lnv[:, 0:256], rstd[:, 0:256]), ps[:, 0:256], self.ps[2][:, 0:256],
                          self.g_memqk[:, L * 2 + 1:L * 2 + 2], kmT[j])
        p.dma("sp", wv.ap, W[:, :, 512:1024], w=[wv])
        for t in range(2):
            ps = self.ps[3 + t]
            p.mm(ps, [(memn[kc][:, t * 128:(t + 1) * 128], wv[:, kc, :]) for kc in range(16)])
            p.copy(vm[t], ps)
        return kmT, vm

    def attn_chunk(self, ktiles, vtiles, qpairs_fn, extra_fn, bias_fn, pt_tiles, ps_s, ps_o, ps_d, rec, out_bf,
                   dv=128, nq=512):
        p = self.p
        n = ktiles

        def scores(j):
            p.mm(ps_s[j % 2][:, 0:nq], list(qpairs_fn(j)) + list(extra_fn(j)))

        scores(0)
        for j in range(n):
            pt = pt_tiles[j % 2]
            p.act(pt[:, 0:nq], ps_s[j % 2][:, 0:nq], AF.Exp, bias=bias_fn(j))
            if j + 1 < n:
                scores(j + 1)
            p.mm(ps_o[0:dv, 0:nq], [(vtiles(j), pt[:, 0:nq])], first=(j == 0), last=(j == n - 1))
            p.mm(ps_d[0:dv, 0:nq], [(self.onesb[:, 0:dv], pt[:, 0:nq])], first=(j == 0), last=(j == n - 1))
        p.op("dve", lambda e: e.reciprocal(rec.ap[0:dv, 0:nq], ps_d.ap[0:dv, 0:nq]), w=[rec], r=[ps_d])
        p.tt(out_bf, ps_o[0:dv, 0:nq], rec[0:dv, 0:nq], ALU.mult)

    def attention(self, L):
        m = L % 4
        if m == 1:
            self.rope_tables()
        self.in_proj(L)
        if m == 0:
            self.fox_core(L)
        elif m == 1:
            self.mla_core(L)
        elif m == 2:
            self.dil_core(L)
        elif m == 3:
            self.dsa_core(L)
        self.out_proj(L)

    def in_proj(self, L):
        p = self.p
        nc = self.nc
        m = L % 4
        ph = Phase(nc)
        wkey = {0: "a_w_in", 1: "b_w_in", 2: "c_w_in", 3: "d_w_in"}[m]
        W = self.WB[f"{wkey}_{L}"]
        ncols = WSPECS[wkey][1]
        memq0 = ncols - 512
        kmT, vm = self.mem_kv(L, ph)
        hT = [ph.tile([128, 512], BF16, "hT") for _ in range(16)]
        xs = [ph.tile([128, 512], F32, "xs") for _ in range(3)]
        sqx = [ph.tile([128, 512], BF16, "sqx") for _ in range(2)]
        lnv = ph.tile([128, 512], F32, "lnv")
        rstd = ph.tile([128, 512], F32, "rstd")
        hsq = ph.tile([128, 512], BF16, "hsq2")
        hsq2 = [hsq, ph.tile([128, 512], BF16, "hsq3")]
        hln = ph.tile([128, 512], F32, "hln2")
        hrs = ph.tile([128, 512], F32, "hrs2")
        wt = [ph.tile([128, 16, 128], BF16, "wt") for _ in range(2)]
        wvt = [ph.tile([128, 16, 512], BF16, "wvt") for _ in range(2)] if m in (0, 2, 3) else None
        ob = [ph.tile([128, 512], BF16, "ob") for _ in range(3)]
        qm = [ph.tile([128, 512], BF16, "qm") for _ in range(2)]
        ptt = [ph.tile([128, 512], BF16, "ptm") for _ in range(2)]
        rec = ph.tile([128, 512], F32, "rec")
        jobs = []
        vjobs = []
        if m == 0:
            self.gq_a = self.colv_small("a_qk_g", 2, ph)
            gq = ph.tile([128, 1], F32, "gqs")
            p.ts(gq, self.gq_a[:, 0:1], 128 ** -0.5, ALU.mult)
            for h in range(16):
                jobs.append((h * 128, 128, gq[:, 0:1], h))
            for h in range(16):
                jobs.append((2048 + h * 128, 128, self.gq_a[:, 1:2], 16 + h))
            for g in range(4):
                vjobs.append((4096 + g * 512, g * 512))
        if m == 2:
            self.gq_c = self.colv_small("c_qk_g", 6, ph)
            gqc = ph.tile([128, 3], F32, "gqc")
            for g in range(3):
                p.ts(gqc[:, g:g + 1], self.gq_c[:, 2 * g:2 * g + 1], 128 ** -0.5, ALU.mult)
            for g in range(3):
                for h in range(16):
                    jobs.append((g * 6144 + h * 128, 128, gqc[:, g:g + 1], g * 32 + h))
                    jobs.append((g * 6144 + 2048 + h * 128, 128, self.gq_c[:, 2 * g + 1:2 * g + 2], g * 32 + 16 + h))
                for v4 in range(4):
                    vjobs.append((g * 6144 + 4096 + v4 * 512, g * 2048 + v4 * 512))
        if m == 3:
            self.gq_d = self.colv_small("d_qk_g", 2, ph)
            gqd = ph.tile([128, 1], F32, "gqd")
            p.ts(gqd, self.gq_d[:, 0:1], 128 ** -0.5, ALU.mult)
            for h in range(16):
                jobs.append((h * 128, 128, gqd[:, 0:1], h))
            for g in range(4):
                jobs.append((2048 + g * 128, 128, self.gq_d[:, 1:2], 16 + g))
            vjobs.append((2560, 0))
            for h in range(16):
                jobs.append((3072 + h * 64, 64, 64 ** -0.5, 20 + h))
            jobs.append((4096, 64, 1.0, 36))
            jobs.append((4160, 16, 16 ** -0.5, 37))
        mla = None
        if m == 1:
            mla = self.mla_setup(L, ph)
            mla["wt"] = wt
        gmq = ph.tile([128, 1], F32, "gmq")
        p.ts(gmq, self.g_memqk[:, L * 2:L * 2 + 1], 128 ** -0.5, ALU.mult)
        groups = []
        jgroup = {}
        for ji, job in enumerate(jobs):
            c0, cw = job[0], job[1]
            if groups and groups[-1][0] + groups[-1][1] == c0 and groups[-1][1] + cw <= 512:
                groups[-1][1] += cw
                groups[-1][2].append(ji)
            else:
                groups.append([c0, cw, [ji]])
            jgroup[ji] = (len(groups) - 1, c0 - groups[-1][0])
        wgt = [ph.tile([128, 16, 512], BF16, "wgt") for _ in range(2)] if jobs else None
        kx = 0
        ko = 0
        for tb in range(NSUB):
            tok = slice(tb * 512, (tb + 1) * 512)
            for kc in range(16):
                xt = xs[kx % 3]
                kx += 1
                p.dma("sp", xt.ap, self.XT[kc][:, tok], w=[xt])
                q = sqx[kc % 2]
                p.act(q, xt, AF.Square)
                p.mm(self.ps[0], [(self.onesb, q)], first=(kc == 0), last=(kc == 15))
            self.rstd_from_ssq(self.ps[0], D, lnv, rstd)
            for kc in range(16):
                xt = xs[kx % 3]
                kx += 1
                p.dma("sp", xt.ap, self.XT[kc][:, tok], w=[xt])
                p.stt(hT[kc], xt, self.g_attn[:, L * 16 + kc:L * 16 + kc + 1], rstd, ALU.mult, ALU.mult)
            pend = None

            def stage_b(st):
                nonlocal ko
                ji, cw, gcol, chunk, ps = st
                o = ob[ko % 3]
                ko += 1
                if isinstance(gcol, float):
                    p.ts(o[0:cw, :], ps[0:cw, :], gcol, ALU.mult)
                else:
                    q = hsq2[ji % 2]
                    p.mm(self.ps[3][0:cw, :], [(self.onesb[0:cw, 0:cw], q[0:cw, :])])
                    p.act(hln[0:cw, :], self.ps[3][0:cw, :], AF.Ln, bias=self.eps_col[0:cw, 0:1], scale=1.0 / cw)
                    p.act(hrs[0:cw, :], hln[0:cw, :], AF.Exp, scale=-0.5)
                    p.stt(o[0:cw, :], ps[0:cw, :], gcol, hrs[0:cw, :], ALU.mult, ALU.mult)
                p.dma("sp", self.PT[chunk][0:cw, tok], o.ap[0:cw, :], r=[o])

            for ji, (c0, cw, gcol, chunk) in enumerate(jobs):
                gi, off = jgroup[ji]
                if groups[gi][2][0] == ji:
                    if gi == 0:
                        p.dma("sp", wgt[0].ap[:, :, 0:groups[0][1]], W[:, :, groups[0][0]:groups[0][0] + groups[0][1]],
                              w=[wgt[0]])
                    if gi + 1 < len(groups):
                        g1 = groups[gi + 1]
                        t1_ = wgt[(gi + 1) % 2]
                        p.dma("sp", t1_.ap[:, :, 0:g1[1]], W[:, :, g1[0]:g1[0] + g1[1]], w=[t1_])
                w_ = wgt[gi % 2]
                ps = self.ps[(1, 2, 4, 5)[ji % 4]]
                p.mm(ps[0:cw, :], [(w_[:, kc, off:off + cw], hT[kc]) for kc in range(16)])
                if not isinstance(gcol, float):
                    p.act(hsq2[ji % 2][0:cw, :], ps[0:cw, :], AF.Square)
                if pend is not None:
                    stage_b(pend)
                pend = (ji, cw, gcol, chunk, ps)
            if pend is not None:
                stage_b(pend)
            if m == 1:
                def ob_next():
                    nonlocal ko
                    o_ = ob[ko % 3]
                    ko += 1
                    return o_
                self.mla_block(L, mla, tb, hT, W, (hsq, hln, hrs), ob_next)
            if m == 0:
                w_ = wt[0]
                p.dma("sp", w_.ap[:, :, 0:16], W[:, :, 6144:6160], w=[w_])
                ps = self.ps[1]
                p.mm(ps[0:16, :], [(w_[:, kc, 0:16], hT[kc]) for kc in range(16)])
                o32 = xs[kx % 3]
                kx += 1
                p.copy(o32[0:16, :], ps[0:16, :])
                p.dma("sp", self.FG[:, tok], o32.ap[0:16, :], r=[o32])
            for vi, (c0, v0) in enumerate(vjobs):
                w_ = wvt[vi % 2]
                p.dma("sp", w_.ap, W[:, :, c0:c0 + 512], w=[w_])
                for t in range(4):
                    ps = self.ps[4 + t % 2]
                    p.mm(ps, [(hT[kc][:, t * 128:(t + 1) * 128], w_[:, kc, :]) for kc in range(16)])
                    o = ob[ko % 3]
                    ko += 1
                    p.copy(o, ps, E=("act" if t % 2 else "dve"))
                    r0 = tb * 512 + t * 128
                    p.dma("sp", self.VTM[r0:r0 + 128, v0:v0 + 512], o.ap, r=[o])
            for j in range(4):
                w_ = wt[j % 2]
                c0 = memq0 + j * 128
                p.dma("sp", w_.ap, W[:, :, c0:c0 + 128], w=[w_])
                ps = self.ps[1 + j % 2]
                p.mm(ps, [(w_[:, kc, :], hT[kc]) for kc in range(16)])
                q_ = qm[j % 2]
                self.headnorm((hsq, hln, hrs), ps, self.ps[3], gmq[:, 0:1], q_)
                o = ob[ko % 3]
                ko += 1
                self.attn_chunk(
                    2, lambda t, j=j: vm[t][:, j * 128:(j + 1) * 128],
                    lambda t, j=j, q_=q_: [(kmT[j][:, t * 128:(t + 1) * 128], q_)],
                    lambda t: [], lambda t: None, ptt, (self.ps[4], self.ps[5]), self.ps[6], self.ps[7], rec, o)
                p.dma("sp", self.AT[16 + j][:, tok], o.ap, r=[o])
        p.barrier()
        ph.close()

    def colv_small(self, name, n, ph):
        p = self.p
        w = self.inp[name].shape[1]
        st = ph.tile([128, 128], F32, "cvs")
        out = ph.tile([128, n], F32, "cvo")
        p.dma("sp", st.ap[0:n, 0:w], self.inp[name], w=[st])
        ps = self.ps[7]
        p.op("pe", lambda e: e.transpose(ps.ap[0:w, 0:n], st.ap[0:n, 0:w], self.ident.ap[0:n, 0:n]),
             w=[ps], r=[st, self.ident])
        p.copy(out[0:w, :], ps[0:w, 0:n])
        return out

    def fox_core(self, L):
        p = self.p
        nc = self.nc
        ph = Phase(nc)
        fg = ph.tile([80, S], F32, "fg")
        negbf = ph.tile([80, 1], F32, "negbf")
        p.op("pool", lambda e: e.memset(fg.ap, 0.0), w=[fg])
        p.op("pool", lambda e: e.memset(negbf.ap, 0.0), w=[negbf])
        for i in range(3):
            p.dma("sp", fg.ap[32 * i:32 * i + 16, :], self.FG, w=[fg])
            p.dma("sp", negbf.ap[32 * i:32 * i + 16, :], self.inp["a_b_f"], w=[negbf])
        p.ts(negbf, negbf, -1.0, ALU.mult)
        ones16 = ph.tile([80, S], F32, "ones16")
        p.op("pool", lambda e: e.memset(ones16.ap, 1.0), w=[ones16])
        lf = ph.tile([80, S], F32, "lf")
        ncum = ph.tile([80, S], F32, "ncum")
        p.act(lf, fg, AF.Exp, bias=negbf[:, 0:1], scale=-1.0)
        p.act(lf, lf, AF.Ln, bias=1.0)
        p.op("dve", lambda e: e.tensor_tensor_scan(ncum.ap, ones16.ap, lf.ap, 0.0, ALU.mult, ALU.add),
             w=[ncum], r=[ones16, lf])
        c_hi = ph.tile([80, S], BF16, "chi")
        c_mid = ph.tile([80, S], BF16, "cmid")
        c_lo = ph.tile([80, S], BF16, "clo")
        r1 = lf
        r2 = ones16
        p.ts(c_hi, ncum, -1.0, ALU.mult)
        p.stt(r1, ncum, -1.0, c_hi, ALU.mult, ALU.subtract)
        p.copy(c_mid, r1)
        p.tt(r2, r1, c_mid, ALU.subtract)
        p.copy(c_lo, r2)
        c_all = fg_b = ph.tile([80, S], BF16, "call")
        p.op("pool", lambda e: e.memset(c_all.ap, 0.0), w=[c_all])
        p.copy(c_all[0:16, :], c_hi[0:16, :])
        p.copy(c_all[32:48, :], c_mid[32:48, :])
        p.copy(c_all[64:80, :], c_lo[64:80, :])
        nct = ph.tile([128, 32, 16], F32, "nct")
        ps = self.ps[7]
        for j in range(32):
            p.op("pe", lambda e, j=j: e.transpose(ps.ap[:, j * 16:(j + 1) * 16], ncum.ap[0:16, j * 128:(j + 1) * 128],
                                                  self.ident.ap[0:16, 0:16]),
                 w=[ps], r=[ncum, self.ident], inc=(j == 31))
        p.copy(nct, ps.ap.rearrange("p (j h) -> p j h", h=16) if False else ps)
        sel = ph.tile([80, 16, 128], BF16, "sel80")
        p.dma("sp", sel.ap, self.inp["c_sel80"], w=[sel])
        cm = ph.tile([128, 4, 512], BF16, "cmask")
        p.dma("sp", cm.ap, self.inp["c_cmask"], w=[cm])
        qT = [ph.tile([128, S], BF16, "qT") for _ in range(2)]
        kT = [ph.tile([128, S], BF16, "kT") for _ in range(2)]
        vt = [ph.tile([128, 32, 128], BF16, "vt") for _ in range(2)]
        ptt = [ph.tile([128, 512], BF16, "pt") for _ in range(2)]
        rec = ph.tile([128, 512], F32, "rec")
        ob = [ph.tile([128, 512], BF16, "ob") for _ in range(2)]
        nctv = nct.ap
        ko = 0
        for h in range(16):
            q_ = qT[h % 2]
            k_ = kT[h % 2]
            v_ = vt[h % 2]
            p.dma("sp", q_.ap, self.PT[h], w=[q_])
            p.dma("sp", k_.ap, self.PT[16 + h], w=[k_])
            p.dma("sp", v_.ap, self.VTM[:, h * 128:(h + 1) * 128].rearrange("(j p) d -> p j d", p=128), w=[v_])
            for c in range(NSUB):
                qs = slice(c * 512, (c + 1) * 512)

                def qpairs(j, k_=k_, q_=q_, qs=qs):
                    return [(k_[:, j * 128:(j + 1) * 128], q_[:, qs])]

                def extra(j, c=c, h=h, qs=qs):
                    e = [(sel[:, h, :], c_all[:, qs])]
                    if j >= 4 * c:
                        e.append((self.identb, cm[:, j - 4 * c, :]))
                    return e

                def bias(j, h=h):
                    return V(nct, nctv[:, j, h:h + 1])

                o = ob[ko % 2]
                ko += 1
                self.attn_chunk(4 * c + 4, lambda j, v_=v_: v_[:, j, :], qpairs, extra, bias, ptt,
                                (self.ps[0], self.ps[1]), self.ps[2], self.ps[3], rec, o)
                p.dma("sp", self.AT[h][:, qs], o.ap, r=[o])
        p.barrier()
        ph.close()

    def dil_core(self, L):
        p = self.p
        nc = self.nc
        ph = Phase(nc)
        tab = ph.tile([33, 16], F32, "tab33")
        p.op("dve", lambda e: e.memset(tab.ap[32:33, :], NEG), w=[tab])
        p.dma("sp", tab.ap[0:32, :], self.inp["t5_table"], w=[tab])
        ohg = ph.tile([33, 3, 384], F32, "ohg")
        p.dma("sp", ohg.ap, self.inp["c_ohg"].rearrange("g v i -> v g i"), w=[ohg])
        vx = ph.tile([16, 384], F32, "vx")
        for g in range(3):
            ps = self.ps[g]
            p.mm(ps[0:16, 0:384], [(tab, ohg[:, g, :])])
            p.copy(vx, ps[0:16, 0:384])
            p.dma("sp", self.VEXT[g * 16:(g + 1) * 16, 0:384], vx.ap, r=[vx])
        p.barrier()
        bz = [[ph.tile([128, 256], BF16, "bz") for _ in range(16)] for _ in range(3)]
        hk = [ph.tile([128, 256], F32, "hk") for _ in range(2)]
        hkb = [ph.tile([128, 256], BF16, "hkb") for _ in range(2)]
        for g in range(3):
            for h in range(16):
                i = g * 16 + h
                a = hk[i % 2]
                b = hkb[i % 2]
                src = bass.AP(tensor=self.VEXT_h, offset=i * 512, ap=[[1, 128], [1, 256]])
                p.dma("sp", a.ap, src, w=[a])
                p.copy(b, a)
                ps = self.ps[i % 2]
                p.mm(ps[:, 0:256], [(self.antib, b)])
                p.copy(bz[g][h], ps[:, 0:256], E=("act" if i % 2 else "dve"))
        num = ph.tile([128, S], F32, "num")
        den = ph.tile([128, S], F32, "den")
        qT = [ph.tile([128, S], BF16, "qT") for _ in range(2)]
        kT = [ph.tile([128, S], BF16, "kT") for _ in range(2)]
        vt = [ph.tile([128, 32, 128], BF16, "vt") for _ in range(2)]
        ptt = [ph.tile([128, 128], BF16, "pt") for _ in range(2)]
        ob = ph.tile([128, S], BF16, "ob")
        kk = 0
        kb = 0
        for h in range(16):
            for g, dil in enumerate((1, 4, 16)):
                q_ = qT[kk % 2]
                k_ = kT[kk % 2]
                v_ = vt[kk % 2]
                kk += 1
                nb = S // dil // 128
                p.dma("sp", q_.ap, self.PT[g * 32 + h], w=[q_])
                p.dma("sp", k_.ap, self.PT[g * 32 + 16 + h], w=[k_])
                c0 = g * 2048 + h * 128
                vsrc = self.VTM[:, c0:c0 + 128].rearrange("(jj p r) d -> p r jj d", p=128, r=dil)
                vdst = v_.ap.rearrange("p (r jj) d -> p r jj d", r=dil)
                for r in range(dil):
                    p.dma("sp", vdst[:, r], vsrc[:, r], w=[v_])
                units = []
                for r in range(dil):
                    for i in range(nb):
                        tl = [i] if i == 0 else [i - 1, i]
                        for ti, jj in enumerate(tl):
                            units.append((r, i, ti, jj, len(tl)))

                def u_scores(ui):
                    r, i, ti, jj, nt = units[ui]
                    q0 = r + dil * 128 * i
                    k0 = r + dil * 128 * jj
                    qsl = slice(q0, q0 + dil * 127 + 1, dil)
                    ksl = slice(k0, k0 + dil * 127 + 1, dil)
                    bsl = slice(0, 128) if jj == i else slice(128, 256)
                    p.mm(self.ps[ui % 2][:, 0:128], [(k_[:, ksl], q_[:, qsl]), (self.identb, bz[g][h][:, bsl])])

                u_scores(0)
                for ui, (r, i, ti, jj, nt) in enumerate(units):
                    pt = ptt[ui % 2]
                    p.act(pt, self.ps[ui % 2][:, 0:128], AF.Exp)
                    if ui + 1 < len(units):
                        u_scores(ui + 1)
                    if ti == 0:
                        kb += 1
                    ps_o = self.ps[2 + kb % 2]
                    ps_d = self.ps[4 + kb % 2]
                    p.mm(ps_o[:, 0:128], [(v_[:, r * nb + jj, :], pt)], first=(ti == 0), last=(ti == nt - 1))
                    p.mm(ps_d[:, 0:128], [(self.onesb, pt)], first=(ti == 0), last=(ti == nt - 1))
                    if ti == nt - 1:
                        q0 = r + dil * 128 * i
                        qsl = slice(q0, q0 + dil * 127 + 1, dil)
                        if g == 0:
                            p.copy(num[:, qsl], ps_o[:, 0:128], E="act")
                            p.copy(den[:, qsl], ps_d[:, 0:128], E="dve")
                        else:
                            p.tt(num[:, qsl], ps_o[:, 0:128], num[:, qsl], ALU.add)
                            p.tt(den[:, qsl], ps_d[:, 0:128], den[:, qsl], ALU.add)
            p.op("dve", lambda e: e.reciprocal(den.ap, den.ap), w=[den], r=[den])
            p.tt(ob, num, den, ALU.mult)
            p.dma("sp", self.AT[h], ob.ap, r=[ob])
        p.barrier()
        ph.close()

    def rope_tables(self):
        p = self.p
        ph = Phase(self.nc)
        TWO_PI = 2.0 * math.pi
        cr = ph.tile([64, 2], F32, "crope")
        p.dma("sp", cr.ap, self.inp["c_rope"], w=[cr])
        posi = ph.tile([64, S], I32, "posi")
        pa = self.inp["positions"]
        p.dma("sp", posi.ap, bass.AP(tensor=pa.tensor, offset=0, ap=[[0, 64], [1, S]]), w=[posi])
        ang = ph.tile([64, S], F32, "ang")
        t1 = ph.tile([64, S], F32, "t1")
        ki = ph.tile([64, S], I32, "ki")
        p.copy(ang, posi)
        p.ts(ang, ang, cr[:, 0:1], ALU.mult)
        p.ts(t1, ang, 1.0 / TWO_PI, ALU.mult)
        p.copy(ki, t1)
        p.copy(t1, ki)
        r = ph.tile([64, S], F32, "r")
        p.stt(r, t1, -TWO_PI, ang, ALU.mult, ALU.add)
        p.ts(t1, r, math.pi, ALU.is_gt)
        p.stt(r, t1, -TWO_PI, r, ALU.mult, ALU.add)
        p.ts(t1, r, -1.0, ALU.mult, math.pi, ALU.is_gt)
        p.stt(r, t1, TWO_PI, r, ALU.mult, ALU.add)
        p.ts(r, r, 3.14159, ALU.min, -3.14159, ALU.max)
        p.act(t1, r, AF.Sin)
        p.ts(t1, t1, cr[:, 1:2], ALU.mult)
        p.dma("sp", self.ROPE[1], t1.ap, r=[t1])
        p.stt(ang, r, -1.0, r, ALU.mult, ALU.max)
        hp = ph.tile([64, 1], F32, "halfpi")
        p.op("dve", lambda e: e.memset(hp.ap, math.pi / 2), w=[hp])
        p.act(ang, ang, AF.Sin, bias=hp[:, 0:1], scale=-1.0)
        p.dma("sp", self.ROPE[0], ang.ap, r=[ang])
        p.barrier()
        ph.close()

    def mla_setup(self, L, ph):
        p = self.p
        st = {}
        st["gq"] = self.colv_small("b_q_norm", 4, ph)
        st["gkv"] = self.colv_small("b_kv_norm", 4, ph)
        gn = self.colv_small("b_nope_g", 2, ph)
        gr = self.colv_small("b_rope_g", 2, ph)
        sc = 192 ** -0.5
        gqn = ph.tile([128, 1], F32, "gqn")
        p.ts(gqn, gn[:, 0:1], sc, ALU.mult)
        st["gqn"] = gqn
        st["gkn"] = gn
        grs = ph.tile([64, 4], F32, "grs")
        p.ts(grs[:, 0:1], gr[0:64, 0:1], sc, ALU.mult)
        p.copy(grs[:, 2:3], gr[0:64, 1:2])
        stg = ph.tile([128, 128], F32, "grst")
        src = self.inp["b_rope_g"]
        p.dma("sp", stg.ap[0:2, 0:32], src[:, 32:64], w=[stg])
        p.dma("sp", stg.ap[0:2, 32:64], src[:, 0:32], w=[stg])
        ps = self.ps[7]
        p.op("pe", lambda e: e.transpose(ps.ap[0:64, 0:2], stg.ap[0:2, 0:64], self.ident.ap[0:2, 0:2]),
             w=[ps], r=[stg, self.ident])
        p.ts(grs[:, 1:2], ps[0:64, 0:1], sc, ALU.mult)
        p.copy(grs[:, 3:4], ps[0:64, 1:2])
        st["grs"] = grs
        st["cqf"] = [ph.tile([128, 512], F32, "cqf") for _ in range(4)]
        st["cqn"] = [ph.tile([128, 512], BF16, "cqn") for _ in range(4)]
        st["ckvn"] = [ph.tile([128, 512], BF16, "ckvn") for _ in range(4)]
        st["cs"] = [ph.tile([64, 512], F32, "cs") for _ in range(2)]
        st["ce"] = [ph.tile([64, 512], F32, "ce") for _ in range(2)]
        st["tt"] = [ph.tile([64, 512], F32, "ropet") for _ in range(2)]
        st["wq"] = [ph.tile([128, 4, 128], BF16, "wuq") for _ in range(2)]
        st["wq2"] = [ph.tile([128, 4, 64], BF16, "wuq2") for _ in range(2)]
        st["wv"] = [ph.tile([128, 4, 512], BF16, "wukvv") for _ in range(2)]
        return st

    def rope_apply(self, st, ps_a, ps_b, rstd64, g_a, g_b, out_bf):
        p = self.p
        ce = st["ce"]
        cs = st["cs"]
        tt = st["tt"]
        p.tt(ce[0], cs[0], rstd64, ALU.mult)
        p.tt(ce[1], cs[1], rstd64, ALU.mult)
        p.stt(tt[0], ps_a, g_a, ce[0], ALU.mult, ALU.mult)
        p.stt(tt[1], ps_b, g_b, ce[1], ALU.mult, ALU.mult)
        p.tt(out_bf, tt[0], tt[1], ALU.add)

    def mla_block(self, L, st, tb, hT, W, tmp, ob_next):
        p = self.p
        tok = slice(tb * 512, (tb + 1) * 512)
        hsq, hln, hrs = tmp
        Wuq = self.WB[f"b_w_uq_{L}"]
        Wukv = self.WB[f"b_w_ukv_{L}"]
        wt = st["wt"]
        p.dma("sp", st["cs"][0].ap, self.ROPE[0][:, tok], w=[st["cs"][0]])
        p.dma("sp", st["cs"][1].ap, self.ROPE[1][:, tok], w=[st["cs"][1]])
        for which, c_base, gcols, dst in (("q", 0, st["gq"], st["cqn"]), ("kv", 512, st["gkv"], st["ckvn"])):
            for kc in range(4):
                w_ = wt[kc % 2]
                p.dma("sp", w_.ap, W[:, :, c_base + kc * 128:c_base + (kc + 1) * 128], w=[w_])
                ps = self.ps[1 + kc % 2]
                p.mm(ps, [(w_[:, k2, :], hT[k2]) for k2 in range(16)])
                p.copy(st["cqf"][kc], ps)
                p.act(hsq, ps, AF.Square)
                p.mm(self.ps[3], [(self.onesb, hsq)], first=(kc == 0), last=(kc == 3))
            p.act(hln, self.ps[3], AF.Ln, bias=self.eps_col[:, 0:1], scale=1.0 / 512)
            p.act(hrs, hln, AF.Exp, scale=-0.5)
            for kc in range(4):
                p.stt(dst[kc], st["cqf"][kc], gcols[:, kc:kc + 1], hrs, ALU.mult, ALU.mult)
        w_ = wt[0]
        w2 = wt[1]
        p.dma("sp", w_.ap[:, :, 0:64], W[:, :, 1024:1088], w=[w_])
        p.dma("sp", w2.ap[:, :, 0:32], W[:, :, 1056:1088], w=[w2])
        p.dma("sp", w2.ap[:, :, 32:64], W[:, :, 1024:1056], w=[w2])
        pa = self.ps[1]
        pb = self.ps[2]
        p.mm(pa[0:64, :], [(w_[:, k2, 0:64], hT[k2]) for k2 in range(16)])
        p.mm(pb[0:64, :], [(w2[:, k2, 0:64], hT[k2]) for k2 in range(16)])
        self.rstd_part(pa, 64, hsq, hln, hrs)
        o = ob_next()
        self.rope_apply(st, pa[0:64, :], pb[0:64, :], hrs[0:64, :], st["grs"][:, 2:3], st["grs"][:, 3:4], o[0:64, :])
        p.dma("sp", self.PT[48][0:64, tok], o.ap[0:64, :], r=[o])
        for h in range(16):
            wq = st["wq"][h % 2]
            p.dma("sp", wq.ap, Wuq[:, :, h * 192:h * 192 + 128], w=[wq])
            ps = self.ps[1 + h % 2]
            p.mm(ps, [(wq[:, kc, :], st["cqn"][kc]) for kc in range(4)])
            o = ob_next()
            self.headnorm((hsq, hln, hrs), ps, self.ps[3], st["gqn"][:, 0:1], o)
            p.dma("sp", self.PT[h][:, tok], o.ap, r=[o])
            wa = st["wq2"][0]
            wb = st["wq2"][1]
            c0 = h * 192 + 128
            p.dma("sp", wa.ap, Wuq[:, :, c0:c0 + 64], w=[wa])
            p.dma("sp", wb.ap[:, :, 0:32], Wuq[:, :, c0 + 32:c0 + 64], w=[wb])
            p.dma("sp", wb.ap[:, :, 32:64], Wuq[:, :, c0:c0 + 32], w=[wb])
            pa = self.ps[4]
            pb = self.ps[5]
            p.mm(pa[0:64, :], [(wa[:, kc, :], st["cqn"][kc]) for kc in range(4)])
            p.mm(pb[0:64, :], [(wb[:, kc, :], st["cqn"][kc]) for kc in range(4)])
            self.rstd_part(pa, 64, hsq, hln, hrs)
            o = ob_next()
            self.rope_apply(st, pa[0:64, :], pb[0:64, :], hrs[0:64, :], st["grs"][:, 0:1], st["grs"][:, 1:2],
                            o[0:64, :])
            p.dma("sp", self.PT[16 + h][0:64, tok], o.ap[0:64, :], r=[o])
            wq = st["wq"][(h + 1) % 2]
            p.dma("sp", wq.ap, Wukv[:, :, h * 256:h * 256 + 128], w=[wq])
            ps = self.ps[1 + (h + 1) % 2]
            p.mm(ps, [(wq[:, kc, :], st["ckvn"][kc]) for kc in range(4)])
            o = ob_next()
            self.headnorm((hsq, hln, hrs), ps, self.ps[3], st["gkn"][:, 1:2], o)
            p.dma("sp", self.PT[32 + h][:, tok], o.ap, r=[o])
        wv5 = Wukv.rearrange("p kc (h two d) -> p kc h two d", two=2, d=128)
        for g4 in range(4):
            wv = st["wv"][g4 % 2]
            wdst = wv.ap.rearrange("p kc (h d) -> p kc h d", d=128)
            for kc in range(4):
                p.dma("sp", wdst[:, kc], wv5[:, kc, g4 * 4:(g4 + 1) * 4, 1, :], w=[wv])
            for t in range(4):
                ps = self.ps[4 + t % 2]
                p.mm(ps, [(st["ckvn"][kc][:, t * 128:(t + 1) * 128], wv[:, kc, :]) for kc in range(4)])
                o = ob_next()
                p.copy(o, ps, E=("act" if t % 2 else "dve"))
                r0 = tb * 512 + t * 128
                p.dma("sp", self.VTM[r0:r0 + 128, g4 * 512:(g4 + 1) * 512], o.ap, r=[o])

    def rstd_part(self, ps_in, npart, sq, lnv, rstd):
        p = self.p
        p.act(sq[0:npart, :], ps_in[0:npart, :], AF.Square)
        p.mm(self.ps[3][0:npart, :], [(self.onesb[0:npart, 0:npart], sq[0:npart, :])])
        p.act(lnv[0:npart, :], self.ps[3][0:npart, :], AF.Ln, bias=self.eps_col[0:npart, 0:1], scale=1.0 / npart)
        p.act(rstd[0:npart, :], lnv[0:npart, :], AF.Exp, scale=-0.5)

    def mla_core(self, L):
        p = self.p
        ph = Phase(self.nc)
        cm = ph.tile([128, 4, 512], BF16, "cmask")
        p.dma("sp", cm.ap, self.inp["c_cmask"], w=[cm])
        kr = ph.tile([64, S], BF16, "krT")
        p.dma("sp", kr.ap, self.PT[48][0:64, :], w=[kr])
        qT = [ph.tile([128, S], BF16, "qT") for _ in range(2)]
        qR = [ph.tile([64, S], BF16, "qR") for _ in range(2)]
        kT = [ph.tile([128, S], BF16, "kT") for _ in range(2)]
        vt = [ph.tile([128, 32, 128], BF16, "vt") for _ in range(2)]
        ptt = [ph.tile([128, 512], BF16, "pt") for _ in range(2)]
        rec = ph.tile([128, 512], F32, "rec")
        ob = [ph.tile([128, 512], BF16, "ob") for _ in range(2)]
        ko = 0
        for h in range(16):
            q_ = qT[h % 2]
            r_ = qR[h % 2]
            k_ = kT[h % 2]
            v_ = vt[h % 2]
            p.dma("sp", q_.ap, self.PT[h], w=[q_])
            p.dma("sp", r_.ap, self.PT[16 + h][0:64, :], w=[r_])
            p.dma("sp", k_.ap, self.PT[32 + h], w=[k_])
            p.dma("sp", v_.ap, self.VTM[:, h * 128:(h + 1) * 128].rearrange("(j p) d -> p j d", p=128), w=[v_])
            for c in range(NSUB):
                qs = slice(c * 512, (c + 1) * 512)

                def qpairs(j, k_=k_, q_=q_, r_=r_, qs=qs):
                    ks = slice(j * 128, (j + 1) * 128)
                    return [(k_[:, ks], q_[:, qs]), (kr[:, ks], r_[:, qs])]

                def extra(j, c=c):
                    if j >= 4 * c:
                        return [(self.identb, cm[:, j - 4 * c, :])]
                    return []

                o = ob[ko % 2]
                ko += 1
                self.attn_chunk(4 * c + 4, lambda j, v_=v_: v_[:, j, :], qpairs, extra, lambda j: None, ptt,
                                (self.ps[0], self.ps[1]), self.ps[2], self.ps[3], rec, o)
                p.dma("sp", self.AT[h][:, qs], o.ap, r=[o])
        p.barrier()
        ph.close()

    def dsa_core(self, L):
        p = self.p
        nc = self.nc
        ph = Phase(nc)
        tab = ph.tile([32, 16], F32, "tab")
        p.dma("sp", tab.ap, self.inp["t5_table"], w=[tab])
        ohd = ph.tile([32, 2688], F32, "ohd")
        p.dma("sp", ohd.ap, self.inp["c_ohd"], w=[ohd])
        bv = ph.tile([16, 2688], F32, "bv")
        for i in range(6):
            w_ = min(512, 2688 - i * 512)
            ps = self.ps[i % 2]
            p.mm(ps[0:16, 0:w_], [(tab, ohd[:, i * 512:i * 512 + w_])])
            p.copy(bv[:, i * 512:i * 512 + w_], ps[0:16, 0:w_])
        p.dma("sp", self.BEXT, bv.ap, r=[bv])
        p.barrier()
        hk = [ph.tile([128, 2560], F32, "hk") for _ in range(2)]
        hkb = [ph.tile([128, 2560], BF16, "hkb") for _ in range(2)]
        tzb = [ph.tile([128, 2560], BF16, "tzb") for _ in range(2)]
        for h in range(16):
            a = hk[h % 2]
            b = hkb[h % 2]
            t = tzb[h % 2]
            p.dma("sp", a.ap, bass.AP(tensor=self.BEXT_h, offset=h * 2688, ap=[[1, 128], [1, 2560]]), w=[a])
            p.copy(b, a, E=("act" if h % 2 else "dve"))
            for i in range(5):
                ps = self.ps[2 + i % 2]
                p.mm(ps, [(self.antib, b[:, i * 512:(i + 1) * 512])])
                p.copy(t[:, i * 512:(i + 1) * 512], ps, E=("dve" if i % 2 else "act"))
            p.dma("sp", self.TZ[h], t.ap, r=[t])
        p.barrier()
        ph.close()
        ph = Phase(nc)
        cm = ph.tile([128, 4, 512], BF16, "cmask")
        p.dma("sp", cm.ap, self.inp["c_cmask"], w=[cm])
        sel = ph.tile([16, 16, 128], BF16, "sel16")
        p.dma("sp", sel.ap, self.inp["c_sel16"], w=[sel])
        kiT = ph.tile([64, S], BF16, "kiT")
        p.dma("sp", kiT.ap, self.PT[36][0:64, :], w=[kiT])
        wiT = ph.tile([16, 512], BF16, "wiT")
        idx = [ph.tile([128, 512], F32, "idx") for _ in range(32)]
        selb = [ph.tile([128, 512], BF16, "selb") for _ in range(32)]
        wrep = [ph.tile([128, 512], F32, "wrep") for _ in range(2)]
        qi = [ph.tile([64, 512], BF16, "qi") for _ in range(2)]
        tmp = [ph.tile([128, 512], F32, "itmp") for _ in range(2)]
        cmp = [ph.tile([128, 512], BF16, "cmp") for _ in range(2)]
        lo = ph.tile([128, 512], F32, "lo")
        mid = ph.tile([128, 512], F32, "mid")
        tsel = ph.tile([128, 512], F32, "tsel")
        tz = [ph.tile([128, 2560], BF16, "tz") for _ in range(2)]
        qT = [ph.tile([128, 512], BF16, "qT") for _ in range(2)]
        kT = [ph.tile([128, S], BF16, "kT") for _ in range(2)]
        vt = [ph.tile([128, 32, 128], BF16, "vt") for _ in range(2)]
        ptt = [ph.tile([128, 512], BF16, "pt") for _ in range(2)]
        rec = ph.tile([128, 512], F32, "rec")
        ob = [ph.tile([128, 512], BF16, "ob") for _ in range(2)]
        NIT = 24
        ko = 0
        kq = 0
        kg = 0
        for c in range(NSUB):
            qs = slice(c * 512, (c + 1) * 512)
            nk = 4 * c + 4
            p.dma("sp", wiT.ap, self.PT[37][0:16, qs], w=[wiT])
            for h in range(16):
                wr = wrep[h % 2]
                ps = self.ps[0]
                p.mm(ps, [(sel[:, h, :], wiT)])
                p.copy(wr, ps, E="act")
                q_ = qi[h % 2]
                p.dma("sp", q_.ap, self.PT[20 + h][0:64, qs], w=[q_])
                for j in range(nk):
                    ps = self.ps[1 + j % 2]
                    p.mm(ps, [(kiT[:, j * 128:(j + 1) * 128], q_)])
                    if h == 0:
                        p.stt(idx[j], ps, 0.0, wr, ALU.max, ALU.mult)
                    else:
                        t_ = tmp[j % 2]
                        p.stt(t_, ps, 0.0, wr, ALU.max, ALU.mult)
                        p.tt(idx[j], idx[j], t_, ALU.add, E="pool")
            for j in range(4 * c, nk):
                p.tt(idx[j], idx[j], cm[:, j - 4 * c, :], ALU.add, E="pool")
            p.op("dve", lambda e: e.memset(lo.ap, -64.0), w=[lo])
            for it in range(NIT):
                ck = 64.0 / (2 ** it)
                p.ts(mid, lo, ck, ALU.add)
                pc = self.ps[3 + it % 2]
                for j in range(nk):
                    cp = cmp[j % 2]
                    p.tt(cp, idx[j], mid, ALU.is_ge)
                    p.mm(pc, [(self.onesb, cp)], first=(j == 0), last=(j == nk - 1))
                p.ts(tsel, pc, 255.5, ALU.is_ge, ck, ALU.mult)
                p.tt(lo, lo, tsel, ALU.add)
            for j in range(nk):
                cp = tmp[j % 2]
                p.tt(cp, idx[j], lo, ALU.is_ge)
                p.ts(selb[j], cp, -1.0, ALU.add, -NEG, ALU.mult, E="pool")
            tzw = min(512 * c, 1664) + 896
            for g in range(4):
                k_ = kT[kg % 2]
                v_ = vt[kg % 2]
                kg += 1
                p.dma("sp", k_.ap[:, 0:nk * 128], self.PT[16 + g][:, 0:nk * 128], w=[k_])
                p.dma("sp", v_.ap[:, 0:nk, :],
                      self.VTM[0:nk * 128, g * 128:(g + 1) * 128].rearrange("(j p) d -> p j d", p=128), w=[v_])
                for r in range(4):
                    h = g * 4 + r
                    q_ = qT[kq % 2]
                    z_ = tz[kq % 2]
                    kq += 1
                    p.dma("sp", q_.ap, self.PT[h][:, qs], w=[q_])
                    p.dma("sp", z_.ap[:, 0:tzw], self.TZ[h][:, 0:tzw], w=[z_])

                    def qpairs(j, k_=k_, q_=q_):
                        return [(k_[:, j * 128:(j + 1) * 128], q_)]

                    def extra(j, c=c, z_=z_):
                        d0 = min(512 * c - 128 * j, 1664)
                        m0 = d0 + 384
                        return [(self.identb, z_[:, m0:m0 + 512]), (self.identb, selb[j])]

                    o = ob[ko % 2]
                    ko += 1
                    self.attn_chunk(nk, lambda j, v_=v_: v_[:, j, :], qpairs, extra, lambda j: None, ptt,
                                    (self.ps[5], self.ps[6]), self.ps[7], self.ps[0], rec, o)
                    p.dma("sp", self.AT[h][:, qs], o.ap, r=[o])
        p.barrier()
        ph.close()

    def out_proj(self, L):
        p = self.p
        ph = Phase(self.nc)
        W = self.WB[f"w_out_{L}"]
        m = L % 4
        nch = 20
        at = [[ph.tile([128, 512], BF16, "at") for _ in range(nch)] for _ in range(2)]
        wt = [ph.tile([128, 20, 128], BF16, "wo") for _ in range(2)]
        xs = [ph.tile([128, 512], F32, "xs") for _ in range(2)]
        xo = [ph.tile([128, 512], F32, "xo") for _ in range(2)]
        c_start = 0
        for tb in range(NSUB):
            tok = slice(tb * 512, (tb + 1) * 512)
            a = at[tb % 2]
            for c in range(c_start, nch):
                p.dma("sp", a[c].ap, self.AT[c][:, tok], w=[a[c]])
            for dc in range(16):
                w_ = wt[dc % 2]
                p.dma("sp", w_.ap, W[:, :, dc * 128:(dc + 1) * 128], w=[w_])
                ps = self.ps[dc % 2]
                p.mm(ps, [(w_[:, c, :], a[c]) for c in range(c_start, nch)])
                xt = xs[dc % 2]
                p.dma("sp", xt.ap, self.XT[dc][:, tok], w=[xt])
                o = xo[dc % 2]
                p.tt(o, ps, xt, ALU.add)
                p.dma("sp", self.XT[dc][:, tok], o.ap, r=[o])
        p.barrier()
        ph.close()

    def build(self):
        p = self.p
        self.declare()
        self.load_consts()
        self.eps_col = self.cst.tile([128, 1], F32, "eps")
        self.memT = [self.cst.tile([128, 256], F32, "memT") for _ in range(16)]
        self.mem_rstd = self.cst.tile([128, 256], F32, "memrstd")

        p.op("dve", lambda e: e.memset(self.eps_col.ap, EPS), w=[self.eps_col])
        if self.stop != "xt":
            self.convert_all()
        if self.stop not in ("xt", "cv", "ffn0"):
            self.mem_setup()
        import os
        if not os.environ.get("SKIP_XT"):
            self.x_to_xt()
        for L in (() if self.stop in ("xt", "cv") else self.layers):
            self.ffn(L, 0)
            if self.stop == "ffn0":
                break
            self.attention(L)
            if self.stop == f"att{L}":
                break
            self.ffn(L, 1)
            if self.stop == f"ffn1_{L}":
                break
        if not os.environ.get("SKIP_OUT"):
            self.xt_to_out()
        p.wait_all_dma("sp")
        return self.nc


def t5_bucket_np(dist):
    n = np.maximum(dist, 0)
    nf = np.maximum(n, 1).astype(np.float32)
    large = 16 + (np.log(nf / np.float32(16)) / np.float32(math.log(2048 / 16)) * np.float32(16)).astype(np.int32)
    large = np.minimum(large, 31)
    return np.where(n < 16, n, large)


def host_consts():
    bf = ml_dtypes.bfloat16
    c = {}
    c["c_ident"] = np.eye(128, dtype=np.float32)
    c["c_identb"] = np.eye(128, dtype=np.float32).astype(bf)
    c["c_antib"] = np.eye(128, dtype=np.float32)[::-1].copy().astype(bf)
    c["c_onesb"] = np.ones((128, 128), dtype=np.float32).astype(bf)
    k = np.arange(128)[:, None, None]
    o = np.arange(4)[None, :, None]
    q = np.arange(512)[None, None, :]
    c["c_cmask"] = np.where(128 * o + k <= q, 0.0, NEG).astype(np.float32).astype(bf)
    sel = np.zeros((16, 16, 128), dtype=np.float32)
    for h in range(16):
        sel[h, h, :] = 1.0
    c["c_sel16"] = sel.astype(bf)
    sel80 = np.zeros((80, 16, 128), dtype=np.float32)
    for i in range(3):
        sel80[32 * i:32 * i + 16] = sel
    c["c_sel80"] = sel80.astype(bf)
    i = np.arange(2688)
    dist = i - 511
    oh = np.zeros((32, 2688), dtype=np.float32)
    b = t5_bucket_np(dist)
    valid = dist >= 0
    oh[b[valid], i[valid]] = 1.0
    c["c_ohd"] = oh
    ohg = np.zeros((3, 33, 384), dtype=np.float32)
    for g, dil in enumerate((1, 4, 16)):
        for ii in range(384):
            rel = ii - 127
            if 0 <= rel <= 128:
                ohg[g, int(t5_bucket_np(np.array(rel * dil))), ii] = 1.0
            else:
                ohg[g, 32, ii] = 1.0
    c["c_ohg"] = ohg
    half = 32
    inv = (np.float32(10000.0) ** (-np.arange(half, dtype=np.float32) / np.float32(half))).astype(np.float32)
    rope = np.zeros((64, 2), dtype=np.float32)
    rope[:, 0] = np.concatenate([inv, inv])
    rope[:, 1] = np.concatenate([-np.ones(32), np.ones(32)])
    c["c_rope"] = rope
    return c


def prep_inputs(inputs, b):
    f = lambda a: np.ascontiguousarray(a)
    m = {}
    m["x"] = f(inputs["x"][b])
    m["mem"] = f(inputs["mem"][b])
    m["positions"] = f(inputs["positions"][b].reshape(1, S).astype(np.int32))
    m["t5_table"] = f(inputs["t5_table"])
    m["ffn_norm"] = f(inputs["ffn_norm"].reshape(DEPTH * 2 * 16, 128))
    for n in ("ffn_w_gate", "ffn_w_up", "ffn_w_down", "mem_w_kv", "w_out", "a_w_in", "b_w_in", "b_w_uq",
              "b_w_ukv", "c_w_in", "d_w_in"):
        m[n] = inputs[n]
    m["attn_norm"] = f(inputs["attn_norm"].reshape(DEPTH * 16, 128))
    m["mem_norm"] = f(inputs["mem_norm"].reshape(DEPTH * 16, 128))
    m["mem_qk_g"] = f(inputs["mem_qk_g"].reshape(DEPTH * 2, 128))
    m["a_b_f"] = f(inputs["a_b_f"].reshape(16, 1))
    m["a_qk_g"] = f(inputs["a_qk_g"].reshape(2, 128))
    m["b_q_norm"] = f(inputs["b_q_norm"].reshape(4, 128))
    m["b_kv_norm"] = f(inputs["b_kv_norm"].reshape(4, 128))
    m["b_nope_g"] = f(inputs["b_nope_g"].reshape(2, 128))
    m["b_rope_g"] = f(inputs["b_rope_g"].reshape(2, 64))
    m["c_qk_g"] = f(inputs["c_qk_g"].reshape(6, 128))
    m["d_qk_g"] = f(inputs["d_qk_g"].reshape(2, 128))
    return m


_CACHE = {}


def kernel(**inputs):
    inputs = {k: np.asarray(v) for k, v in inputs.items()}
    if "nc" not in _CACHE:
        _CACHE["nc"] = K().build()
    nc = _CACHE["nc"]
    consts = host_consts()
    in_maps = []
    for b in range(8):
        m = prep_inputs(inputs, b)
        m.update(consts)
        in_maps.append(m)
    res = run_bass_kernel_spmd(nc, in_maps, core_ids=list(range(8)))
    out = np.stack([np.asarray(r["out"]) for r in res.results], axis=0)
    return out.astype(np.float32)
```

```python
import math
import os
import numpy as np
import ml_dtypes
import concourse.bass as bass
import concourse.mybir as mybir
from concourse.bass_utils import run_bass_kernel_spmd

F32 = mybir.dt.float32
BF16 = mybir.dt.bfloat16
I32 = mybir.dt.int32
AF = mybir.ActivationFunctionType
ALU = mybir.AluOpType
AX = mybir.AxisListType

S = 4096
D = 2048
DFF = 5632
NSUB = S // 512
DEPTH = 4
EPS = 1e-6
NEG = -30000.0


class T:
    __slots__ = ("ap", "w", "r", "psum")

    def __init__(self, ap, psum=False):
        self.ap = ap
        self.w = None
        self.r = {}
        self.psum = psum

    def __getitem__(self, idx):
        return V(self, self.ap[idx])


class V:
    __slots__ = ("t", "ap")

    def __init__(self, t, ap):
        self.t = t
        self.ap = ap

    def __getitem__(self, idx):
        return V(self.t, self.ap[idx])


def _tile(x):
    return x.t if isinstance(x, V) else x


class P:
    ENGS = ("pe", "act", "dve", "pool", "sp")

    def __init__(self, nc, n_dma_sems=8):
        self.nc = nc
        self.eng = {"pe": nc.tensor, "act": nc.scalar, "dve": nc.vector,
                    "pool": nc.gpsimd, "sp": nc.sync}
        self.sem = {}
        self.cnt = {}
        self.semid = {}
        self._nid = 0
        for e in self.ENGS:
            self._nid += 1
            self.sem[e] = nc.alloc_semaphore(f"s_{e}")
            self.cnt[e] = 0
            self.semid[e] = self._nid
        self.dsem = {}
        for q in ("sp", "act", "pool"):
            lst = []
            for i in range(n_dma_sems):
                self._nid += 1
                lst.append([nc.alloc_semaphore(f"d_{q}{i}"), 0, self._nid])
            self.dsem[q] = lst
        self.drr = {"sp": 0, "act": 0, "pool": 0}
        self.waited = {e: {} for e in self.ENGS}
        self.n_inst = 0
        self.n_wait = 0

    def _wait(self, E, tk, kind):
        if tk is None:
            return
        src, sid, sh, v = tk
        if src == E:
            if E == "pe" or kind != "raw":
                return
        wd = self.waited[E]
        if wd.get(sid, 0) >= v:
            return
        self.eng[E].wait_ge(sh, v)
        self.n_wait += 1
        wd[sid] = v

    def _deps(self, E, w, r):
        for x in r:
            t = _tile(x)
            if t is not None:
                self._wait(E, t.w, "raw")
                if t.psum:
                    for tk in t.r.values():
                        self._wait(E, tk, "war")
        for x in w:
            t = _tile(x)
            if t is not None:
                self._wait(E, t.w, "waw")
                for tk in t.r.values():
                    self._wait(E, tk, "war")

    def _mark(self, tk, w, r):
        for x in r:
            t = _tile(x)
            if t is not None:
                t.r[tk[1]] = tk
        for x in w:
            t = _tile(x)
            if t is not None:
                t.w = tk
                t.r = {}

    def op(self, E, fn, w=(), r=(), inc=True):
        self._deps(E, w, r)
        inst = fn(self.eng[E])
        self.n_inst += 1
        if inc:
            self.cnt[E] += 1
            inst.then_inc(self.sem[E], 1)
            tk = (E, self.semid[E], self.sem[E], self.cnt[E])
        else:
            tk = (E, self.semid[E], self.sem[E], self.cnt[E] + 1)
        self._mark(tk, w, r)
        return inst

    def dma(self, Q, out, in_, w=(), r=(), **kw):
        self._deps(Q, w, r)
        lst = self.dsem[Q]
        k = self.drr[Q]
        self.drr[Q] = (k + 1) % len(lst)
        ent = lst[k]
        if ent[1] > 0:
            self._wait(Q, ("dma", ent[2], ent[0], ent[1]), "raw")
        inst = self.eng[Q].dma_start(out=out, in_=in_, **kw)
        self.n_inst += 1
        ent[1] += 16
        inst.then_inc(ent[0], 16)
        tk = ("dma", ent[2], ent[0], ent[1])
        self._mark(tk, w, r)
        return tk

    def barrier(self):
        for E in self.ENGS:
            for E2 in self.ENGS:
                if E2 != E and self.cnt[E2] > 0:
                    self._wait(E, (E2, self.semid[E2], self.sem[E2], self.cnt[E2]), "raw")
            self.wait_all_dma(E)

    def wait_all_dma(self, E):
        for q in self.dsem:
            for ent in self.dsem[q]:
                if ent[1] > 0:
                    self._wait(E, ("dma", ent[2], ent[0], ent[1]), "raw")

    def mm(self, out, pairs, first=True, last=True):
        n = len(pairs)
        for i, (l, r_) in enumerate(pairs):
            st = first and i == 0
            sp = last and i == n - 1
            self.op("pe", lambda e, l=l, r_=r_, st=st, sp=sp: e.matmul(
                out.ap, l.ap, r_.ap, start=st, stop=sp), w=[out], r=[l, r_], inc=(i == n - 1))

    def act(self, out, in_, func, bias=None, scale=None, extra_r=(), accum=None):
        kw = {}
        rr = [in_] + list(extra_r)
        ww = [out]
        if bias is not None:
            if isinstance(bias, (V, T)):
                kw["bias"] = bias.ap
                rr.append(bias)
            else:
                kw["bias"] = bias
        if scale is not None:
            if isinstance(scale, (V, T)):
                kw["scale"] = scale.ap
                rr.append(scale)
            else:
                kw["scale"] = scale
        if accum is not None:
            kw["accum_out"] = accum.ap
            ww.append(accum)
        self.op("act", lambda e: e.activation(out.ap, in_.ap, func, **kw), w=ww, r=rr)

    def stt(self, out, in0, scalar, in1, op0, op1, E="dve"):
        rr = [in0, in1]
        sc = scalar
        if isinstance(scalar, (V, T)):
            sc = scalar.ap
            rr.append(scalar)
        self.op(E, lambda e: e.scalar_tensor_tensor(out.ap, in0.ap, sc, in1.ap, op0, op1), w=[out], r=rr)

    def tt(self, out, in0, in1, op, E="dve"):
        self.op(E, lambda e: e.tensor_tensor(out.ap, in0.ap, in1.ap, op), w=[out], r=[in0, in1])

    def ts(self, out, in0, s1, op0, s2=None, op1=None, E="dve"):
        rr = [in0]
        a1 = s1
        if isinstance(s1, (V, T)):
            a1 = s1.ap
            rr.append(s1)
        a2 = s2
        if isinstance(s2, (V, T)):
            a2 = s2.ap
            rr.append(s2)
        if op1 is None:
            self.op(E, lambda e: e.tensor_scalar(out.ap, in0.ap, a1, None, op0), w=[out], r=rr)
        else:
            self.op(E, lambda e: e.tensor_scalar(out.ap, in0.ap, a1, a2, op0, op1), w=[out], r=rr)

    def copy(self, out, in_, E="dve"):
        if E == "act":
            self.op("act", lambda e: e.activation(out.ap, in_.ap, AF.Copy), w=[out], r=[in_])
        else:
            self.op(E, lambda e: e.tensor_copy(out.ap, in_.ap), w=[out], r=[in_])


class Phase:
    _uid = 0

    def __init__(self, nc):
        from contextlib import ExitStack
        self.nc = nc
        self.st = ExitStack()
        self.k = 0

    def tile(self, shape, dt, name="t"):
        Phase._uid += 1
        h = self.st.enter_context(self.nc.sbuf_tensor(f"{name}_{Phase._uid}", list(shape), dt))
        return T(h.ap())

    def close(self):
        self.st.close()


WSPECS = {
    "ffn_w_gate": (D, DFF), "ffn_w_up": (D, DFF), "ffn_w_down": (DFF, D),
    "mem_w_kv": (D, 1024), "w_out": (2560, D),
    "a_w_in": (D, 6672), "b_w_in": (D, 1600), "b_w_uq": (512, 3072), "b_w_ukv": (512, 4096),
    "c_w_in": (D, 18944), "d_w_in": (D, 4688),
}


class K:
    def __init__(self, layers=(0, 1, 2, 3), stop=None, dbg=()):
        self.layers = layers
        self.stop = stop
        self.dbg = dbg
        nc = bass.Bass("TRN2", target_bir_lowering=False)
        self.nc = nc
        self.p = P(nc)
        self.inp = {}
        self.dbg_out = {}

    def din(self, name, shape, dt=F32):
        if self.stop == "xt" and (name in WSPECS):
            return None
        self.inp[name] = self.nc.dram_tensor(name, list(shape), dt, kind="ExternalInput").ap()
        return self.inp[name]

    def dscr(self, name, shape, dt):
        return self.nc.dram_tensor(name, list(shape), dt, kind="Internal").ap()

    def declare(self):
        nc = self.nc
        self.din("x", [S, D])
        self.din("mem", [256, D])
        self.din("positions", [1, S], I32)
        self.din("t5_table", [32, 16])
        self.din("ffn_norm", [DEPTH * 2 * 16, 128])
        for n in ("ffn_w_gate", "ffn_w_up", "ffn_w_down"):
            k, c = WSPECS[n]
            self.din(n, [DEPTH, 2, k, c])
        self.din("attn_norm", [DEPTH * 16, 128])
        self.din("mem_norm", [DEPTH * 16, 128])
        self.din("mem_w_kv", [DEPTH, D, 1024])
        self.din("mem_qk_g", [DEPTH * 2, 128])
        self.din("w_out", [DEPTH, 2560, D])
        self.din("a_w_in", [1, D, 6672])
        self.din("a_b_f", [16, 1])
        self.din("a_qk_g", [2, 128])
        self.din("b_w_in", [1, D, 1600])
        self.din("b_q_norm", [4, 128])
        self.din("b_w_uq", [1, 512, 3072])
        self.din("b_kv_norm", [4, 128])
        self.din("b_w_ukv", [1, 512, 4096])
        self.din("b_nope_g", [2, 128])
        self.din("b_rope_g", [2, 64])
        self.din("c_w_in", [1, D, 18944])
        self.din("c_qk_g", [6, 128])
        self.din("d_w_in", [1, D, 4688])
        self.din("d_qk_g", [2, 128])
        self.din("c_ident", [128, 128])
        self.din("c_identb", [128, 128], BF16)
        self.din("c_antib", [128, 128], BF16)
        self.din("c_onesb", [128, 128], BF16)
        self.din("c_cmask", [128, 4, 512], BF16)
        self.din("c_sel16", [16, 16, 128], BF16)
        self.din("c_sel80", [80, 16, 128], BF16)
        self.din("c_ohd", [32, 2688])
        self.din("c_ohg", [3, 33, 384])
        self.din("c_rope", [64, 2])
        self.out = nc.dram_tensor("out", [S, D], F32, kind="ExternalOutput").ap()
        self.XT = self.dscr("XT", [16, 128, S], F32)
        self.PT = self.dscr("PTs", [112, 128, S], BF16)
        self.VTM = self.dscr("VTM", [S, 6144], BF16)
        self.AT = self.dscr("ATs", [20, 128, S], BF16)
        self.FG = self.dscr("FGs", [16, S], F32)
        self.ROPE = self.dscr("ROPEs", [2, 64, S], F32)
        self.BEXT_h = self.nc.dram_tensor("BEXTs", [16, 2688], F32, kind="Internal")
        self.BEXT = self.BEXT_h.ap()
        self.TZ = self.dscr("TZs", [16, 128, 2560], BF16)
        self.VEXT_h = self.nc.dram_tensor("VEXTs", [48, 512], F32, kind="Internal")
        self.VEXT = self.VEXT_h.ap()
        self.WB = {}

    def dbgout(self, name, src_ap, shape, dt):
        o = self.nc.dram_tensor("dbg_" + name, list(shape), dt, kind="ExternalOutput").ap()
        self.p.barrier()
        self.p.dma("sp", o, src_ap)
        self.p.barrier()

    def load_consts(self):
        p = self.p
        nc = self.nc
        self.cst = Phase(nc)
        c = self.cst
        self.ident = c.tile([128, 128], F32, "ident")
        self.identb = c.tile([128, 128], BF16, "identb")
        self.antib = c.tile([128, 128], BF16, "antib")
        self.onesb = c.tile([128, 128], BF16, "onesb")
        for t, n in ((self.ident, "c_ident"), (self.identb, "c_identb"), (self.antib, "c_antib"),
                     (self.onesb, "c_onesb")):
            p.dma("sp", t.ap, self.inp[n], w=[t])
        self.ps = [T(nc.alloc_psum_tensor(f"ps{i}", [128, 512], F32).ap(), psum=True) for i in range(8)]
        self.g_ffn = self.colvecs("ffn_norm", 128)
        self.g_attn = self.colvecs("attn_norm", 64)
        self.g_mem = self.colvecs("mem_norm", 64)
        self.g_memqk = self.colvecs("mem_qk_g", 8)

    def colvecs(self, name, n):
        p = self.p
        c = self.cst
        st = c.tile([128, 128], F32, "cvst")
        out = c.tile([128, n], F32, "cv")
        p.dma("sp", st.ap[0:n, :], self.inp[name], w=[st])
        ps = self.ps[7]
        p.op("pe", lambda e: e.transpose(ps.ap[:, 0:n], st.ap[0:n, :], self.ident.ap[0:n, 0:n]),
             w=[ps], r=[st, self.ident])
        p.copy(out, ps[:, 0:n])
        return out

    def convert(self, key, src, Kdim, C):
        p = self.p
        KC = Kdim // 128
        dst = self.dscr("WB_" + key, [128, KC, C], BF16)
        self.WB[key] = dst
        CW = 2048
        jobs = [(kc, c0, min(CW, C - c0)) for kc in range(KC) for c0 in range(0, C, CW)]
        ph = self.cvph
        deferred = []
        engs = ("pool", "act", "dve")
        for i, (kc, c0, cw) in enumerate(jobs):
            st = self.cv_st[i % 3]
            bf = self.cv_bf[i % 3]
            p.dma("sp", st.ap[:, 0:cw], src[kc * 128:(kc + 1) * 128, c0:c0 + cw], w=[st])
            E = engs[i % 3]
            p.copy(bf[:, 0:cw], st[:, 0:cw], E=E)
            deferred.append((dst[:, kc, c0:c0 + cw], bf, cw))
            if len(deferred) > 1:
                d_, b_, w_ = deferred.pop(0)
                p.dma("sp", d_, b_.ap[:, 0:w_], r=[b_])
        while deferred:
            d_, b_, w_ = deferred.pop(0)
            p.dma("sp", d_, b_.ap[:, 0:w_], r=[b_])
        return dst

    def layer_weights(self, L):
        mixw = {0: [("a_w_in", 0)], 1: [("b_w_in", 0), ("b_w_uq", 0), ("b_w_ukv", 0)],
                2: [("c_w_in", 0)], 3: [("d_w_in", 0)]}
        out = []
        for s_ in (0, 1):
            for n in ("ffn_w_gate", "ffn_w_up", "ffn_w_down"):
                k, c = WSPECS[n]
                out.append((f"{n}_{L}_{s_}", self.inp[n][L, s_], k, c))
        if self.stop == "ffn0":
            return out
        out.append((f"mem_w_kv_{L}", self.inp["mem_w_kv"][L], D, 1024))
        out.append((f"w_out_{L}", self.inp["w_out"][L], 2560, D))
        for n, j in mixw[L % 4]:
            k, c = WSPECS[n]
            out.append((f"{n}_{L}", self.inp[n][j], k, c))
        return out

    def convert_all(self):
        nc = self.nc
        self.cvph = Phase(nc)
        self.cv_st = [self.cvph.tile([128, 2048], F32, "cvs") for _ in range(3)]
        self.cv_bf = [self.cvph.tile([128, 2048], BF16, "cvb") for _ in range(3)]
        for key, src, k, c in self.layer_weights(self.layers[0]):
            self.convert(key, src, k, c)
        self.p.barrier()
        self.cvph.close()

    def conv_jobs(self, L):
        jobs = []
        CW = 1024
        for key, src, Kdim, C in self.layer_weights(L):
            KC = Kdim // 128
            dst = self.dscr("WB_" + key, [128, KC, C], BF16)
            self.WB[key] = dst
            for kc in range(KC):
                for c0 in range(0, C, CW):
                    cw = min(CW, C - c0)
                    jobs.append((src[kc * 128:(kc + 1) * 128, c0:c0 + cw], dst[:, kc, c0:c0 + cw], cw))
        return jobs

    def pump_open(self, ph):
        self.pst = [ph.tile([128, 1024], F32, "pst") for _ in range(3)]
        self.pbf = [ph.tile([128, 1024], BF16, "pbf") for _ in range(3)]
        self.pi = 0
        assert not getattr(self, "pdef", [])
        self.pdef = []

    def pump(self, n):
        p = self.p
        for _ in range(n):
            if not self.pending:
                return
            src, dst, cw = self.pending.pop(0)
            st = self.pst[self.pi % 3]
            bf = self.pbf[self.pi % 3]
            self.pi += 1
            p.dma("sp", st.ap[:, 0:cw], src, w=[st])
            p.copy(bf[:, 0:cw], st[:, 0:cw], E="pool")
            self.pdef.append((dst, bf, cw))
            if len(self.pdef) > 2:
                d_, b_, w_ = self.pdef.pop(0)
                p.dma("sp", d_, b_.ap[:, 0:w_], r=[b_])

    def pump_flush(self):
        while self.pdef:
            d_, b_, w_ = self.pdef.pop(0)
            self.p.dma("sp", d_, b_.ap[:, 0:w_], r=[b_])

    def x_to_xt(self):
        p = self.p
        ph = Phase(self.nc)
        xt = [ph.tile([128, D], F32, "xin") for _ in range(8)]
        ob = [ph.tile([128, 512], F32, "xo") for _ in range(3)]
        k = 0
        for tb in range(NSUB):
            tiles = []
            for j in range(4):
                t = xt[(tb % 2) * 4 + j]
                r0 = tb * 512 + j * 128
                p.dma("sp", t.ap, self.inp["x"][r0:r0 + 128, :], w=[t])
                tiles.append(t)
            for dc in range(16):
                ps = self.ps[dc % 4]
                for j in range(4):
                    p.op("pe", lambda e, j=j, ps=ps, dc=dc: e.transpose(
                        ps.ap[:, j * 128:(j + 1) * 128], tiles[j].ap[:, dc * 128:(dc + 1) * 128], self.ident.ap),
                        w=[ps], r=[tiles[j], self.ident], inc=(j == 3))
                o = ob[k % 3]
                k += 1
                p.copy(o, ps, E=("dve" if dc % 2 == 0 else "act"))
                p.dma("sp", self.XT[dc][:, tb * 512:(tb + 1) * 512], o.ap, r=[o])
        p.barrier()
        ph.close()

    def xt_to_out(self):
        p = self.p
        ph = Phase(self.nc)
        xin = [ph.tile([128, 512], F32, "xi") for _ in range(6)]
        ot = [ph.tile([128, D], F32, "xo") for _ in range(8)]
        k = 0
        for tb in range(NSUB):
            outs = [ot[(tb % 2) * 4 + j] for j in range(4)]
            for dc in range(16):
                xi = xin[k % 6]
                k += 1
                p.dma("sp", xi.ap, self.XT[dc][:, tb * 512:(tb + 1) * 512], w=[xi])
                ps = self.ps[dc % 4]
                for j in range(4):
                    p.op("pe", lambda e, j=j, ps=ps, xi=xi: e.transpose(
                        ps.ap[:, j * 128:(j + 1) * 128], xi.ap[:, j * 128:(j + 1) * 128], self.ident.ap),
                        w=[ps], r=[xi, self.ident], inc=(j == 3))
                for j in range(4):
                    p.copy(outs[j][:, dc * 128:(dc + 1) * 128], ps[:, j * 128:(j + 1) * 128],
                           E=("dve" if (j % 2 == 0 or os.environ.get("XO_DVE")) else "act"))
            for j in range(4):
                r0 = tb * 512 + j * 128
                p.dma("sp", self.out[r0:r0 + 128, :], outs[j].ap, r=[outs[j]])
        p.barrier()
        ph.close()

    def rstd_from_ssq(self, ps_ssq, n_feat, lnv, rstd):
        p = self.p
        p.act(lnv, ps_ssq, AF.Ln, bias=self.eps_col[:, 0:1], scale=1.0 / n_feat)
        p.act(rstd, lnv, AF.Exp, scale=-0.5)

    def ffn(self, L, s):
        p = self.p
        nc = self.nc
        ph = Phase(nc)
        Wg = self.WB[f"ffn_w_gate_{L}_{s}"]
        Wu = self.WB[f"ffn_w_up_{L}_{s}"]
        Wd = self.WB[f"ffn_w_down_{L}_{s}"]
        gcol = self.g_ffn
        gbase = (L * 2 + s) * 16
        xnT = [[ph.tile([128, 512], BF16, "xn") for _ in range(16)] for _ in range(2)]
        hT = [[ph.tile([128, 512], BF16, "h") for _ in range(44)] for _ in range(2)]
        xs = [ph.tile([128, 512], F32, "xs") for _ in range(4)]
        sq = [ph.tile([128, 512], BF16, "sq") for _ in range(2)]
        lnv = ph.tile([128, 512], F32, "lnv")
        rstd = ph.tile([128, 512], F32, "rstd")
        wg = [ph.tile([128, 16, 128], BF16, "wg") for _ in range(2)]
        wu = [ph.tile([128, 16, 128], BF16, "wu") for _ in range(2)]
        wd = [ph.tile([128, 44, 128], BF16, "wd") for _ in range(2)]
        sg = [ph.tile([128, 512], F32, "sg") for _ in range(2)]
        xo = [ph.tile([128, 512], F32, "xo") for _ in range(2)]
        ps = self.ps
        kx = 0
        for tb in range(S // 1024):
            for sub in range(2):
                tok = slice((2 * tb + sub) * 512, (2 * tb + sub + 1) * 512)
                for kc in range(16):
                    xt = xs[kx % 4]
                    kx += 1
                    p.dma("sp", xt.ap, self.XT[kc][:, tok], w=[xt])
                    q = sq[kc % 2]
                    p.act(q, xt, AF.Square)
                    p.mm(ps[0], [(self.onesb, q)], first=(kc == 0), last=(kc == 15))
                self.rstd_from_ssq(ps[0], D, lnv, rstd)
                for kc in range(16):
                    xt = xs[kx % 4]
                    kx += 1
                    p.dma("sp", xt.ap, self.XT[kc][:, tok], w=[xt])
                    p.stt(xnT[sub][kc], xt, gcol[:, gbase + kc:gbase + kc + 1], rstd, ALU.mult, ALU.mult)
            for fc in range(44):
                a = wg[fc % 2]
                b = wu[fc % 2]
                p.dma("sp", a.ap, Wg[:, :, fc * 128:(fc + 1) * 128], w=[a])
                p.dma("sp", b.ap, Wu[:, :, fc * 128:(fc + 1) * 128], w=[b])
                for sub in range(2):
                    pg = ps[1 + sub * 2]
                    pu = ps[2 + sub * 2]
                    p.mm(pg, [(a[:, kc, :], xnT[sub][kc]) for kc in range(16)])
                    p.mm(pu, [(b[:, kc, :], xnT[sub][kc]) for kc in range(16)])
                    g_ = sg[sub]
                    p.act(g_, pg, AF.Silu)
                    p.tt(hT[sub][fc], g_, pu, ALU.mult)
            for dc in range(16):
                w_ = wd[dc % 2]
                p.dma("sp", w_.ap, Wd[:, :, dc * 128:(dc + 1) * 128], w=[w_])
                for sub in range(2):
                    tok = slice((2 * tb + sub) * 512, (2 * tb + sub + 1) * 512)
                    py = ps[5 + sub]
                    p.mm(py, [(w_[:, fc, :], hT[sub][fc]) for fc in range(44)])
                    xt = xs[kx % 4]
                    kx += 1
                    p.dma("sp", xt.ap, self.XT[dc][:, tok], w=[xt])
                    o = xo[sub]
                    p.stt(o, py, 0.5, xt, ALU.mult, ALU.add)
                    p.dma("sp", self.XT[dc][:, tok], o.ap, r=[o])
        p.barrier()
        ph.close()

    def headnorm(self, ph_tiles, ps_in, ps_ssq, gcol, out_bf, nfeat=128, npart=128):
        p = self.p
        sq, lnv, rstd = ph_tiles
        p.act(sq[0:npart, :], ps_in[0:npart, :], AF.Square)
        p.mm(ps_ssq[0:npart, :], [(self.onesb[0:npart, 0:npart], sq[0:npart, :])])
        p.act(lnv[0:npart, :], ps_ssq[0:npart, :], AF.Ln, bias=self.eps_col[0:npart, 0:1], scale=1.0 / nfeat)
        p.act(rstd[0:npart, :], lnv[0:npart, :], AF.Exp, scale=-0.5)
        p.stt(out_bf, ps_in[0:npart, :], gcol, rstd[0:npart, :], ALU.mult, ALU.mult)

    def mem_setup(self):
        p = self.p
        ph = Phase(self.nc)
        mt = [ph.tile([128, D], F32, "memin") for _ in range(2)]
        sq = [ph.tile([128, 256], BF16, "msq") for _ in range(2)]
        lnv = ph.tile([128, 256], F32, "mln")
        for j in range(2):
            p.dma("sp", mt[j].ap, self.inp["mem"][j * 128:(j + 1) * 128, :], w=[mt[j]])
        for kc in range(16):
            ps = self.ps[kc % 2]
            for j in range(2):
                p.op("pe", lambda e, j=j, ps=ps, kc=kc: e.transpose(
                    ps.ap[:, j * 128:(j + 1) * 128], mt[j].ap[:, kc * 128:(kc + 1) * 128], self.ident.ap),
                    w=[ps], r=[mt[j], self.ident], inc=(j == 1))
            p.copy(self.memT[kc], ps[:, 0:256])
            q = sq[kc % 2]
            p.act(q, self.memT[kc], AF.Square)
            p.mm(self.ps[2][:, 0:256], [(self.onesb, q)], first=(kc == 0), last=(kc == 15))
        p.act(lnv, self.ps[2][:, 0:256], AF.Ln, bias=self.eps_col[:, 0:1], scale=1.0 / D)
        p.act(self.mem_rstd, lnv, AF.Exp, scale=-0.5)
        p.barrier()
        ph.close()

    def mem_kv(self, L, ph):
        p = self.p
        W = self.WB[f"mem_w_kv_{L}"]
        memn = [ph.tile([128, 256], BF16, "memn") for _ in range(16)]
        for kc in range(16):
            p.stt(memn[kc], self.memT[kc], self.g_mem[:, L * 16 + kc:L * 16 + kc + 1], self.mem_rstd,
                  ALU.mult, ALU.mult)
        kmT = [ph.tile([128, 256], BF16, "kmT") for _ in range(4)]
        vm = [ph.tile([128, 512], BF16, "vm") for _ in range(2)]
        wk = [ph.tile([128, 16, 128], BF16, "wmk") for _ in range(2)]
        wv = ph.tile([128, 16, 512], BF16, "wmv")
        sq = ph.tile([128, 512], BF16, "hsq")
        lnv = ph.tile([128, 512], F32, "hln")
        rstd = ph.tile([128, 512], F32, "hrs")
        for j in range(4):
            w_ = wk[j % 2]
            p.dma("sp", w_.ap, W[:, :, j * 128:(j + 1) * 128], w=[w_])
            ps = self.ps[j % 2]
            p.mm(ps[:, 0:256], [(w_[:, kc, :], memn[kc]) for kc in range(16)])
            self.headnorm((sq[:, 0:256], lnv[:, 0:256], rstd[:, 0:256]), ps[:, 0:256], self.ps[2][:, 0:256],
                          self.g_memqk[:, L * 2 + 1:L * 2 + 2], kmT[j])
        p.dma("sp", wv.ap, W[:, :, 512:1024], w=[wv])
        for t in range(2):
            ps = self.ps[3 + t]
            p.mm(ps, [(memn[kc][:, t * 128:(t + 1) * 128], wv[:, kc, :]) for kc in range(16)])
            p.copy(vm[t], ps)
        return kmT, vm

    def attn_chunk(self, ktiles, vtiles, qpairs_fn, extra_fn, bias_fn, pt_tiles, ps_s, ps_o, ps_d, rec, out_bf,
                   dv=128, nq=512):
        p = self.p
        n = ktiles

        def scores(j):
            p.mm(ps_s[j % 2][:, 0:nq], list(qpairs_fn(j)) + list(extra_fn(j)))

        scores(0)
        for j in range(n):
            pt = pt_tiles[j % 2]
            p.act(pt[:, 0:nq], ps_s[j % 2][:, 0:nq], AF.Exp, bias=bias_fn(j))
            if j + 1 < n:
                scores(j + 1)
            p.mm(ps_o[0:dv, 0:nq], [(vtiles(j), pt[:, 0:nq])], first=(j == 0), last=(j == n - 1))
            p.mm(ps_d[0:dv, 0:nq], [(self.onesb[:, 0:dv], pt[:, 0:nq])], first=(j == 0), last=(j == n - 1))
        p.op("dve", lambda e: e.reciprocal(rec.ap[0:dv, 0:nq], ps_d.ap[0:dv, 0:nq]), w=[rec], r=[ps_d])
        p.tt(out_bf, ps_o[0:dv, 0:nq], rec[0:dv, 0:nq], ALU.mult)

    def attention(self, L):
        m = L % 4
        if m == 1:
            self.rope_tables()
        self.in_proj(L)
        if m == 0:
            self.fox_core(L)
        elif m == 1:
            self.mla_core(L)
        elif m == 2:
            self.dil_core(L)
        elif m == 3:
            self.dsa_core(L)
        self.out_proj(L)

    def in_proj(self, L):
        p = self.p
        nc = self.nc
        m = L % 4
        ph = Phase(nc)
        wkey = {0: "a_w_in", 1: "b_w_in", 2: "c_w_in", 3: "d_w_in"}[m]
        W = self.WB[f"{wkey}_{L}"]
        ncols = WSPECS[wkey][1]
        memq0 = ncols - 512
        kmT, vm = self.mem_kv(L, ph)
        hT = [ph.tile([128, 512], BF16, "hT") for _ in range(16)]
        xs = [ph.tile([128, 512], F32, "xs") for _ in range(3)]
        sqx = [ph.tile([128, 512], BF16, "sqx") for _ in range(2)]
        lnv = ph.tile([128, 512], F32, "lnv")
        rstd = ph.tile([128, 512], F32, "rstd")
        hsq = ph.tile([128, 512], BF16, "hsq2")
        hsq2 = [hsq, ph.tile([128, 512], BF16, "hsq3")]
        hln = ph.tile([128, 512], F32, "hln2")
        hrs = ph.tile([128, 512], F32, "hrs2")
        wt = [ph.tile([128, 16, 128], BF16, "wt") for _ in range(2)]
        wvt = [ph.tile([128, 16, 512], BF16, "wvt") for _ in range(2)] if m in (0, 2, 3) else None
        ob = [ph.tile([128, 512], BF16, "ob") for _ in range(3)]
        qm = [ph.tile([128, 512], BF16, "qm") for _ in range(2)]
        ptt = [ph.tile([128, 512], BF16, "ptm") for _ in range(2)]
        rec = ph.tile([128, 512], F32, "rec")
        jobs = []
        vjobs = []
        if m == 0:
            self.gq_a = self.colv_small("a_qk_g", 2, ph)
            gq = ph.tile([128, 1], F32, "gqs")
            p.ts(gq, self.gq_a[:, 0:1], 128 ** -0.5, ALU.mult)
            for h in range(16):
                jobs.append((h * 128, 128, gq[:, 0:1], h))
            for h in range(16):
                jobs.append((2048 + h * 128, 128, self.gq_a[:, 1:2], 16 + h))
            for g in range(4):
                vjobs.append((4096 + g * 512, g * 512))
        if m == 2:
            self.gq_c = self.colv_small("c_qk_g", 6, ph)
            gqc = ph.tile([128, 3], F32, "gqc")
            for g in range(3):
                p.ts(gqc[:, g:g + 1], self.gq_c[:, 2 * g:2 * g + 1], 128 ** -0.5, ALU.mult)
            for g in range(3):
                for h in range(16):
                    jobs.append((g * 6144 + h * 128, 128, gqc[:, g:g + 1], g * 32 + h))
                    jobs.append((g * 6144 + 2048 + h * 128, 128, self.gq_c[:, 2 * g + 1:2 * g + 2], g * 32 + 16 + h))
                for v4 in range(4):
                    vjobs.append((g * 6144 + 4096 + v4 * 512, g * 2048 + v4 * 512))
        if m == 3:
            self.gq_d = self.colv_small("d_qk_g", 2, ph)
            gqd = ph.tile([128, 1], F32, "gqd")
            p.ts(gqd, self.gq_d[:, 0:1], 128 ** -0.5, ALU.mult)
            for h in range(16):
                jobs.append((h * 128, 128, gqd[:, 0:1], h))
            for g in range(4):
                jobs.append((2048 + g * 128, 128, self.gq_d[:, 1:2], 16 + g))
            vjobs.append((2560, 0))
            for h in range(16):
                jobs.append((3072 + h * 64, 64, 64 ** -0.5, 20 + h))
            jobs.append((4096, 64, 1.0, 36))
            jobs.append((4160, 16, 16 ** -0.5, 37))
        mla = None
        if m == 1:
            mla = self.mla_setup(L, ph)
            mla["wt"] = wt
        gmq = ph.tile([128, 1], F32, "gmq")
        p.ts(gmq, self.g_memqk[:, L * 2:L * 2 + 1], 128 ** -0.5, ALU.mult)
        groups = []
        jgroup = {}
        for ji, job in enumerate(jobs):
            c0, cw = job[0], job[1]
            if groups and groups[-1][0] + groups[-1][1] == c0 and groups[-1][1] + cw <= 512:
                groups[-1][1] += cw
                groups[-1][2].append(ji)
            else:
                groups.append([c0, cw, [ji]])
            jgroup[ji] = (len(groups) - 1, c0 - groups[-1][0])
        wgt = [ph.tile([128, 16, 512], BF16, "wgt") for _ in range(2)] if jobs else None
        kx = 0
        ko = 0
        for tb in range(NSUB):
            tok = slice(tb * 512, (tb + 1) * 512)
            for kc in range(16):
                xt = xs[kx % 3]
                kx += 1
                p.dma("sp", xt.ap, self.XT[kc][:, tok], w=[xt])
                q = sqx[kc % 2]
                p.act(q, xt, AF.Square)
                p.mm(self.ps[0], [(self.onesb, q)], first=(kc == 0), last=(kc == 15))
            self.rstd_from_ssq(self.ps[0], D, lnv, rstd)
            for kc in range(16):
                xt = xs[kx % 3]
                kx += 1
                p.dma("sp", xt.ap, self.XT[kc][:, tok], w=[xt])
                p.stt(hT[kc], xt, self.g_attn[:, L * 16 + kc:L * 16 + kc + 1], rstd, ALU.mult, ALU.mult)
            pend = None

            def stage_b(st):
                nonlocal ko
                ji, cw, gcol, chunk, ps = st
                o = ob[ko % 3]
                ko += 1
                if isinstance(gcol, float):
                    p.ts(o[0:cw, :], ps[0:cw, :], gcol, ALU.mult)
                else:
                    q = hsq2[ji % 2]
                    p.mm(self.ps[3][0:cw, :], [(self.onesb[0:cw, 0:cw], q[0:cw, :])])
                    p.act(hln[0:cw, :], self.ps[3][0:cw, :], AF.Ln, bias=self.eps_col[0:cw, 0:1], scale=1.0 / cw)
                    p.act(hrs[0:cw, :], hln[0:cw, :], AF.Exp, scale=-0.5)
                    p.stt(o[0:cw, :], ps[0:cw, :], gcol, hrs[0:cw, :], ALU.mult, ALU.mult)
                p.dma("sp", self.PT[chunk][0:cw, tok], o.ap[0:cw, :], r=[o])

            for ji, (c0, cw, gcol, chunk) in enumerate(jobs):
                gi, off = jgroup[ji]
                if groups[gi][2][0] == ji:
                    if gi == 0:
                        p.dma("sp", wgt[0].ap[:, :, 0:groups[0][1]], W[:, :, groups[0][0]:groups[0][0] + groups[0][1]],
                              w=[wgt[0]])
                    if gi + 1 < len(groups):
                        g1 = groups[gi + 1]
                        t1_ = wgt[(gi + 1) % 2]
                        p.dma("sp", t1_.ap[:, :, 0:g1[1]], W[:, :, g1[0]:g1[0] + g1[1]], w=[t1_])
                w_ = wgt[gi % 2]
                ps = self.ps[(1, 2, 4, 5)[ji % 4]]
                p.mm(ps[0:cw, :], [(w_[:, kc, off:off + cw], hT[kc]) for kc in range(16)])
                if not isinstance(gcol, float):
                    p.act(hsq2[ji % 2][0:cw, :], ps[0:cw, :], AF.Square)
                if pend is not None:
                    stage_b(pend)
                pend = (ji, cw, gcol, chunk, ps)
            if pend is not None:
                stage_b(pend)
            if m == 1:
                def ob_next():
                    nonlocal ko
                    o_ = ob[ko % 3]
                    ko += 1
                    return o_
                self.mla_block(L, mla, tb, hT, W, (hsq, hln, hrs), ob_next)
            if m == 0:
                w_ = wt[0]
                p.dma("sp", w_.ap[:, :, 0:16], W[:, :, 6144:6160], w=[w_])
                ps = self.ps[1]
                p.mm(ps[0:16, :], [(w_[:, kc, 0:16], hT[kc]) for kc in range(16)])
                o32 = xs[kx % 3]
                kx += 1
                p.copy(o32[0:16, :], ps[0:16, :])
                p.dma("sp", self.FG[:, tok], o32.ap[0:16, :], r=[o32])
            for vi, (c0, v0) in enumerate(vjobs):
                w_ = wvt[vi % 2]
                p.dma("sp", w_.ap, W[:, :, c0:c0 + 512], w=[w_])
                for t in range(4):
                    ps = self.ps[4 + t % 2]
                    p.mm(ps, [(hT[kc][:, t * 128:(t + 1) * 128], w_[:, kc, :]) for kc in range(16)])
                    o = ob[ko % 3]
                    ko += 1
                    p.copy(o, ps, E=("act" if t % 2 else "dve"))
                    r0 = tb * 512 + t * 128
                    p.dma("sp", self.VTM[r0:r0 + 128, v0:v0 + 512], o.ap, r=[o])
            for j in range(4):
                w_ = wt[j % 2]
                c0 = memq0 + j * 128
                p.dma("sp", w_.ap, W[:, :, c0:c0 + 128], w=[w_])
                ps = self.ps[1 + j % 2]
                p.mm(ps, [(w_[:, kc, :], hT[kc]) for kc in range(16)])
                q_ = qm[j % 2]
                self.headnorm((hsq, hln, hrs), ps, self.ps[3], gmq[:, 0:1], q_)
                o = ob[ko % 3]
                ko += 1
                self.attn_chunk(
                    2, lambda t, j=j: vm[t][:, j * 128:(j + 1) * 128],
                    lambda t, j=j, q_=q_: [(kmT[j][:, t * 128:(t + 1) * 128], q_)],
                    lambda t: [], lambda t: None, ptt, (self.ps[4], self.ps[5]), self.ps[6], self.ps[7], rec, o)
                p.dma("sp", self.AT[16 + j][:, tok], o.ap, r=[o])
        p.barrier()
        ph.close()

    def colv_small(self, name, n, ph):
        p = self.p
        w = self.inp[name].shape[1]
        st = ph.tile([128, 128], F32, "cvs")
        out = ph.tile([128, n], F32, "cvo")
        p.dma("sp", st.ap[0:n, 0:w], self.inp[name], w=[st])
        ps = self.ps[7]
        p.op("pe", lambda e: e.transpose(ps.ap[0:w, 0:n], st.ap[0:n, 0:w], self.ident.ap[0:n, 0:n]),
             w=[ps], r=[st, self.ident])
        p.copy(out[0:w, :], ps[0:w, 0:n])
        return out

    def fox_core(self, L):
        p = self.p
        nc = self.nc
        ph = Phase(nc)
        self.pump_open(ph)
        npump = len(self.pending) * 9 // (10 * 128) + 1
        fg = ph.tile([80, S], F32, "fg")
        negbf = ph.tile([80, 1], F32, "negbf")
        p.op("pool", lambda e: e.memset(fg.ap, 0.0), w=[fg])
        p.op("pool", lambda e: e.memset(negbf.ap, 0.0), w=[negbf])
        for i in range(3):
            p.dma("sp", fg.ap[32 * i:32 * i + 16, :], self.FG, w=[fg])
            p.dma("sp", negbf.ap[32 * i:32 * i + 16, :], self.inp["a_b_f"], w=[negbf])
        p.ts(negbf, negbf, -1.0, ALU.mult)
        ones16 = ph.tile([80, S], F32, "ones16")
        p.op("pool", lambda e: e.memset(ones16.ap, 1.0), w=[ones16])
        lf = ph.tile([80, S], F32, "lf")
        ncum = ph.tile([80, S], F32, "ncum")
        p.act(lf, fg, AF.Exp, bias=negbf[:, 0:1], scale=-1.0)
        p.act(lf, lf, AF.Ln, bias=1.0)
        p.op("dve", lambda e: e.tensor_tensor_scan(ncum.ap, ones16.ap, lf.ap, 0.0, ALU.mult, ALU.add),
             w=[ncum], r=[ones16, lf])
        c_hi = ph.tile([80, S], BF16, "chi")
        c_mid = ph.tile([80, S], BF16, "cmid")
        c_lo = ph.tile([80, S], BF16, "clo")
        r1 = lf
        r2 = ones16
        p.ts(c_hi, ncum, -1.0, ALU.mult)
        p.stt(r1, ncum, -1.0, c_hi, ALU.mult, ALU.subtract)
        p.copy(c_mid, r1)
        p.tt(r2, r1, c_mid, ALU.subtract)
        p.copy(c_lo, r2)
        c_all = fg_b = ph.tile([80, S], BF16, "call")
        p.op("pool", lambda e: e.memset(c_all.ap, 0.0), w=[c_all])
        p.copy(c_all[0:16, :], c_hi[0:16, :])
        p.copy(c_all[32:48, :], c_mid[32:48, :])
        p.copy(c_all[64:80, :], c_lo[64:80, :])
        nct = ph.tile([128, 32, 16], F32, "nct")
        ps = self.ps[7]
        for j in range(32):
            p.op("pe", lambda e, j=j: e.transpose(ps.ap[:, j * 16:(j + 1) * 16], ncum.ap[0:16, j * 128:(j + 1) * 128],
                                                  self.ident.ap[0:16, 0:16]),
                 w=[ps], r=[ncum, self.ident], inc=(j == 31))
        p.copy(nct, ps.ap.rearrange("p (j h) -> p j h", h=16) if False else ps)
        sel = ph.tile([80, 16, 128], BF16, "sel80")
        p.dma("sp", sel.ap, self.inp["c_sel80"], w=[sel])
        cm = ph.tile([128, 4, 512], BF16, "cmask")
        p.dma("sp", cm.ap, self.inp["c_cmask"], w=[cm])
        qT = [ph.tile([128, S], BF16, "qT") for _ in range(2)]
        kT = [ph.tile([128, S], BF16, "kT") for _ in range(2)]
        vt = [ph.tile([128, 32, 128], BF16, "vt") for _ in range(2)]
        ptt = [ph.tile([128, 512], BF16, "pt") for _ in range(2)]
        rec = ph.tile([128, 512], F32, "rec")
        ob = [ph.tile([128, 512], BF16, "ob") for _ in range(2)]
        nctv = nct.ap
        ko = 0
        for h in range(16):
            q_ = qT[h % 2]
            k_ = kT[h % 2]
            v_ = vt[h % 2]
            p.dma("sp", q_.ap, self.PT[h], w=[q_])
            p.dma("sp", k_.ap, self.PT[16 + h], w=[k_])
            p.dma("sp", v_.ap, self.VTM[:, h * 128:(h + 1) * 128].rearrange("(j p) d -> p j d", p=128), w=[v_])
            for c in range(NSUB):
                qs = slice(c * 512, (c + 1) * 512)

                def qpairs(j, k_=k_, q_=q_, qs=qs):
                    return [(k_[:, j * 128:(j + 1) * 128], q_[:, qs])]

                def extra(j, c=c, h=h, qs=qs):
                    e = [(sel[:, h, :], c_all[:, qs])]
                    if j >= 4 * c:
                        e.append((self.identb, cm[:, j - 4 * c, :]))
                    return e

                def bias(j, h=h):
                    return V(nct, nctv[:, j, h:h + 1])

                o = ob[ko % 2]
                ko += 1
                self.attn_chunk(4 * c + 4, lambda j, v_=v_: v_[:, j, :], qpairs, extra, bias, ptt,
                                (self.ps[0], self.ps[1]), self.ps[2], self.ps[3], rec, o)
                p.dma("sp", self.AT[h][:, qs], o.ap, r=[o])
                self.pump(npump)
        self.pump_flush()
        p.barrier()
        ph.close()

    def dil_core(self, L):
        p = self.p
        nc = self.nc
        ph = Phase(nc)
        tab = ph.tile([33, 16], F32, "tab33")
        p.op("dve", lambda e: e.memset(tab.ap[32:33, :], NEG), w=[tab])
        p.dma("sp", tab.ap[0:32, :], self.inp["t5_table"], w=[tab])
        ohg = ph.tile([33, 3, 384], F32, "ohg")
        p.dma("sp", ohg.ap, self.inp["c_ohg"].rearrange("g v i -> v g i"), w=[ohg])
        vx = ph.tile([16, 384], F32, "vx")
        for g in range(3):
            ps = self.ps[g]
            p.mm(ps[0:16, 0:384], [(tab, ohg[:, g, :])])
            p.copy(vx, ps[0:16, 0:384])
            p.dma("sp", self.VEXT[g * 16:(g + 1) * 16, 0:384], vx.ap, r=[vx])
        p.barrier()
        bz = [[ph.tile([128, 256], BF16, "bz") for _ in range(16)] for _ in range(3)]
        hk = [ph.tile([128, 256], F32, "hk") for _ in range(2)]
        hkb = [ph.tile([128, 256], BF16, "hkb") for _ in range(2)]
        for g in range(3):
            for h in range(16):
                i = g * 16 + h
                a = hk[i % 2]
                b = hkb[i % 2]
                src = bass.AP(tensor=self.VEXT_h, offset=i * 512, ap=[[1, 128], [1, 256]])
                p.dma("sp", a.ap, src, w=[a])
                p.copy(b, a)
                ps = self.ps[i % 2]
                p.mm(ps[:, 0:256], [(self.antib, b)])
                p.copy(bz[g][h], ps[:, 0:256], E=("act" if i % 2 else "dve"))
        self.pump_open(ph)
        npump = len(self.pending) * 9 // (10 * 372) + 1
        num = ph.tile([128, S], F32, "num")
        den = ph.tile([128, S], F32, "den")
        qT = [ph.tile([128, S], BF16, "qT") for _ in range(2)]
        kT = [ph.tile([128, S], BF16, "kT") for _ in range(2)]
        vt = [ph.tile([128, 32, 128], BF16, "vt") for _ in range(2)]
        ptt = [ph.tile([128, 128], BF16, "pt") for _ in range(2)]
        ob = ph.tile([128, S], BF16, "ob")
        kk = 0
        kb = 0
        for h in range(16):
            for g, dil in enumerate((1, 4, 16)):
                q_ = qT[kk % 2]
                k_ = kT[kk % 2]
                v_ = vt[kk % 2]
                kk += 1
                nb = S // dil // 128
                p.dma("sp", q_.ap, self.PT[g * 32 + h], w=[q_])
                p.dma("sp", k_.ap, self.PT[g * 32 + 16 + h], w=[k_])
                c0 = g * 2048 + h * 128
                vsrc = self.VTM[:, c0:c0 + 128].rearrange("(jj p r) d -> p r jj d", p=128, r=dil)
                vdst = v_.ap.rearrange("p (r jj) d -> p r jj d", r=dil)
                for r in range(dil):
                    p.dma("sp", vdst[:, r], vsrc[:, r], w=[v_])
                units = []
                for r in range(dil):
                    for i in range(nb):
                        tl = [i] if i == 0 else [i - 1, i]
                        for ti, jj in enumerate(tl):
                            units.append((r, i, ti, jj, len(tl)))

                def u_scores(ui):
                    r, i, ti, jj, nt = units[ui]
                    q0 = r + dil * 128 * i
                    k0 = r + dil * 128 * jj
                    qsl = slice(q0, q0 + dil * 127 + 1, dil)
                    ksl = slice(k0, k0 + dil * 127 + 1, dil)
                    bsl = slice(0, 128) if jj == i else slice(128, 256)
                    p.mm(self.ps[ui % 2][:, 0:128], [(k_[:, ksl], q_[:, qsl]), (self.identb, bz[g][h][:, bsl])])

                u_scores(0)
                for ui, (r, i, ti, jj, nt) in enumerate(units):
                    pt = ptt[ui % 2]
                    p.act(pt, self.ps[ui % 2][:, 0:128], AF.Exp)
                    if ui + 1 < len(units):
                        u_scores(ui + 1)
                    if ti == 0:
                        kb += 1
                    ps_o = self.ps[2 + kb % 2]
                    ps_d = self.ps[4 + kb % 2]
                    p.mm(ps_o[:, 0:128], [(v_[:, r * nb + jj, :], pt)], first=(ti == 0), last=(ti == nt - 1))
                    p.mm(ps_d[:, 0:128], [(self.onesb, pt)], first=(ti == 0), last=(ti == nt - 1))
                    if ui % 8 == 7:
                        self.pump(npump)
                    if ti == nt - 1:
                        q0 = r + dil * 128 * i
                        qsl = slice(q0, q0 + dil * 127 + 1, dil)
                        if g == 0:
                            p.copy(num[:, qsl], ps_o[:, 0:128], E="act")
                            p.copy(den[:, qsl], ps_d[:, 0:128], E="dve")
                        else:
                            p.tt(num[:, qsl], ps_o[:, 0:128], num[:, qsl], ALU.add)
                            p.tt(den[:, qsl], ps_d[:, 0:128], den[:, qsl], ALU.add)
            p.op("dve", lambda e: e.reciprocal(den.ap, den.ap), w=[den], r=[den])
            p.tt(ob, num, den, ALU.mult)
            p.dma("sp", self.AT[h], ob.ap, r=[ob])
        self.pump_flush()
        p.barrier()
        ph.close()

    def rope_tables(self):
        p = self.p
        ph = Phase(self.nc)
        TWO_PI = 2.0 * math.pi
        cr = ph.tile([64, 2], F32, "crope")
        p.dma("sp", cr.ap, self.inp["c_rope"], w=[cr])
        posi = ph.tile([64, S], I32, "posi")
        pa = self.inp["positions"]
        p.dma("sp", posi.ap, bass.AP(tensor=pa.tensor, offset=0, ap=[[0, 64], [1, S]]), w=[posi])
        ang = ph.tile([64, S], F32, "ang")
        t1 = ph.tile([64, S], F32, "t1")
        ki = ph.tile([64, S], I32, "ki")
        p.copy(ang, posi)
        p.ts(ang, ang, cr[:, 0:1], ALU.mult)
        p.ts(t1, ang, 1.0 / TWO_PI, ALU.mult)
        p.copy(ki, t1)
        p.copy(t1, ki)
        r = ph.tile([64, S], F32, "r")
        p.stt(r, t1, -TWO_PI, ang, ALU.mult, ALU.add)
        p.ts(t1, r, math.pi, ALU.is_gt)
        p.stt(r, t1, -TWO_PI, r, ALU.mult, ALU.add)
        p.ts(t1, r, -1.0, ALU.mult, math.pi, ALU.is_gt)
        p.stt(r, t1, TWO_PI, r, ALU.mult, ALU.add)
        p.ts(r, r, 3.14159, ALU.min, -3.14159, ALU.max)
        p.act(t1, r, AF.Sin)
        p.ts(t1, t1, cr[:, 1:2], ALU.mult)
        p.dma("sp", self.ROPE[1], t1.ap, r=[t1])
        p.stt(ang, r, -1.0, r, ALU.mult, ALU.max)
        hp = ph.tile([64, 1], F32, "halfpi")
        p.op("dve", lambda e: e.memset(hp.ap, math.pi / 2), w=[hp])
        p.act(ang, ang, AF.Sin, bias=hp[:, 0:1], scale=-1.0)
        p.dma("sp", self.ROPE[0], ang.ap, r=[ang])
        p.barrier()
        ph.close()

    def mla_setup(self, L, ph):
        p = self.p
        st = {}
        st["gq"] = self.colv_small("b_q_norm", 4, ph)
        st["gkv"] = self.colv_small("b_kv_norm", 4, ph)
        gn = self.colv_small("b_nope_g", 2, ph)
        gr = self.colv_small("b_rope_g", 2, ph)
        sc = 192 ** -0.5
        gqn = ph.tile([128, 1], F32, "gqn")
        p.ts(gqn, gn[:, 0:1], sc, ALU.mult)
        st["gqn"] = gqn
        st["gkn"] = gn
        grs = ph.tile([64, 4], F32, "grs")
        p.ts(grs[:, 0:1], gr[0:64, 0:1], sc, ALU.mult)
        p.copy(grs[:, 2:3], gr[0:64, 1:2])
        stg = ph.tile([128, 128], F32, "grst")
        src = self.inp["b_rope_g"]
        p.dma("sp", stg.ap[0:2, 0:32], src[:, 32:64], w=[stg])
        p.dma("sp", stg.ap[0:2, 32:64], src[:, 0:32], w=[stg])
        ps = self.ps[7]
        p.op("pe", lambda e: e.transpose(ps.ap[0:64, 0:2], stg.ap[0:2, 0:64], self.ident.ap[0:2, 0:2]),
             w=[ps], r=[stg, self.ident])
        p.ts(grs[:, 1:2], ps[0:64, 0:1], sc, ALU.mult)
        p.copy(grs[:, 3:4], ps[0:64, 1:2])
        st["grs"] = grs
        st["cqf"] = [ph.tile([128, 512], F32, "cqf") for _ in range(4)]
        st["cqn"] = [ph.tile([128, 512], BF16, "cqn") for _ in range(4)]
        st["ckvn"] = [ph.tile([128, 512], BF16, "ckvn") for _ in range(4)]
        st["cs"] = [ph.tile([64, 512], F32, "cs") for _ in range(2)]
        st["ce"] = [ph.tile([64, 512], F32, "ce") for _ in range(2)]
        st["tt"] = [ph.tile([64, 512], F32, "ropet") for _ in range(2)]
        st["wq"] = [ph.tile([128, 4, 128], BF16, "wuq") for _ in range(2)]
        st["wq2"] = [ph.tile([128, 4, 64], BF16, "wuq2") for _ in range(2)]
        st["wv"] = [ph.tile([128, 4, 512], BF16, "wukvv") for _ in range(2)]
        return st

    def rope_apply(self, st, ps_a, ps_b, rstd64, g_a, g_b, out_bf):
        p = self.p
        ce = st["ce"]
        cs = st["cs"]
        tt = st["tt"]
        p.tt(ce[0], cs[0], rstd64, ALU.mult)
        p.tt(ce[1], cs[1], rstd64, ALU.mult)
        p.stt(tt[0], ps_a, g_a, ce[0], ALU.mult, ALU.mult)
        p.stt(tt[1], ps_b, g_b, ce[1], ALU.mult, ALU.mult)
        p.tt(out_bf, tt[0], tt[1], ALU.add)

    def mla_block(self, L, st, tb, hT, W, tmp, ob_next):
        p = self.p
        tok = slice(tb * 512, (tb + 1) * 512)
        hsq, hln, hrs = tmp
        Wuq = self.WB[f"b_w_uq_{L}"]
        Wukv = self.WB[f"b_w_ukv_{L}"]
        wt = st["wt"]
        p.dma("sp", st["cs"][0].ap, self.ROPE[0][:, tok], w=[st["cs"][0]])
        p.dma("sp", st["cs"][1].ap, self.ROPE[1][:, tok], w=[st["cs"][1]])
        for which, c_base, gcols, dst in (("q", 0, st["gq"], st["cqn"]), ("kv", 512, st["gkv"], st["ckvn"])):
            for kc in range(4):
                w_ = wt[kc % 2]
                p.dma("sp", w_.ap, W[:, :, c_base + kc * 128:c_base + (kc + 1) * 128], w=[w_])
                ps = self.ps[1 + kc % 2]
                p.mm(ps, [(w_[:, k2, :], hT[k2]) for k2 in range(16)])
                p.copy(st["cqf"][kc], ps)
                p.act(hsq, ps, AF.Square)
                p.mm(self.ps[3], [(self.onesb, hsq)], first=(kc == 0), last=(kc == 3))
            p.act(hln, self.ps[3], AF.Ln, bias=self.eps_col[:, 0:1], scale=1.0 / 512)
            p.act(hrs, hln, AF.Exp, scale=-0.5)
            for kc in range(4):
                p.stt(dst[kc], st["cqf"][kc], gcols[:, kc:kc + 1], hrs, ALU.mult, ALU.mult)
        w_ = wt[0]
        w2 = wt[1]
        p.dma("sp", w_.ap[:, :, 0:64], W[:, :, 1024:1088], w=[w_])
        p.dma("sp", w2.ap[:, :, 0:32], W[:, :, 1056:1088], w=[w2])
        p.dma("sp", w2.ap[:, :, 32:64], W[:, :, 1024:1056], w=[w2])
        pa = self.ps[1]
        pb = self.ps[2]
        p.mm(pa[0:64, :], [(w_[:, k2, 0:64], hT[k2]) for k2 in range(16)])
        p.mm(pb[0:64, :], [(w2[:, k2, 0:64], hT[k2]) for k2 in range(16)])
        self.rstd_part(pa, 64, hsq, hln, hrs)
        o = ob_next()
        self.rope_apply(st, pa[0:64, :], pb[0:64, :], hrs[0:64, :], st["grs"][:, 2:3], st["grs"][:, 3:4], o[0:64, :])
        p.dma("sp", self.PT[48][0:64, tok], o.ap[0:64, :], r=[o])
        for h in range(16):
            wq = st["wq"][h % 2]
            p.dma("sp", wq.ap, Wuq[:, :, h * 192:h * 192 + 128], w=[wq])
            ps = self.ps[1 + h % 2]
            p.mm(ps, [(wq[:, kc, :], st["cqn"][kc]) for kc in range(4)])
            o = ob_next()
            self.headnorm((hsq, hln, hrs), ps, self.ps[3], st["gqn"][:, 0:1], o)
            p.dma("sp", self.PT[h][:, tok], o.ap, r=[o])
            wa = st["wq2"][0]
            wb = st["wq2"][1]
            c0 = h * 192 + 128
            p.dma("sp", wa.ap, Wuq[:, :, c0:c0 + 64], w=[wa])
            p.dma("sp", wb.ap[:, :, 0:32], Wuq[:, :, c0 + 32:c0 + 64], w=[wb])
            p.dma("sp", wb.ap[:, :, 32:64], Wuq[:, :, c0:c0 + 32], w=[wb])
            pa = self.ps[4]
            pb = self.ps[5]
            p.mm(pa[0:64, :], [(wa[:, kc, :], st["cqn"][kc]) for kc in range(4)])
            p.mm(pb[0:64, :], [(wb[:, kc, :], st["cqn"][kc]) for kc in range(4)])
            self.rstd_part(pa, 64, hsq, hln, hrs)
            o = ob_next()
            self.rope_apply(st, pa[0:64, :], pb[0:64, :], hrs[0:64, :], st["grs"][:, 0:1], st["grs"][:, 1:2],
                            o[0:64, :])
            p.dma("sp", self.PT[16 + h][0:64, tok], o.ap[0:64, :], r=[o])
            wq = st["wq"][(h + 1) % 2]
            p.dma("sp", wq.ap, Wukv[:, :, h * 256:h * 256 + 128], w=[wq])
            ps = self.ps[1 + (h + 1) % 2]
            p.mm(ps, [(wq[:, kc, :], st["ckvn"][kc]) for kc in range(4)])
            o = ob_next()
            self.headnorm((hsq, hln, hrs), ps, self.ps[3], st["gkn"][:, 1:2], o)
            p.dma("sp", self.PT[32 + h][:, tok], o.ap, r=[o])
        wv5 = Wukv.rearrange("p kc (h two d) -> p kc h two d", two=2, d=128)
        for g4 in range(4):
            wv = st["wv"][g4 % 2]
            wdst = wv.ap.rearrange("p kc (h d) -> p kc h d", d=128)
            for kc in range(4):
                p.dma("sp", wdst[:, kc], wv5[:, kc, g4 * 4:(g4 + 1) * 4, 1, :], w=[wv])
            for t in range(4):
                ps = self.ps[4 + t % 2]
                p.mm(ps, [(st["ckvn"][kc][:, t * 128:(t + 1) * 128], wv[:, kc, :]) for kc in range(4)])
                o = ob_next()
                p.copy(o, ps, E=("act" if t % 2 else "dve"))
                r0 = tb * 512 + t * 128
                p.dma("sp", self.VTM[r0:r0 + 128, g4 * 512:(g4 + 1) * 512], o.ap, r=[o])

    def rstd_part(self, ps_in, npart, sq, lnv, rstd):
        p = self.p
        p.act(sq[0:npart, :], ps_in[0:npart, :], AF.Square)
        p.mm(self.ps[3][0:npart, :], [(self.onesb[0:npart, 0:npart], sq[0:npart, :])])
        p.act(lnv[0:npart, :], self.ps[3][0:npart, :], AF.Ln, bias=self.eps_col[0:npart, 0:1], scale=1.0 / npart)
        p.act(rstd[0:npart, :], lnv[0:npart, :], AF.Exp, scale=-0.5)

    def mla_core(self, L):
        p = self.p
        ph = Phase(self.nc)
        cm = ph.tile([128, 4, 512], BF16, "cmask")
        p.dma("sp", cm.ap, self.inp["c_cmask"], w=[cm])
        self.pump_open(ph)
        npump = len(self.pending) * 9 // (10 * 128) + 1
        kr = ph.tile([64, S], BF16, "krT")
        p.dma("sp", kr.ap, self.PT[48][0:64, :], w=[kr])
        qT = [ph.tile([128, S], BF16, "qT") for _ in range(2)]
        qR = [ph.tile([64, S], BF16, "qR") for _ in range(2)]
        kT = [ph.tile([128, S], BF16, "kT") for _ in range(2)]
        vt = [ph.tile([128, 32, 128], BF16, "vt") for _ in range(2)]
        ptt = [ph.tile([128, 512], BF16, "pt") for _ in range(2)]
        rec = ph.tile([128, 512], F32, "rec")
        ob = [ph.tile([128, 512], BF16, "ob") for _ in range(2)]
        ko = 0
        for h in range(16):
            q_ = qT[h % 2]
            r_ = qR[h % 2]
            k_ = kT[h % 2]
            v_ = vt[h % 2]
            p.dma("sp", q_.ap, self.PT[h], w=[q_])
            p.dma("sp", r_.ap, self.PT[16 + h][0:64, :], w=[r_])
            p.dma("sp", k_.ap, self.PT[32 + h], w=[k_])
            p.dma("sp", v_.ap, self.VTM[:, h * 128:(h + 1) * 128].rearrange("(j p) d -> p j d", p=128), w=[v_])
            for c in range(NSUB):
                qs = slice(c * 512, (c + 1) * 512)

                def qpairs(j, k_=k_, q_=q_, r_=r_, qs=qs):
                    ks = slice(j * 128, (j + 1) * 128)
                    return [(k_[:, ks], q_[:, qs]), (kr[:, ks], r_[:, qs])]

                def extra(j, c=c):
                    if j >= 4 * c:
                        return [(self.identb, cm[:, j - 4 * c, :])]
                    return []

                o = ob[ko % 2]
                ko += 1
                self.attn_chunk(4 * c + 4, lambda j, v_=v_: v_[:, j, :], qpairs, extra, lambda j: None, ptt,
                                (self.ps[0], self.ps[1]), self.ps[2], self.ps[3], rec, o)
                p.dma("sp", self.AT[h][:, qs], o.ap, r=[o])
                self.pump(npump)
        self.pump_flush()
        p.barrier()
        ph.close()

    def dsa_core(self, L):
        p = self.p
        nc = self.nc
        ph = Phase(nc)
        tab = ph.tile([32, 16], F32, "tab")
        p.dma("sp", tab.ap, self.inp["t5_table"], w=[tab])
        ohd = ph.tile([32, 2688], F32, "ohd")
        p.dma("sp", ohd.ap, self.inp["c_ohd"], w=[ohd])
        bv = ph.tile([16, 2688], F32, "bv")
        for i in range(6):
            w_ = min(512, 2688 - i * 512)
            ps = self.ps[i % 2]
            p.mm(ps[0:16, 0:w_], [(tab, ohd[:, i * 512:i * 512 + w_])])
            p.copy(bv[:, i * 512:i * 512 + w_], ps[0:16, 0:w_])
        p.dma("sp", self.BEXT, bv.ap, r=[bv])
        p.barrier()
        hk = [ph.tile([128, 2560], F32, "hk") for _ in range(2)]
        hkb = [ph.tile([128, 2560], BF16, "hkb") for _ in range(2)]
        tzb = [ph.tile([128, 2560], BF16, "tzb") for _ in range(2)]
        for h in range(16):
            a = hk[h % 2]
            b = hkb[h % 2]
            t = tzb[h % 2]
            p.dma("sp", a.ap, bass.AP(tensor=self.BEXT_h, offset=h * 2688, ap=[[1, 128], [1, 2560]]), w=[a])
            p.copy(b, a, E=("act" if h % 2 else "dve"))
            for i in range(5):
                ps = self.ps[2 + i % 2]
                p.mm(ps, [(self.antib, b[:, i * 512:(i + 1) * 512])])
                p.copy(t[:, i * 512:(i + 1) * 512], ps, E=("dve" if i % 2 else "act"))
            p.dma("sp", self.TZ[h], t.ap, r=[t])
        p.barrier()
        ph.close()
        ph = Phase(nc)
        cm = ph.tile([128, 4, 512], BF16, "cmask")
        p.dma("sp", cm.ap, self.inp["c_cmask"], w=[cm])
        sel = ph.tile([16, 16, 128], BF16, "sel16")
        p.dma("sp", sel.ap, self.inp["c_sel16"], w=[sel])
        kiT = ph.tile([64, S], BF16, "kiT")
        p.dma("sp", kiT.ap, self.PT[36][0:64, :], w=[kiT])
        wiT = ph.tile([16, 512], BF16, "wiT")
        idx = [ph.tile([128, 512], F32, "idx") for _ in range(32)]
        selb = [ph.tile([128, 512], BF16, "selb") for _ in range(32)]
        wrep = [ph.tile([128, 512], F32, "wrep") for _ in range(2)]
        qi = [ph.tile([64, 512], BF16, "qi") for _ in range(2)]
        tmp = [ph.tile([128, 512], F32, "itmp") for _ in range(2)]
        cmp = [ph.tile([128, 512], BF16, "cmp") for _ in range(2)]
        lo = ph.tile([128, 512], F32, "lo")
        mid = ph.tile([128, 512], F32, "mid")
        tsel = ph.tile([128, 512], F32, "tsel")
        tz = [ph.tile([128, 2560], BF16, "tz") for _ in range(2)]
        qT = [ph.tile([128, 512], BF16, "qT") for _ in range(2)]
        kT = [ph.tile([128, S], BF16, "kT") for _ in range(2)]
        vt = [ph.tile([128, 32, 128], BF16, "vt") for _ in range(2)]
        ptt = [ph.tile([128, 512], BF16, "pt") for _ in range(2)]
        rec = ph.tile([128, 512], F32, "rec")
        ob = [ph.tile([128, 512], BF16, "ob") for _ in range(2)]
        NIT = 21
        ko = 0
        kq = 0
        kg = 0
        for c in range(NSUB):
            qs = slice(c * 512, (c + 1) * 512)
            nk = 4 * c + 4
            p.dma("sp", wiT.ap, self.PT[37][0:16, qs], w=[wiT])
            for h in range(16):
                wr = wrep[h % 2]
                ps = self.ps[0]
                p.mm(ps, [(sel[:, h, :], wiT)])
                p.copy(wr, ps, E="act")
                q_ = qi[h % 2]
                p.dma("sp", q_.ap, self.PT[20 + h][0:64, qs], w=[q_])
                for j in range(nk):
                    ps = self.ps[1 + j % 2]
                    p.mm(ps, [(kiT[:, j * 128:(j + 1) * 128], q_)])
                    if h == 0:
                        p.stt(idx[j], ps, 0.0, wr, ALU.max, ALU.mult)
                    else:
                        t_ = tmp[j % 2]
                        p.stt(t_, ps, 0.0, wr, ALU.max, ALU.mult)
                        p.tt(idx[j], idx[j], t_, ALU.add, E="pool")
            for j in range(4 * c, nk):
                p.tt(idx[j], idx[j], cm[:, j - 4 * c, :], ALU.add, E="pool")
            p.op("dve", lambda e: e.memset(lo.ap, -64.0), w=[lo])
            for it in range(NIT):
                ck = 64.0 / (2 ** it)
                p.ts(mid, lo, ck, ALU.add)
                pc = self.ps[3 + it % 2]
                for j in range(nk):
                    cp = cmp[j % 2]
                    p.tt(cp, idx[j], mid, ALU.is_ge)
                    p.mm(pc, [(self.onesb, cp)], first=(j == 0), last=(j == nk - 1))
                p.ts(tsel, pc, 255.5, ALU.is_ge, ck, ALU.mult)
                p.tt(lo, lo, tsel, ALU.add)
            for j in range(nk):
                cp = tmp[j % 2]
                p.tt(cp, idx[j], lo, ALU.is_ge)
                p.ts(selb[j], cp, -1.0, ALU.add, -NEG, ALU.mult, E="pool")
            tzw = min(512 * c, 1664) + 896
            for g in range(4):
                k_ = kT[kg % 2]
                v_ = vt[kg % 2]
                kg += 1
                p.dma("sp", k_.ap[:, 0:nk * 128], self.PT[16 + g][:, 0:nk * 128], w=[k_])
                p.dma("sp", v_.ap[:, 0:nk, :],
                      self.VTM[0:nk * 128, g * 128:(g + 1) * 128].rearrange("(j p) d -> p j d", p=128), w=[v_])
                for r in range(4):
                    h = g * 4 + r
                    q_ = qT[kq % 2]
                    z_ = tz[kq % 2]
                    kq += 1
                    p.dma("sp", q_.ap, self.PT[h][:, qs], w=[q_])
                    p.dma("sp", z_.ap[:, 0:tzw], self.TZ[h][:, 0:tzw], w=[z_])

                    def qpairs(j, k_=k_, q_=q_):
                        return [(k_[:, j * 128:(j + 1) * 128], q_)]

                    def extra(j, c=c, z_=z_):
                        d0 = min(512 * c - 128 * j, 1664)
                        m0 = d0 + 384
                        return [(self.identb, z_[:, m0:m0 + 512]), (self.identb, selb[j])]

                    o = ob[ko % 2]
                    ko += 1
                    self.attn_chunk(nk, lambda j, v_=v_: v_[:, j, :], qpairs, extra, lambda j: None, ptt,
                                    (self.ps[5], self.ps[6]), self.ps[7], self.ps[0], rec, o)
                    p.dma("sp", self.AT[h][:, qs], o.ap, r=[o])
        p.barrier()
        ph.close()

    def out_proj(self, L):
        p = self.p
        ph = Phase(self.nc)
        W = self.WB[f"w_out_{L}"]
        m = L % 4
        nch = 20
        at = [[ph.tile([128, 512], BF16, "at") for _ in range(nch)] for _ in range(2)]
        wo = ph.tile([128, 20, D], BF16, "wo")
        for i in range(4):
            p.dma("sp", wo.ap[:, i * 5:(i + 1) * 5, :], W[:, i * 5:(i + 1) * 5, :], w=[wo])
        xs = [ph.tile([128, 512], F32, "xs") for _ in range(2)]
        xo = [ph.tile([128, 512], F32, "xo") for _ in range(2)]
        self.pump_open(ph)
        c_start = 0
        for tb in range(NSUB):
            tok = slice(tb * 512, (tb + 1) * 512)
            a = at[tb % 2]
            for c in range(c_start, nch):
                p.dma("sp", a[c].ap, self.AT[c][:, tok], w=[a[c]])
            for dc in range(16):
                ps = self.ps[dc % 2]
                p.mm(ps, [(wo[:, c, dc * 128:(dc + 1) * 128], a[c]) for c in range(c_start, nch)])
                self.pump(2)
                xt = xs[dc % 2]
                p.dma("sp", xt.ap, self.XT[dc][:, tok], w=[xt])
                o = xo[dc % 2]
                p.tt(o, ps, xt, ALU.add)
                p.dma("sp", self.XT[dc][:, tok], o.ap, r=[o])
        self.pump(10 ** 9)
        self.pump_flush()
        p.barrier()
        ph.close()

    def build(self):
        p = self.p
        self.declare()
        self.load_consts()
        self.eps_col = self.cst.tile([128, 1], F32, "eps")
        self.memT = [self.cst.tile([128, 256], F32, "memT") for _ in range(16)]
        self.mem_rstd = self.cst.tile([128, 256], F32, "memrstd")

        p.op("dve", lambda e: e.memset(self.eps_col.ap, EPS), w=[self.eps_col])
        if self.stop != "xt":
            self.convert_all()
        if self.stop not in ("xt", "cv", "ffn0"):
            self.mem_setup()
        import os
        if not os.environ.get("SKIP_XT"):
            self.x_to_xt()
        self.pending = []
        lay = () if self.stop in ("xt", "cv") else self.layers
        for li, L in enumerate(lay):
            self.ffn(L, 0)
            if li + 1 < len(lay) and self.stop is None:
                self.pending = self.conv_jobs(lay[li + 1])
            if self.stop == "ffn0":
                break
            self.attention(L)
            if self.stop == f"att{L}":
                break
            self.ffn(L, 1)
            if self.stop == f"ffn1_{L}":
                break
        if not os.environ.get("SKIP_OUT"):
            self.xt_to_out()
        p.wait_all_dma("sp")
        return self.nc


def t5_bucket_np(dist):
    n = np.maximum(dist, 0)
    nf = np.maximum(n, 1).astype(np.float32)
    large = 16 + (np.log(nf / np.float32(16)) / np.float32(math.log(2048 / 16)) * np.float32(16)).astype(np.int32)
    large = np.minimum(large, 31)
    return np.where(n < 16, n, large)


def host_consts():
    bf = ml_dtypes.bfloat16
    c = {}
    c["c_ident"] = np.eye(128, dtype=np.float32)
    c["c_identb"] = np.eye(128, dtype=np.float32).astype(bf)
    c["c_antib"] = np.eye(128, dtype=np.float32)[::-1].copy().astype(bf)
    c["c_onesb"] = np.ones((128, 128), dtype=np.float32).astype(bf)
    k = np.arange(128)[:, None, None]
    o = np.arange(4)[None, :, None]
    q = np.arange(512)[None, None, :]
    c["c_cmask"] = np.where(128 * o + k <= q, 0.0, NEG).astype(np.float32).astype(bf)
    sel = np.zeros((16, 16, 128), dtype=np.float32)
    for h in range(16):
        sel[h, h, :] = 1.0
    c["c_sel16"] = sel.astype(bf)
    sel80 = np.zeros((80, 16, 128), dtype=np.float32)
    for i in range(3):
        sel80[32 * i:32 * i + 16] = sel
    c["c_sel80"] = sel80.astype(bf)
    i = np.arange(2688)
    dist = i - 511
    oh = np.zeros((32, 2688), dtype=np.float32)
    b = t5_bucket_np(dist)
    valid = dist >= 0
    oh[b[valid], i[valid]] = 1.0
    c["c_ohd"] = oh
    ohg = np.zeros((3, 33, 384), dtype=np.float32)
    for g, dil in enumerate((1, 4, 16)):
        for ii in range(384):
            rel = ii - 127
            if 0 <= rel <= 128:
                ohg[g, int(t5_bucket_np(np.array(rel * dil))), ii] = 1.0
            else:
                ohg[g, 32, ii] = 1.0
    c["c_ohg"] = ohg
    half = 32
    inv = (np.float32(10000.0) ** (-np.arange(half, dtype=np.float32) / np.float32(half))).astype(np.float32)
    rope = np.zeros((64, 2), dtype=np.float32)
    rope[:, 0] = np.concatenate([inv, inv])
    rope[:, 1] = np.concatenate([-np.ones(32), np.ones(32)])
    c["c_rope"] = rope
    return c


def prep_inputs(inputs, b):
    f = lambda a: np.ascontiguousarray(a)
    m = {}
    m["x"] = f(inputs["x"][b])
    m["mem"] = f(inputs["mem"][b])
    m["positions"] = f(inputs["positions"][b].reshape(1, S).astype(np.int32))
    m["t5_table"] = f(inputs["t5_table"])
    m["ffn_norm"] = f(inputs["ffn_norm"].reshape(DEPTH * 2 * 16, 128))
    for n in ("ffn_w_gate", "ffn_w_up", "ffn_w_down", "mem_w_kv", "w_out", "a_w_in", "b_w_in", "b_w_uq",
              "b_w_ukv", "c_w_in", "d_w_in"):
        m[n] = inputs[n]
    m["attn_norm"] = f(inputs["attn_norm"].reshape(DEPTH * 16, 128))
    m["mem_norm"] = f(inputs["mem_norm"].reshape(DEPTH * 16, 128))
    m["mem_qk_g"] = f(inputs["mem_qk_g"].reshape(DEPTH * 2, 128))
    m["a_b_f"] = f(inputs["a_b_f"].reshape(16, 1))
    m["a_qk_g"] = f(inputs["a_qk_g"].reshape(2, 128))
    m["b_q_norm"] = f(inputs["b_q_norm"].reshape(4, 128))
    m["b_kv_norm"] = f(inputs["b_kv_norm"].reshape(4, 128))
    m["b_nope_g"] = f(inputs["b_nope_g"].reshape(2, 128))
    m["b_rope_g"] = f(inputs["b_rope_g"].reshape(2, 64))
    m["c_qk_g"] = f(inputs["c_qk_g"].reshape(6, 128))
    m["d_qk_g"] = f(inputs["d_qk_g"].reshape(2, 128))
    return m


_CACHE = {}


def kernel(**inputs):
    inputs = {k: np.asarray(v) for k, v in inputs.items()}
    if "nc" not in _CACHE:
        _CACHE["nc"] = K().build()
    nc = _CACHE["nc"]
    consts = host_consts()
    in_maps = []
    for b in range(8):
        m = prep_inputs(inputs, b)
        m.update(consts)
        in_maps.append(m)
    res = run_bass_kernel_spmd(nc, in_maps, core_ids=list(range(8)))
    out = np.stack([np.asarray(r["out"]) for r in res.results], axis=0)
    return out.astype(np.float32)
```

```python
import math
import os
import numpy as np
import ml_dtypes
import concourse.bass as bass
import concourse.mybir as mybir
from concourse.bass_utils import run_bass_kernel_spmd

F32 = mybir.dt.float32
BF16 = mybir.dt.bfloat16
I32 = mybir.dt.int32
AF = mybir.ActivationFunctionType
ALU = mybir.AluOpType
AX = mybir.AxisListType

S = 4096
D = 2048
DFF = 5632
NSUB = S // 512
DEPTH = 4
EPS = 1e-6
NEG = -30000.0


class T:
    __slots__ = ("ap", "w", "r", "psum")

    def __init__(self, ap, psum=False):
        self.ap = ap
        self.w = None
        self.r = {}
        self.psum = psum

    def __getitem__(self, idx):
        return V(self, self.ap[idx])


class V:
    __slots__ = ("t", "ap")

    def __init__(self, t, ap):
        self.t = t
        self.ap = ap

    def __getitem__(self, idx):
        return V(self.t, self.ap[idx])


def _tile(x):
    return x.t if isinstance(x, V) else x


class P:
    ENGS = ("pe", "act", "dve", "pool", "sp")

    def __init__(self, nc, n_dma_sems=8):
        self.nc = nc
        self.eng = {"pe": nc.tensor, "act": nc.scalar, "dve": nc.vector,
                    "pool": nc.gpsimd, "sp": nc.sync}
        self.sem = {}
        self.cnt = {}
        self.semid = {}
        self._nid = 0
        for e in self.ENGS:
            self._nid += 1
            self.sem[e] = nc.alloc_semaphore(f"s_{e}")
            self.cnt[e] = 0
            self.semid[e] = self._nid
        self.dsem = {}
        for q in ("sp", "act", "pool"):
            lst = []
            for i in range(n_dma_sems):
                self._nid += 1
                lst.append([nc.alloc_semaphore(f"d_{q}{i}"), 0, self._nid])
            self.dsem[q] = lst
        self.drr = {"sp": 0, "act": 0, "pool": 0}
        self.waited = {e: {} for e in self.ENGS}
        self.n_inst = 0
        self.n_wait = 0

    def _wait(self, E, tk, kind):
        if tk is None:
            return
        src, sid, sh, v = tk
        if src == E:
            if E == "pe" or kind != "raw":
                return
        wd = self.waited[E]
        if wd.get(sid, 0) >= v:
            return
        self.eng[E].wait_ge(sh, v)
        self.n_wait += 1
        wd[sid] = v

    def _deps(self, E, w, r):
        for x in r:
            t = _tile(x)
            if t is not None:
                self._wait(E, t.w, "raw")
                if t.psum:
                    for tk in t.r.values():
                        self._wait(E, tk, "war")
        for x in w:
            t = _tile(x)
            if t is not None:
                self._wait(E, t.w, "waw")
                for tk in t.r.values():
                    self._wait(E, tk, "war")

    def _mark(self, tk, w, r):
        for x in r:
            t = _tile(x)
            if t is not None:
                t.r[tk[1]] = tk
        for x in w:
            t = _tile(x)
            if t is not None:
                t.w = tk
                t.r = {}

    def op(self, E, fn, w=(), r=(), inc=True):
        self._deps(E, w, r)
        inst = fn(self.eng[E])
        self.n_inst += 1
        if inc:
            self.cnt[E] += 1
            inst.then_inc(self.sem[E], 1)
            tk = (E, self.semid[E], self.sem[E], self.cnt[E])
        else:
            tk = (E, self.semid[E], self.sem[E], self.cnt[E] + 1)
        self._mark(tk, w, r)
        return inst

    def dma(self, Q, out, in_, w=(), r=(), **kw):
        self._deps(Q, w, r)
        lst = self.dsem[Q]
        k = self.drr[Q]
        self.drr[Q] = (k + 1) % len(lst)
        ent = lst[k]
        if ent[1] > 0:
            self._wait(Q, ("dma", ent[2], ent[0], ent[1]), "raw")
        inst = self.eng[Q].dma_start(out=out, in_=in_, **kw)
        self.n_inst += 1
        ent[1] += 16
        inst.then_inc(ent[0], 16)
        tk = ("dma", ent[2], ent[0], ent[1])
        self._mark(tk, w, r)
        return tk

    def barrier(self):
        for E in self.ENGS:
            for E2 in self.ENGS:
                if E2 != E and self.cnt[E2] > 0:
                    self._wait(E, (E2, self.semid[E2], self.sem[E2], self.cnt[E2]), "raw")
            self.wait_all_dma(E)

    def wait_all_dma(self, E):
        for q in self.dsem:
            for ent in self.dsem[q]:
                if ent[1] > 0:
                    self._wait(E, ("dma", ent[2], ent[0], ent[1]), "raw")

    def mm(self, out, pairs, first=True, last=True):
        n = len(pairs)
        for i, (l, r_) in enumerate(pairs):
            st = first and i == 0
            sp = last and i == n - 1
            self.op("pe", lambda e, l=l, r_=r_, st=st, sp=sp: e.matmul(
                out.ap, l.ap, r_.ap, start=st, stop=sp), w=[out], r=[l, r_], inc=(i == n - 1))

    def act(self, out, in_, func, bias=None, scale=None, extra_r=(), accum=None):
        kw = {}
        rr = [in_] + list(extra_r)
        ww = [out]
        if bias is not None:
            if isinstance(bias, (V, T)):
                kw["bias"] = bias.ap
                rr.append(bias)
            else:
                kw["bias"] = bias
        if scale is not None:
            if isinstance(scale, (V, T)):
                kw["scale"] = scale.ap
                rr.append(scale)
            else:
                kw["scale"] = scale
        if accum is not None:
            kw["accum_out"] = accum.ap
            ww.append(accum)
        self.op("act", lambda e: e.activation(out.ap, in_.ap, func, **kw), w=ww, r=rr)

    def stt(self, out, in0, scalar, in1, op0, op1, E="dve"):
        rr = [in0, in1]
        sc = scalar
        if isinstance(scalar, (V, T)):
            sc = scalar.ap
            rr.append(scalar)
        self.op(E, lambda e: e.scalar_tensor_tensor(out.ap, in0.ap, sc, in1.ap, op0, op1), w=[out], r=rr)

    def tt(self, out, in0, in1, op, E="dve"):
        self.op(E, lambda e: e.tensor_tensor(out.ap, in0.ap, in1.ap, op), w=[out], r=[in0, in1])

    def ts(self, out, in0, s1, op0, s2=None, op1=None, E="dve"):
        rr = [in0]
        a1 = s1
        if isinstance(s1, (V, T)):
            a1 = s1.ap
            rr.append(s1)
        a2 = s2
        if isinstance(s2, (V, T)):
            a2 = s2.ap
            rr.append(s2)
        if op1 is None:
            self.op(E, lambda e: e.tensor_scalar(out.ap, in0.ap, a1, None, op0), w=[out], r=rr)
        else:
            self.op(E, lambda e: e.tensor_scalar(out.ap, in0.ap, a1, a2, op0, op1), w=[out], r=rr)

    def copy(self, out, in_, E="dve"):
        if E == "act":
            self.op("act", lambda e: e.activation(out.ap, in_.ap, AF.Copy), w=[out], r=[in_])
        else:
            self.op(E, lambda e: e.tensor_copy(out.ap, in_.ap), w=[out], r=[in_])


class Phase:
    _uid = 0

    def __init__(self, nc):
        from contextlib import ExitStack
        self.nc = nc
        self.st = ExitStack()
        self.k = 0

    def tile(self, shape, dt, name="t"):
        Phase._uid += 1
        h = self.st.enter_context(self.nc.sbuf_tensor(f"{name}_{Phase._uid}", list(shape), dt))
        return T(h.ap())

    def close(self):
        self.st.close()


WSPECS = {
    "ffn_w_gate": (D, DFF), "ffn_w_up": (D, DFF), "ffn_w_down": (DFF, D),
    "mem_w_kv": (D, 1024), "w_out": (2560, D),
    "a_w_in": (D, 6672), "b_w_in": (D, 1600), "b_w_uq": (512, 3072), "b_w_ukv": (512, 4096),
    "c_w_in": (D, 18944), "d_w_in": (D, 4688),
}


class K:
    def __init__(self, layers=(0, 1, 2, 3), stop=None, dbg=()):
        self.layers = layers
        self.stop = stop
        self.dbg = dbg
        nc = bass.Bass("TRN2", target_bir_lowering=False)
        self.nc = nc
        self.p = P(nc)
        self.inp = {}
        self.dbg_out = {}

    def din(self, name, shape, dt=F32):
        if self.stop == "xt" and (name in WSPECS):
            return None
        self.inp[name] = self.nc.dram_tensor(name, list(shape), dt, kind="ExternalInput").ap()
        return self.inp[name]

    def dscr(self, name, shape, dt):
        return self.nc.dram_tensor(name, list(shape), dt, kind="Internal").ap()

    def declare(self):
        nc = self.nc
        self.din("x", [S, D])
        self.din("mem", [256, D])
        self.din("positions", [1, S], I32)
        self.din("t5_table", [32, 16])
        self.din("ffn_norm", [DEPTH * 2 * 16, 128])
        for n in ("ffn_w_gate", "ffn_w_up", "ffn_w_down"):
            k, c = WSPECS[n]
            self.din(n, [DEPTH, 2, k, c])
        self.din("attn_norm", [DEPTH * 16, 128])
        self.din("mem_norm", [DEPTH * 16, 128])
        self.din("mem_w_kv", [DEPTH, D, 1024])
        self.din("mem_qk_g", [DEPTH * 2, 128])
        self.din("w_out", [DEPTH, 2560, D])
        self.din("a_w_in", [1, D, 6672])
        self.din("a_b_f", [16, 1])
        self.din("a_qk_g", [2, 128])
        self.din("b_w_in", [1, D, 1600])
        self.din("b_q_norm", [4, 128])
        self.din("b_w_uq", [1, 512, 3072])
        self.din("b_kv_norm", [4, 128])
        self.din("b_w_ukv", [1, 512, 4096])
        self.din("b_nope_g", [2, 128])
        self.din("b_rope_g", [2, 64])
        self.din("c_w_in", [1, D, 18944])
        self.din("c_qk_g", [6, 128])
        self.din("d_w_in", [1, D, 4688])
        self.din("d_qk_g", [2, 128])
        self.din("c_ident", [128, 128])
        self.din("c_identb", [128, 128], BF16)
        self.din("c_antib", [128, 128], BF16)
        self.din("c_onesb", [128, 128], BF16)
        self.din("c_cmask", [128, 4, 512], BF16)
        self.din("c_sel16", [16, 16, 128], BF16)
        self.din("c_sel80", [80, 16, 128], BF16)
        self.din("c_ohd", [32, 2688])
        self.din("c_ohg", [3, 33, 384])
        self.din("c_rope", [64, 2])
        self.out = nc.dram_tensor("out", [S, D], F32, kind="ExternalOutput").ap()
        self.XT = self.dscr("XT", [16, 128, S], F32)
        self.PT = self.dscr("PTs", [112, 128, S], BF16)
        self.VTM = self.dscr("VTM", [S, 6144], BF16)
        self.AT = self.dscr("ATs", [20, 128, S], BF16)
        self.FG = self.dscr("FGs", [16, S], F32)
        self.ROPE = self.dscr("ROPEs", [2, 64, S], F32)
        self.BEXT_h = self.nc.dram_tensor("BEXTs", [16, 2688], F32, kind="Internal")
        self.BEXT = self.BEXT_h.ap()
        self.TZ = self.dscr("TZs", [16, 128, 2560], BF16)
        self.VEXT_h = self.nc.dram_tensor("VEXTs", [48, 512], F32, kind="Internal")
        self.VEXT = self.VEXT_h.ap()
        self.WB = {}

    def dbgout(self, name, src_ap, shape, dt):
        o = self.nc.dram_tensor("dbg_" + name, list(shape), dt, kind="ExternalOutput").ap()
        self.p.barrier()
        self.p.dma("sp", o, src_ap)
        self.p.barrier()

    def load_consts(self):
        p = self.p
        nc = self.nc
        self.cst = Phase(nc)
        c = self.cst
        self.ident = c.tile([128, 128], F32, "ident")
        self.identb = c.tile([128, 128], BF16, "identb")
        self.antib = c.tile([128, 128], BF16, "antib")
        self.onesb = c.tile([128, 128], BF16, "onesb")
        for t, n in ((self.ident, "c_ident"), (self.identb, "c_identb"), (self.antib, "c_antib"),
                     (self.onesb, "c_onesb")):
            p.dma("sp", t.ap, self.inp[n], w=[t])
        self.ps = [T(nc.alloc_psum_tensor(f"ps{i}", [128, 512], F32).ap(), psum=True) for i in range(8)]
        self.g_ffn = self.colvecs("ffn_norm", 128)
        self.g_attn = self.colvecs("attn_norm", 64)
        self.g_mem = self.colvecs("mem_norm", 64)
        self.g_memqk = self.colvecs("mem_qk_g", 8)

    def colvecs(self, name, n):
        p = self.p
        c = self.cst
        st = c.tile([128, 128], F32, "cvst")
        out = c.tile([128, n], F32, "cv")
        p.dma("sp", st.ap[0:n, :], self.inp[name], w=[st])
        ps = self.ps[7]
        p.op("pe", lambda e: e.transpose(ps.ap[:, 0:n], st.ap[0:n, :], self.ident.ap[0:n, 0:n]),
             w=[ps], r=[st, self.ident])
        p.copy(out, ps[:, 0:n])
        return out

    def convert(self, key, src, Kdim, C):
        p = self.p
        KC = Kdim // 128
        dst = self.dscr("WB_" + key, [128, KC, C], BF16)
        self.WB[key] = dst
        CW = 2048
        jobs = [(kc, c0, min(CW, C - c0)) for kc in range(KC) for c0 in range(0, C, CW)]
        ph = self.cvph
        deferred = []
        engs = ("pool", "act", "dve")
        for i, (kc, c0, cw) in enumerate(jobs):
            st = self.cv_st[i % 3]
            bf = self.cv_bf[i % 3]
            p.dma("sp", st.ap[:, 0:cw], src[kc * 128:(kc + 1) * 128, c0:c0 + cw], w=[st])
            E = engs[i % 3]
            p.copy(bf[:, 0:cw], st[:, 0:cw], E=E)
            deferred.append((dst[:, kc, c0:c0 + cw], bf, cw))
            if len(deferred) > 1:
                d_, b_, w_ = deferred.pop(0)
                p.dma("sp", d_, b_.ap[:, 0:w_], r=[b_])
        while deferred:
            d_, b_, w_ = deferred.pop(0)
            p.dma("sp", d_, b_.ap[:, 0:w_], r=[b_])
        return dst

    def layer_weights(self, L):
        mixw = {0: [("a_w_in", 0)], 1: [("b_w_in", 0), ("b_w_uq", 0), ("b_w_ukv", 0)],
                2: [("c_w_in", 0)], 3: [("d_w_in", 0)]}
        out = []
        for s_ in (0, 1):
            for n in ("ffn_w_gate", "ffn_w_up", "ffn_w_down"):
                k, c = WSPECS[n]
                out.append((f"{n}_{L}_{s_}", self.inp[n][L, s_], k, c))
        if self.stop == "ffn0":
            return out
        out.append((f"mem_w_kv_{L}", self.inp["mem_w_kv"][L], D, 1024))
        out.append((f"w_out_{L}", self.inp["w_out"][L], 2560, D))
        for n, j in mixw[L % 4]:
            k, c = WSPECS[n]
            out.append((f"{n}_{L}", self.inp[n][j], k, c))
        return out

    def convert_all(self):
        nc = self.nc
        self.cvph = Phase(nc)
        self.cv_st = [self.cvph.tile([128, 2048], F32, "cvs") for _ in range(3)]
        self.cv_bf = [self.cvph.tile([128, 2048], BF16, "cvb") for _ in range(3)]
        for key, src, k, c in self.layer_weights(self.layers[0]):
            self.convert(key, src, k, c)
        self.p.barrier()
        self.cvph.close()

    def conv_jobs(self, L):
        jobs = []
        CW = 1024
        for key, src, Kdim, C in self.layer_weights(L):
            KC = Kdim // 128
            dst = self.dscr("WB_" + key, [128, KC, C], BF16)
            self.WB[key] = dst
            for kc in range(KC):
                for c0 in range(0, C, CW):
                    cw = min(CW, C - c0)
                    jobs.append((src[kc * 128:(kc + 1) * 128, c0:c0 + cw], dst[:, kc, c0:c0 + cw], cw))
        return jobs

    def pump_open(self, ph):
        self.pst = [ph.tile([128, 1024], F32, "pst") for _ in range(3)]
        self.pbf = [ph.tile([128, 1024], BF16, "pbf") for _ in range(3)]
        self.pi = 0
        assert not getattr(self, "pdef", [])
        self.pdef = []

    def pump(self, n):
        p = self.p
        for _ in range(n):
            if not self.pending:
                return
            src, dst, cw = self.pending.pop(0)
            st = self.pst[self.pi % 3]
            bf = self.pbf[self.pi % 3]
            self.pi += 1
            p.dma("sp", st.ap[:, 0:cw], src, w=[st])
            p.copy(bf[:, 0:cw], st[:, 0:cw], E="pool")
            self.pdef.append((dst, bf, cw))
            if len(self.pdef) > 2:
                d_, b_, w_ = self.pdef.pop(0)
                p.dma("sp", d_, b_.ap[:, 0:w_], r=[b_])

    def pump_flush(self):
        while self.pdef:
            d_, b_, w_ = self.pdef.pop(0)
            self.p.dma("sp", d_, b_.ap[:, 0:w_], r=[b_])

    def x_to_xt(self):
        p = self.p
        ph = Phase(self.nc)
        xt = [ph.tile([128, D], F32, "xin") for _ in range(8)]
        ob = [ph.tile([128, 512], F32, "xo") for _ in range(3)]
        k = 0
        for tb in range(NSUB):
            tiles = []
            for j in range(4):
                t = xt[(tb % 2) * 4 + j]
                r0 = tb * 512 + j * 128
                p.dma("sp", t.ap, self.inp["x"][r0:r0 + 128, :], w=[t])
                tiles.append(t)
            for dc in range(16):
                ps = self.ps[dc % 4]
                for j in range(4):
                    p.op("pe", lambda e, j=j, ps=ps, dc=dc: e.transpose(
                        ps.ap[:, j * 128:(j + 1) * 128], tiles[j].ap[:, dc * 128:(dc + 1) * 128], self.ident.ap),
                        w=[ps], r=[tiles[j], self.ident], inc=(j == 3))
                o = ob[k % 3]
                k += 1
                p.copy(o, ps, E=("dve" if dc % 2 == 0 else "act"))
                p.dma("sp", self.XT[dc][:, tb * 512:(tb + 1) * 512], o.ap, r=[o])
        p.barrier()
        ph.close()

    def xt_to_out(self):
        p = self.p
        ph = Phase(self.nc)
        xin = [ph.tile([128, 512], F32, "xi") for _ in range(6)]
        ot = [ph.tile([128, D], F32, "xo") for _ in range(8)]
        k = 0
        for tb in range(NSUB):
            outs = [ot[(tb % 2) * 4 + j] for j in range(4)]
            for dc in range(16):
                xi = xin[k % 6]
                k += 1
                p.dma("sp", xi.ap, self.XT[dc][:, tb * 512:(tb + 1) * 512], w=[xi])
                ps = self.ps[dc % 4]
                for j in range(4):
                    p.op("pe", lambda e, j=j, ps=ps, xi=xi: e.transpose(
                        ps.ap[:, j * 128:(j + 1) * 128], xi.ap[:, j * 128:(j + 1) * 128], self.ident.ap),
                        w=[ps], r=[xi, self.ident], inc=(j == 3))
                for j in range(4):
                    p.copy(outs[j][:, dc * 128:(dc + 1) * 128], ps[:, j * 128:(j + 1) * 128],
                           E=("dve" if (j % 2 == 0 or os.environ.get("XO_DVE")) else "act"))
            for j in range(4):
                r0 = tb * 512 + j * 128
                p.dma("sp", self.out[r0:r0 + 128, :], outs[j].ap, r=[outs[j]])
        p.barrier()
        ph.close()

    def rstd_from_ssq(self, ps_ssq, n_feat, lnv, rstd):
        p = self.p
        p.act(lnv, ps_ssq, AF.Ln, bias=self.eps_col[:, 0:1], scale=1.0 / n_feat)
        p.act(rstd, lnv, AF.Exp, scale=-0.5)

    def ffn(self, L, s):
        p = self.p
        nc = self.nc
        ph = Phase(nc)
        Wg = self.WB[f"ffn_w_gate_{L}_{s}"]
        Wu = self.WB[f"ffn_w_up_{L}_{s}"]
        Wd = self.WB[f"ffn_w_down_{L}_{s}"]
        gcol = self.g_ffn
        gbase = (L * 2 + s) * 16
        xnT = [[ph.tile([128, 512], BF16, "xn") for _ in range(16)] for _ in range(2)]
        hT = [[ph.tile([128, 512], BF16, "h") for _ in range(44)] for _ in range(2)]
        xs = [ph.tile([128, 512], F32, "xs") for _ in range(4)]
        sq = [ph.tile([128, 512], BF16, "sq") for _ in range(2)]
        lnv = ph.tile([128, 512], F32, "lnv")
        rstd = ph.tile([128, 512], F32, "rstd")
        wg = [ph.tile([128, 16, 128], BF16, "wg") for _ in range(2)]
        wu = [ph.tile([128, 16, 128], BF16, "wu") for _ in range(2)]
        wd = [ph.tile([128, 44, 128], BF16, "wd") for _ in range(2)]
        sg = [ph.tile([128, 512], F32, "sg") for _ in range(2)]
        xo = [ph.tile([128, 512], F32, "xo") for _ in range(2)]
        ps = self.ps
        kx = 0
        for tb in range(S // 1024):
            for sub in range(2):
                tok = slice((2 * tb + sub) * 512, (2 * tb + sub + 1) * 512)
                for kc in range(16):
                    xt = xs[kx % 4]
                    kx += 1
                    p.dma("sp", xt.ap, self.XT[kc][:, tok], w=[xt])
                    q = sq[kc % 2]
                    p.act(q, xt, AF.Square)
                    p.mm(ps[0], [(self.onesb, q)], first=(kc == 0), last=(kc == 15))
                self.rstd_from_ssq(ps[0], D, lnv, rstd)
                for kc in range(16):
                    xt = xs[kx % 4]
                    kx += 1
                    p.dma("sp", xt.ap, self.XT[kc][:, tok], w=[xt])
                    p.stt(xnT[sub][kc], xt, gcol[:, gbase + kc:gbase + kc + 1], rstd, ALU.mult, ALU.mult)
            for fc in range(44):
                a = wg[fc % 2]
                b = wu[fc % 2]
                p.dma("sp", a.ap, Wg[:, :, fc * 128:(fc + 1) * 128], w=[a])
                p.dma("sp", b.ap, Wu[:, :, fc * 128:(fc + 1) * 128], w=[b])
                for sub in range(2):
                    pg = ps[1 + sub * 2]
                    pu = ps[2 + sub * 2]
                    p.mm(pg, [(a[:, kc, :], xnT[sub][kc]) for kc in range(16)])
                    p.mm(pu, [(b[:, kc, :], xnT[sub][kc]) for kc in range(16)])
                    g_ = sg[sub]
                    p.act(g_, pg, AF.Silu)
                    p.tt(hT[sub][fc], g_, pu, ALU.mult)
            p.dma("sp", wd[0].ap, Wd[:, :, 0:128], w=[wd[0]])
            for dc in range(16):
                w_ = wd[dc % 2]
                if dc + 1 < 16:
                    wn = wd[(dc + 1) % 2]
                    p.dma("sp", wn.ap, Wd[:, :, (dc + 1) * 128:(dc + 2) * 128], w=[wn])
                for sub in range(2):
                    tok = slice((2 * tb + sub) * 512, (2 * tb + sub + 1) * 512)
                    py = ps[5 + sub]
                    p.mm(py, [(w_[:, fc, :], hT[sub][fc]) for fc in range(44)])
                    xt = xs[kx % 4]
                    kx += 1
                    p.dma("sp", xt.ap, self.XT[dc][:, tok], w=[xt])
                    o = xo[sub]
                    p.stt(o, py, 0.5, xt, ALU.mult, ALU.add)
                    p.dma("sp", self.XT[dc][:, tok], o.ap, r=[o])
        p.barrier()
        ph.close()

    def headnorm(self, ph_tiles, ps_in, ps_ssq, gcol, out_bf, nfeat=128, npart=128):
        p = self.p
        sq, lnv, rstd = ph_tiles
        p.act(sq[0:npart, :], ps_in[0:npart, :], AF.Square)
        p.mm(ps_ssq[0:npart, :], [(self.onesb[0:npart, 0:npart], sq[0:npart, :])])
        p.act(lnv[0:npart, :], ps_ssq[0:npart, :], AF.Ln, bias=self.eps_col[0:npart, 0:1], scale=1.0 / nfeat)
        p.act(rstd[0:npart, :], lnv[0:npart, :], AF.Exp, scale=-0.5)
        p.stt(out_bf, ps_in[0:npart, :], gcol, rstd[0:npart, :], ALU.mult, ALU.mult)

    def mem_setup(self):
        p = self.p
        ph = Phase(self.nc)
        mt = [ph.tile([128, D], F32, "memin") for _ in range(2)]
        sq = [ph.tile([128, 256], BF16, "msq") for _ in range(2)]
        lnv = ph.tile([128, 256], F32, "mln")
        for j in range(2):
            p.dma("sp", mt[j].ap, self.inp["mem"][j * 128:(j + 1) * 128, :], w=[mt[j]])
        for kc in range(16):
            ps = self.ps[kc % 2]
            for j in range(2):
                p.op("pe", lambda e, j=j, ps=ps, kc=kc: e.transpose(
                    ps.ap[:, j * 128:(j + 1) * 128], mt[j].ap[:, kc * 128:(kc + 1) * 128], self.ident.ap),
                    w=[ps], r=[mt[j], self.ident], inc=(j == 1))
            p.copy(self.memT[kc], ps[:, 0:256])
            q = sq[kc % 2]
            p.act(q, self.memT[kc], AF.Square)
            p.mm(self.ps[2][:, 0:256], [(self.onesb, q)], first=(kc == 0), last=(kc == 15))
        p.act(lnv, self.ps[2][:, 0:256], AF.Ln, bias=self.eps_col[:, 0:1], scale=1.0 / D)
        p.act(self.mem_rstd, lnv, AF.Exp, scale=-0.5)
        p.barrier()
        ph.close()

    def mem_kv(self, L, ph):
        p = self.p
        W = self.WB[f"mem_w_kv_{L}"]
        memn = [ph.tile([128, 256], BF16, "memn") for _ in range(16)]
        for kc in range(16):
            p.stt(memn[kc], self.memT[kc], self.g_mem[:, L * 16 + kc:L * 16 + kc + 1], self.mem_rstd,
                  ALU.mult, ALU.mult)
        kmT = [ph.tile([128, 256], BF16, "kmT") for _ in range(4)]
        vm = [ph.tile([128, 512], BF16, "vm") for _ in range(2)]
        wk = [ph.tile([128, 16, 128], BF16, "wmk") for _ in range(2)]
        wv = ph.tile([128, 16, 512], BF16, "wmv")
        sq = ph.tile([128, 512], BF16, "hsq")
        lnv = ph.tile([128, 512], F32, "hln")
        rstd = ph.tile([128, 512], F32, "hrs")
        for j in range(4):
            w_ = wk[j % 2]
            p.dma("sp", w_.ap, W[:, :, j * 128:(j + 1) * 128], w=[w_])
            ps = self.ps[j % 2]
            p.mm(ps[:, 0:256], [(w_[:, kc, :], memn[kc]) for kc in range(16)])
            self.headnorm((sq[:, 0:256], lnv[:, 0:256], rstd[:, 0:256]), ps[:, 0:256], self.ps[2][:, 0:256],
                          self.g_memqk[:, L * 2 + 1:L * 2 + 2], kmT[j])
        p.dma("sp", wv.ap, W[:, :, 512:1024], w=[wv])
        for t in range(2):
            ps = self.ps[3 + t]
            p.mm(ps, [(memn[kc][:, t * 128:(t + 1) * 128], wv[:, kc, :]) for kc in range(16)])
            p.copy(vm[t], ps)
        return kmT, vm

    def attn_chunk(self, ktiles, vtiles, qpairs_fn, extra_fn, bias_fn, pt_tiles, ps_s, ps_o, ps_d, rec, out_bf,
                   dv=128, nq=512):
        p = self.p
        n = ktiles

        def scores(j):
            p.mm(ps_s[j % 2][:, 0:nq], list(qpairs_fn(j)) + list(extra_fn(j)))

        scores(0)
        for j in range(n):
            pt = pt_tiles[j % 2]
            p.act(pt[:, 0:nq], ps_s[j % 2][:, 0:nq], AF.Exp, bias=bias_fn(j))
            if j + 1 < n:
                scores(j + 1)
            p.mm(ps_o[0:dv, 0:nq], [(vtiles(j), pt[:, 0:nq])], first=(j == 0), last=(j == n - 1))
            p.mm(ps_d[0:dv, 0:nq], [(self.onesb[:, 0:dv], pt[:, 0:nq])], first=(j == 0), last=(j == n - 1))
        p.op("dve", lambda e: e.reciprocal(rec.ap[0:dv, 0:nq], ps_d.ap[0:dv, 0:nq]), w=[rec], r=[ps_d])
        p.tt(out_bf, ps_o[0:dv, 0:nq], rec[0:dv, 0:nq], ALU.mult)

    def attention(self, L):
        m = L % 4
        if m == 1:
            self.rope_tables()
        self.in_proj(L)
        if m == 0:
            self.fox_core(L)
        elif m == 1:
            self.mla_core(L)
        elif m == 2:
            self.dil_core(L)
        elif m == 3:
            self.dsa_core(L)
        self.out_proj(L)

    def in_proj(self, L):
        p = self.p
        nc = self.nc
        m = L % 4
        ph = Phase(nc)
        wkey = {0: "a_w_in", 1: "b_w_in", 2: "c_w_in", 3: "d_w_in"}[m]
        W = self.WB[f"{wkey}_{L}"]
        ncols = WSPECS[wkey][1]
        memq0 = ncols - 512
        kmT, vm = self.mem_kv(L, ph)
        hT = [ph.tile([128, 512], BF16, "hT") for _ in range(16)]
        xs = [ph.tile([128, 512], F32, "xs") for _ in range(3)]
        sqx = [ph.tile([128, 512], BF16, "sqx") for _ in range(2)]
        lnv = ph.tile([128, 512], F32, "lnv")
        rstd = ph.tile([128, 512], F32, "rstd")
        hsq = ph.tile([128, 512], BF16, "hsq2")
        hsq2 = [hsq, ph.tile([128, 512], BF16, "hsq3")]
        hln = ph.tile([128, 512], F32, "hln2")
        hrs = ph.tile([128, 512], F32, "hrs2")
        wt = [ph.tile([128, 16, 128], BF16, "wt") for _ in range(2)]
        wvt = [ph.tile([128, 16, 512], BF16, "wvt") for _ in range(2)] if m in (0, 2, 3) else None
        ob = [ph.tile([128, 512], BF16, "ob") for _ in range(3)]
        qm = [ph.tile([128, 512], BF16, "qm") for _ in range(2)]
        ptt = [ph.tile([128, 512], BF16, "ptm") for _ in range(2)]
        rec = ph.tile([128, 512], F32, "rec")
        jobs = []
        vjobs = []
        if m == 0:
            self.gq_a = self.colv_small("a_qk_g", 2, ph)
            gq = ph.tile([128, 1], F32, "gqs")
            p.ts(gq, self.gq_a[:, 0:1], 128 ** -0.5, ALU.mult)
            for h in range(16):
                jobs.append((h * 128, 128, gq[:, 0:1], h))
            for h in range(16):
                jobs.append((2048 + h * 128, 128, self.gq_a[:, 1:2], 16 + h))
            for g in range(4):
                vjobs.append((4096 + g * 512, g * 512))
        if m == 2:
            self.gq_c = self.colv_small("c_qk_g", 6, ph)
            gqc = ph.tile([128, 3], F32, "gqc")
            for g in range(3):
                p.ts(gqc[:, g:g + 1], self.gq_c[:, 2 * g:2 * g + 1], 128 ** -0.5, ALU.mult)
            for g in range(3):
                for h in range(16):
                    jobs.append((g * 6144 + h * 128, 128, gqc[:, g:g + 1], g * 32 + h))
                    jobs.append((g * 6144 + 2048 + h * 128, 128, self.gq_c[:, 2 * g + 1:2 * g + 2], g * 32 + 16 + h))
                for v4 in range(4):
                    vjobs.append((g * 6144 + 4096 + v4 * 512, g * 2048 + v4 * 512))
        if m == 3:
            self.gq_d = self.colv_small("d_qk_g", 2, ph)
            gqd = ph.tile([128, 1], F32, "gqd")
            p.ts(gqd, self.gq_d[:, 0:1], 128 ** -0.5, ALU.mult)
            for h in range(16):
                jobs.append((h * 128, 128, gqd[:, 0:1], h))
            for g in range(4):
                jobs.append((2048 + g * 128, 128, self.gq_d[:, 1:2], 16 + g))
            vjobs.append((2560, 0))
            for h in range(16):
                jobs.append((3072 + h * 64, 64, 64 ** -0.5, 20 + h))
            jobs.append((4096, 64, 1.0, 36))
            jobs.append((4160, 16, 16 ** -0.5, 37))
        mla = None
        if m == 1:
            mla = self.mla_setup(L, ph)
            mla["wt"] = wt
        gmq = ph.tile([128, 1], F32, "gmq")
        p.ts(gmq, self.g_memqk[:, L * 2:L * 2 + 1], 128 ** -0.5, ALU.mult)
        groups = []
        jgroup = {}
        for ji, job in enumerate(jobs):
            c0, cw = job[0], job[1]
            if groups and groups[-1][0] + groups[-1][1] == c0 and groups[-1][1] + cw <= 512:
                groups[-1][1] += cw
                groups[-1][2].append(ji)
            else:
                groups.append([c0, cw, [ji]])
            jgroup[ji] = (len(groups) - 1, c0 - groups[-1][0])
        wgt = [ph.tile([128, 16, 512], BF16, "wgt") for _ in range(2)] if jobs else None
        kx = 0
        ko = 0
        for tb in range(NSUB):
            tok = slice(tb * 512, (tb + 1) * 512)
            for kc in range(16):
                xt = xs[kx % 3]
                kx += 1
                p.dma("sp", xt.ap, self.XT[kc][:, tok], w=[xt])
                q = sqx[kc % 2]
                p.act(q, xt, AF.Square)
                p.mm(self.ps[0], [(self.onesb, q)], first=(kc == 0), last=(kc == 15))
            self.rstd_from_ssq(self.ps[0], D, lnv, rstd)
            for kc in range(16):
                xt = xs[kx % 3]
                kx += 1
                p.dma("sp", xt.ap, self.XT[kc][:, tok], w=[xt])
                p.stt(hT[kc], xt, self.g_attn[:, L * 16 + kc:L * 16 + kc + 1], rstd, ALU.mult, ALU.mult)
            pend = None

            def stage_b(st):
                nonlocal ko
                ji, cw, gcol, chunk, ps = st
                o = ob[ko % 3]
                ko += 1
                if isinstance(gcol, float):
                    p.ts(o[0:cw, :], ps[0:cw, :], gcol, ALU.mult)
                else:
                    q = hsq2[ji % 2]
                    p.mm(self.ps[3][0:cw, :], [(self.onesb[0:cw, 0:cw], q[0:cw, :])])
                    p.act(hln[0:cw, :], self.ps[3][0:cw, :], AF.Ln, bias=self.eps_col[0:cw, 0:1], scale=1.0 / cw)
                    p.act(hrs[0:cw, :], hln[0:cw, :], AF.Exp, scale=-0.5)
                    p.stt(o[0:cw, :], ps[0:cw, :], gcol, hrs[0:cw, :], ALU.mult, ALU.mult)
                p.dma("sp", self.PT[chunk][0:cw, tok], o.ap[0:cw, :], r=[o])

            for ji, (c0, cw, gcol, chunk) in enumerate(jobs):
                gi, off = jgroup[ji]
                if groups[gi][2][0] == ji:
                    if gi == 0:
                        p.dma("sp", wgt[0].ap[:, :, 0:groups[0][1]], W[:, :, groups[0][0]:groups[0][0] + groups[0][1]],
                              w=[wgt[0]])
                    if gi + 1 < len(groups):
                        g1 = groups[gi + 1]
                        t1_ = wgt[(gi + 1) % 2]
                        p.dma("sp", t1_.ap[:, :, 0:g1[1]], W[:, :, g1[0]:g1[0] + g1[1]], w=[t1_])
                w_ = wgt[gi % 2]
                ps = self.ps[(1, 2, 4, 5)[ji % 4]]
                p.mm(ps[0:cw, :], [(w_[:, kc, off:off + cw], hT[kc]) for kc in range(16)])
                if not isinstance(gcol, float):
                    p.act(hsq2[ji % 2][0:cw, :], ps[0:cw, :], AF.Square)
                if pend is not None:
                    stage_b(pend)
                pend = (ji, cw, gcol, chunk, ps)
            if pend is not None:
                stage_b(pend)
            if m == 1:
                def ob_next():
                    nonlocal ko
                    o_ = ob[ko % 3]
                    ko += 1
                    return o_
                self.mla_block(L, mla, tb, hT, W, (hsq, hln, hrs), ob_next)
            if m == 0:
                w_ = wt[0]
                p.dma("sp", w_.ap[:, :, 0:16], W[:, :, 6144:6160], w=[w_])
                ps = self.ps[1]
                p.mm(ps[0:16, :], [(w_[:, kc, 0:16], hT[kc]) for kc in range(16)])
                o32 = xs[kx % 3]
                kx += 1
                p.copy(o32[0:16, :], ps[0:16, :])
                p.dma("sp", self.FG[:, tok], o32.ap[0:16, :], r=[o32])
            for vi, (c0, v0) in enumerate(vjobs):
                w_ = wvt[vi % 2]
                p.dma("sp", w_.ap, W[:, :, c0:c0 + 512], w=[w_])
                for t in range(4):
                    ps = self.ps[4 + t % 2]
                    p.mm(ps, [(hT[kc][:, t * 128:(t + 1) * 128], w_[:, kc, :]) for kc in range(16)])
                    o = ob[ko % 3]
                    ko += 1
                    p.copy(o, ps, E=("act" if t % 2 else "dve"))
                    r0 = tb * 512 + t * 128
                    p.dma("sp", self.VTM[r0:r0 + 128, v0:v0 + 512], o.ap, r=[o])
            for j in range(4):
                w_ = wt[j % 2]
                c0 = memq0 + j * 128
                p.dma("sp", w_.ap, W[:, :, c0:c0 + 128], w=[w_])
                ps = self.ps[1 + j % 2]
                p.mm(ps, [(w_[:, kc, :], hT[kc]) for kc in range(16)])
                q_ = qm[j % 2]
                self.headnorm((hsq, hln, hrs), ps, self.ps[3], gmq[:, 0:1], q_)
                o = ob[ko % 3]
                ko += 1
                self.attn_chunk(
                    2, lambda t, j=j: vm[t][:, j * 128:(j + 1) * 128],
                    lambda t, j=j, q_=q_: [(kmT[j][:, t * 128:(t + 1) * 128], q_)],
                    lambda t: [], lambda t: None, ptt, (self.ps[4], self.ps[5]), self.ps[6], self.ps[7], rec, o)
                p.dma("sp", self.AT[16 + j][:, tok], o.ap, r=[o])
        p.barrier()
        ph.close()

    def colv_small(self, name, n, ph):
        p = self.p
        w = self.inp[name].shape[1]
        st = ph.tile([128, 128], F32, "cvs")
        out = ph.tile([128, n], F32, "cvo")
        p.dma("sp", st.ap[0:n, 0:w], self.inp[name], w=[st])
        ps = self.ps[7]
        p.op("pe", lambda e: e.transpose(ps.ap[0:w, 0:n], st.ap[0:n, 0:w], self.ident.ap[0:n, 0:n]),
             w=[ps], r=[st, self.ident])
        p.copy(out[0:w, :], ps[0:w, 0:n])
        return out

    def fox_core(self, L):
        p = self.p
        nc = self.nc
        ph = Phase(nc)
        self.pump_open(ph)
        npump = len(self.pending) * 9 // (10 * 128) + 1
        fg = ph.tile([80, S], F32, "fg")
        negbf = ph.tile([80, 1], F32, "negbf")
        p.op("pool", lambda e: e.memset(fg.ap, 0.0), w=[fg])
        p.op("pool", lambda e: e.memset(negbf.ap, 0.0), w=[negbf])
        for i in range(3):
            p.dma("sp", fg.ap[32 * i:32 * i + 16, :], self.FG, w=[fg])
            p.dma("sp", negbf.ap[32 * i:32 * i + 16, :], self.inp["a_b_f"], w=[negbf])
        p.ts(negbf, negbf, -1.0, ALU.mult)
        ones16 = ph.tile([80, S], F32, "ones16")
        p.op("pool", lambda e: e.memset(ones16.ap, 1.0), w=[ones16])
        lf = ph.tile([80, S], F32, "lf")
        ncum = ph.tile([80, S], F32, "ncum")
        p.act(lf, fg, AF.Exp, bias=negbf[:, 0:1], scale=-1.0)
        p.act(lf, lf, AF.Ln, bias=1.0)
        p.op("dve", lambda e: e.tensor_tensor_scan(ncum.ap, ones16.ap, lf.ap, 0.0, ALU.mult, ALU.add),
             w=[ncum], r=[ones16, lf])
        c_hi = ph.tile([80, S], BF16, "chi")
        c_mid = ph.tile([80, S], BF16, "cmid")
        c_lo = ph.tile([80, S], BF16, "clo")
        r1 = lf
        r2 = ones16
        p.ts(c_hi, ncum, -1.0, ALU.mult)
        p.stt(r1, ncum, -1.0, c_hi, ALU.mult, ALU.subtract)
        p.copy(c_mid, r1)
        p.tt(r2, r1, c_mid, ALU.subtract)
        p.copy(c_lo, r2)
        c_all = fg_b = ph.tile([80, S], BF16, "call")
        p.op("pool", lambda e: e.memset(c_all.ap, 0.0), w=[c_all])
        p.copy(c_all[0:16, :], c_hi[0:16, :])
        p.copy(c_all[32:48, :], c_mid[32:48, :])
        p.copy(c_all[64:80, :], c_lo[64:80, :])
        nct = ph.tile([128, 32, 16], F32, "nct")
        ps = self.ps[7]
        for j in range(32):
            p.op("pe", lambda e, j=j: e.transpose(ps.ap[:, j * 16:(j + 1) * 16], ncum.ap[0:16, j * 128:(j + 1) * 128],
                                                  self.ident.ap[0:16, 0:16]),
                 w=[ps], r=[ncum, self.ident], inc=(j == 31))
        p.copy(nct, ps.ap.rearrange("p (j h) -> p j h", h=16) if False else ps)
        sel = ph.tile([80, 16, 128], BF16, "sel80")
        p.dma("sp", sel.ap, self.inp["c_sel80"], w=[sel])
        cm = ph.tile([128, 4, 512], BF16, "cmask")
        p.dma("sp", cm.ap, self.inp["c_cmask"], w=[cm])
        qT = [ph.tile([128, S], BF16, "qT") for _ in range(2)]
        kT = [ph.tile([128, S], BF16, "kT") for _ in range(2)]
        vt = [ph.tile([128, 32, 128], BF16, "vt") for _ in range(2)]
        ptt = [ph.tile([128, 512], BF16, "pt") for _ in range(2)]
        rec = ph.tile([128, 512], F32, "rec")
        ob = [ph.tile([128, 512], BF16, "ob") for _ in range(2)]
        nctv = nct.ap
        ko = 0
        for h in range(16):
            q_ = qT[h % 2]
            k_ = kT[h % 2]
            v_ = vt[h % 2]
            p.dma("sp", q_.ap, self.PT[h], w=[q_])
            p.dma("sp", k_.ap, self.PT[16 + h], w=[k_])
            p.dma("sp", v_.ap, self.VTM[:, h * 128:(h + 1) * 128].rearrange("(j p) d -> p j d", p=128), w=[v_])
            for c in range(NSUB):
                qs = slice(c * 512, (c + 1) * 512)

                def qpairs(j, k_=k_, q_=q_, qs=qs):
                    return [(k_[:, j * 128:(j + 1) * 128], q_[:, qs])]

                def extra(j, c=c, h=h, qs=qs):
                    e = [(sel[:, h, :], c_all[:, qs])]
                    if j >= 4 * c:
                        e.append((self.identb, cm[:, j - 4 * c, :]))
                    return e

                def bias(j, h=h):
                    return V(nct, nctv[:, j, h:h + 1])

                o = ob[ko % 2]
                ko += 1
                self.attn_chunk(4 * c + 4, lambda j, v_=v_: v_[:, j, :], qpairs, extra, bias, ptt,
                                (self.ps[0], self.ps[1]), self.ps[2], self.ps[3], rec, o)
                p.dma("sp", self.AT[h][:, qs], o.ap, r=[o])
                self.pump(npump)
        self.pump_flush()
        p.barrier()
        ph.close()

    def dil_core(self, L):
        p = self.p
        nc = self.nc
        ph = Phase(nc)
        tab = ph.tile([33, 16], F32, "tab33")
        p.op("dve", lambda e: e.memset(tab.ap[32:33, :], NEG), w=[tab])
        p.dma("sp", tab.ap[0:32, :], self.inp["t5_table"], w=[tab])
        ohg = ph.tile([33, 3, 384], F32, "ohg")
        p.dma("sp", ohg.ap, self.inp["c_ohg"].rearrange("g v i -> v g i"), w=[ohg])
        vx = ph.tile([16, 384], F32, "vx")
        for g in range(3):
            ps = self.ps[g]
            p.mm(ps[0:16, 0:384], [(tab, ohg[:, g, :])])
            p.copy(vx, ps[0:16, 0:384])
            p.dma("sp", self.VEXT[g * 16:(g + 1) * 16, 0:384], vx.ap, r=[vx])
        p.barrier()
        bz = [[ph.tile([128, 256], BF16, "bz") for _ in range(16)] for _ in range(3)]
        hk = [ph.tile([128, 256], F32, "hk") for _ in range(2)]
        hkb = [ph.tile([128, 256], BF16, "hkb") for _ in range(2)]
        for g in range(3):
            for h in range(16):
                i = g * 16 + h
                a = hk[i % 2]
                b = hkb[i % 2]
                src = bass.AP(tensor=self.VEXT_h, offset=i * 512, ap=[[1, 128], [1, 256]])
                p.dma("sp", a.ap, src, w=[a])
                p.copy(b, a)
                ps = self.ps[i % 2]
                p.mm(ps[:, 0:256], [(self.antib, b)])
                p.copy(bz[g][h], ps[:, 0:256], E=("act" if i % 2 else "dve"))
        self.pump_open(ph)
        npump = len(self.pending) * 9 // (10 * 372) + 1
        num = ph.tile([128, S], F32, "num")
        den = ph.tile([128, S], F32, "den")
        qT = [ph.tile([128, S], BF16, "qT") for _ in range(2)]
        kT = [ph.tile([128, S], BF16, "kT") for _ in range(2)]
        vt = [ph.tile([128, 32, 128], BF16, "vt") for _ in range(2)]
        ptt = [ph.tile([128, 128], BF16, "pt") for _ in range(2)]
        ob = ph.tile([128, S], BF16, "ob")
        kk = 0
        kb = 0
        for h in range(16):
            for g, dil in enumerate((1, 4, 16)):
                q_ = qT[kk % 2]
                k_ = kT[kk % 2]
                v_ = vt[kk % 2]
                kk += 1
                nb = S // dil // 128
                p.dma("sp", q_.ap, self.PT[g * 32 + h], w=[q_])
                p.dma("sp", k_.ap, self.PT[g * 32 + 16 + h], w=[k_])
                c0 = g * 2048 + h * 128
                vsrc = self.VTM[:, c0:c0 + 128].rearrange("(jj p r) d -> p r jj d", p=128, r=dil)
                vdst = v_.ap.rearrange("p (r jj) d -> p r jj d", r=dil)
                for r in range(dil):
                    p.dma("sp", vdst[:, r], vsrc[:, r], w=[v_])
                units = []
                for r in range(dil):
                    for i in range(nb):
                        tl = [i] if i == 0 else [i - 1, i]
                        for ti, jj in enumerate(tl):
                            units.append((r, i, ti, jj, len(tl)))

                def u_scores(ui):
                    r, i, ti, jj, nt = units[ui]
                    q0 = r + dil * 128 * i
                    k0 = r + dil * 128 * jj
                    qsl = slice(q0, q0 + dil * 127 + 1, dil)
                    ksl = slice(k0, k0 + dil * 127 + 1, dil)
                    bsl = slice(0, 128) if jj == i else slice(128, 256)
                    p.mm(self.ps[ui % 2][:, 0:128], [(k_[:, ksl], q_[:, qsl]), (self.identb, bz[g][h][:, bsl])])

                u_scores(0)
                for ui, (r, i, ti, jj, nt) in enumerate(units):
                    pt = ptt[ui % 2]
                    p.act(pt, self.ps[ui % 2][:, 0:128], AF.Exp)
                    if ui + 1 < len(units):
                        u_scores(ui + 1)
                    if ti == 0:
                        kb += 1
                    ps_o = self.ps[2 + kb % 2]
                    ps_d = self.ps[4 + kb % 2]
                    p.mm(ps_o[:, 0:128], [(v_[:, r * nb + jj, :], pt)], first=(ti == 0), last=(ti == nt - 1))
                    p.mm(ps_d[:, 0:128], [(self.onesb, pt)], first=(ti == 0), last=(ti == nt - 1))
                    if ui % 8 == 7:
                        self.pump(npump)
                    if ti == nt - 1:
                        q0 = r + dil * 128 * i
                        qsl = slice(q0, q0 + dil * 127 + 1, dil)
                        if g == 0:
                            p.copy(num[:, qsl], ps_o[:, 0:128], E="act")
                            p.copy(den[:, qsl], ps_d[:, 0:128], E="dve")
                        else:
                            p.tt(num[:, qsl], ps_o[:, 0:128], num[:, qsl], ALU.add)
                            p.tt(den[:, qsl], ps_d[:, 0:128], den[:, qsl], ALU.add)
            p.op("dve", lambda e: e.reciprocal(den.ap, den.ap), w=[den], r=[den])
            p.tt(ob, num, den, ALU.mult)
            p.dma("sp", self.AT[h], ob.ap, r=[ob])
        self.pump_flush()
        p.barrier()
        ph.close()

    def rope_tables(self):
        p = self.p
        ph = Phase(self.nc)
        TWO_PI = 2.0 * math.pi
        cr = ph.tile([64, 2], F32, "crope")
        p.dma("sp", cr.ap, self.inp["c_rope"], w=[cr])
        posi = ph.tile([64, S], I32, "posi")
        pa = self.inp["positions"]
        p.dma("sp", posi.ap, bass.AP(tensor=pa.tensor, offset=0, ap=[[0, 64], [1, S]]), w=[posi])
        ang = ph.tile([64, S], F32, "ang")
        t1 = ph.tile([64, S], F32, "t1")
        ki = ph.tile([64, S], I32, "ki")
        p.copy(ang, posi)
        p.ts(ang, ang, cr[:, 0:1], ALU.mult)
        p.ts(t1, ang, 1.0 / TWO_PI, ALU.mult)
        p.copy(ki, t1)
        p.copy(t1, ki)
        r = ph.tile([64, S], F32, "r")
        p.stt(r, t1, -TWO_PI, ang, ALU.mult, ALU.add)
        p.ts(t1, r, math.pi, ALU.is_gt)
        p.stt(r, t1, -TWO_PI, r, ALU.mult, ALU.add)
        p.ts(t1, r, -1.0, ALU.mult, math.pi, ALU.is_gt)
        p.stt(r, t1, TWO_PI, r, ALU.mult, ALU.add)
        p.ts(r, r, 3.14159, ALU.min, -3.14159, ALU.max)
        p.act(t1, r, AF.Sin)
        p.ts(t1, t1, cr[:, 1:2], ALU.mult)
        p.dma("sp", self.ROPE[1], t1.ap, r=[t1])
        p.stt(ang, r, -1.0, r, ALU.mult, ALU.max)
        hp = ph.tile([64, 1], F32, "halfpi")
        p.op("dve", lambda e: e.memset(hp.ap, math.pi / 2), w=[hp])
        p.act(ang, ang, AF.Sin, bias=hp[:, 0:1], scale=-1.0)
        p.dma("sp", self.ROPE[0], ang.ap, r=[ang])
        p.barrier()
        ph.close()

    def mla_setup(self, L, ph):
        p = self.p
        st = {}
        st["gq"] = self.colv_small("b_q_norm", 4, ph)
        st["gkv"] = self.colv_small("b_kv_norm", 4, ph)
        gn = self.colv_small("b_nope_g", 2, ph)
        gr = self.colv_small("b_rope_g", 2, ph)
        sc = 192 ** -0.5
        gqn = ph.tile([128, 1], F32, "gqn")
        p.ts(gqn, gn[:, 0:1], sc, ALU.mult)
        st["gqn"] = gqn
        st["gkn"] = gn
        grs = ph.tile([64, 4], F32, "grs")
        p.ts(grs[:, 0:1], gr[0:64, 0:1], sc, ALU.mult)
        p.copy(grs[:, 2:3], gr[0:64, 1:2])
        stg = ph.tile([128, 128], F32, "grst")
        src = self.inp["b_rope_g"]
        p.dma("sp", stg.ap[0:2, 0:32], src[:, 32:64], w=[stg])
        p.dma("sp", stg.ap[0:2, 32:64], src[:, 0:32], w=[stg])
        ps = self.ps[7]
        p.op("pe", lambda e: e.transpose(ps.ap[0:64, 0:2], stg.ap[0:2, 0:64], self.ident.ap[0:2, 0:2]),
             w=[ps], r=[stg, self.ident])
        p.ts(grs[:, 1:2], ps[0:64, 0:1], sc, ALU.mult)
        p.copy(grs[:, 3:4], ps[0:64, 1:2])
        st["grs"] = grs
        st["cqf"] = [ph.tile([128, 512], F32, "cqf") for _ in range(4)]
        st["cqn"] = [ph.tile([128, 512], BF16, "cqn") for _ in range(4)]
        st["ckvn"] = [ph.tile([128, 512], BF16, "ckvn") for _ in range(4)]
        st["cs"] = [ph.tile([64, 512], F32, "cs") for _ in range(2)]
        st["ce"] = [ph.tile([64, 512], F32, "ce") for _ in range(2)]
        st["tt"] = [ph.tile([64, 512], F32, "ropet") for _ in range(2)]
        st["wq"] = [ph.tile([128, 4, 128], BF16, "wuq") for _ in range(2)]
        st["wq2"] = [ph.tile([128, 4, 64], BF16, "wuq2") for _ in range(2)]
        st["wv"] = [ph.tile([128, 4, 512], BF16, "wukvv") for _ in range(2)]
        return st

    def rope_apply(self, st, ps_a, ps_b, rstd64, g_a, g_b, out_bf):
        p = self.p
        ce = st["ce"]
        cs = st["cs"]
        tt = st["tt"]
        p.tt(ce[0], cs[0], rstd64, ALU.mult)
        p.tt(ce[1], cs[1], rstd64, ALU.mult)
        p.stt(tt[0], ps_a, g_a, ce[0], ALU.mult, ALU.mult)
        p.stt(tt[1], ps_b, g_b, ce[1], ALU.mult, ALU.mult)
        p.tt(out_bf, tt[0], tt[1], ALU.add)

    def mla_block(self, L, st, tb, hT, W, tmp, ob_next):
        p = self.p
        tok = slice(tb * 512, (tb + 1) * 512)
        hsq, hln, hrs = tmp
        Wuq = self.WB[f"b_w_uq_{L}"]
        Wukv = self.WB[f"b_w_ukv_{L}"]
        wt = st["wt"]
        p.dma("sp", st["cs"][0].ap, self.ROPE[0][:, tok], w=[st["cs"][0]])
        p.dma("sp", st["cs"][1].ap, self.ROPE[1][:, tok], w=[st["cs"][1]])
        for which, c_base, gcols, dst in (("q", 0, st["gq"], st["cqn"]), ("kv", 512, st["gkv"], st["ckvn"])):
            for kc in range(4):
                w_ = wt[kc % 2]
                p.dma("sp", w_.ap, W[:, :, c_base + kc * 128:c_base + (kc + 1) * 128], w=[w_])
                ps = self.ps[1 + kc % 2]
                p.mm(ps, [(w_[:, k2, :], hT[k2]) for k2 in range(16)])
                p.copy(st["cqf"][kc], ps)
                p.act(hsq, ps, AF.Square)
                p.mm(self.ps[3], [(self.onesb, hsq)], first=(kc == 0), last=(kc == 3))
            p.act(hln, self.ps[3], AF.Ln, bias=self.eps_col[:, 0:1], scale=1.0 / 512)
            p.act(hrs, hln, AF.Exp, scale=-0.5)
            for kc in range(4):
                p.stt(dst[kc], st["cqf"][kc], gcols[:, kc:kc + 1], hrs, ALU.mult, ALU.mult)
        w_ = wt[0]
        w2 = wt[1]
        p.dma("sp", w_.ap[:, :, 0:64], W[:, :, 1024:1088], w=[w_])
        p.dma("sp", w2.ap[:, :, 0:32], W[:, :, 1056:1088], w=[w2])
        p.dma("sp", w2.ap[:, :, 32:64], W[:, :, 1024:1056], w=[w2])
        pa = self.ps[1]
        pb = self.ps[2]
        p.mm(pa[0:64, :], [(w_[:, k2, 0:64], hT[k2]) for k2 in range(16)])
        p.mm(pb[0:64, :], [(w2[:, k2, 0:64], hT[k2]) for k2 in range(16)])
        self.rstd_part(pa, 64, hsq, hln, hrs)
        o = ob_next()
        self.rope_apply(st, pa[0:64, :], pb[0:64, :], hrs[0:64, :], st["grs"][:, 2:3], st["grs"][:, 3:4], o[0:64, :])
        p.dma("sp", self.PT[48][0:64, tok], o.ap[0:64, :], r=[o])
        for h in range(16):
            wq = st["wq"][h % 2]
            p.dma("sp", wq.ap, Wuq[:, :, h * 192:h * 192 + 128], w=[wq])
            ps = self.ps[1 + h % 2]
            p.mm(ps, [(wq[:, kc, :], st["cqn"][kc]) for kc in range(4)])
            o = ob_next()
            self.headnorm((hsq, hln, hrs), ps, self.ps[3], st["gqn"][:, 0:1], o)
            p.dma("sp", self.PT[h][:, tok], o.ap, r=[o])
            wa = st["wq2"][0]
            wb = st["wq2"][1]
            c0 = h * 192 + 128
            p.dma("sp", wa.ap, Wuq[:, :, c0:c0 + 64], w=[wa])
            p.dma("sp", wb.ap[:, :, 0:32], Wuq[:, :, c0 + 32:c0 + 64], w=[wb])
            p.dma("sp", wb.ap[:, :, 32:64], Wuq[:, :, c0:c0 + 32], w=[wb])
            pa = self.ps[4]
            pb = self.ps[5]
            p.mm(pa[0:64, :], [(wa[:, kc, :], st["cqn"][kc]) for kc in range(4)])
            p.mm(pb[0:64, :], [(wb[:, kc, :], st["cqn"][kc]) for kc in range(4)])
            self.rstd_part(pa, 64, hsq, hln, hrs)
            o = ob_next()
            self.rope_apply(st, pa[0:64, :], pb[0:64, :], hrs[0:64, :], st["grs"][:, 0:1], st["grs"][:, 1:2],
                            o[0:64, :])
            p.dma("sp", self.PT[16 + h][0:64, tok], o.ap[0:64, :], r=[o])
            wq = st["wq"][(h + 1) % 2]
            p.dma("sp", wq.ap, Wukv[:, :, h * 256:h * 256 + 128], w=[wq])
            ps = self.ps[1 + (h + 1) % 2]
            p.mm(ps, [(wq[:, kc, :], st["ckvn"][kc]) for kc in range(4)])
            o = ob_next()
            self.headnorm((hsq, hln, hrs), ps, self.ps[3], st["gkn"][:, 1:2], o)
            p.dma("sp", self.PT[32 + h][:, tok], o.ap, r=[o])
        wv5 = Wukv.rearrange("p kc (h two d) -> p kc h two d", two=2, d=128)
        for g4 in range(4):
            wv = st["wv"][g4 % 2]
            wdst = wv.ap.rearrange("p kc (h d) -> p kc h d", d=128)
            for kc in range(4):
                p.dma("sp", wdst[:, kc], wv5[:, kc, g4 * 4:(g4 + 1) * 4, 1, :], w=[wv])
            for t in range(4):
                ps = self.ps[4 + t % 2]
                p.mm(ps, [(st["ckvn"][kc][:, t * 128:(t + 1) * 128], wv[:, kc, :]) for kc in range(4)])
                o = ob_next()
                p.copy(o, ps, E=("act" if t % 2 else "dve"))
                r0 = tb * 512 + t * 128
                p.dma("sp", self.VTM[r0:r0 + 128, g4 * 512:(g4 + 1) * 512], o.ap, r=[o])

    def rstd_part(self, ps_in, npart, sq, lnv, rstd):
        p = self.p
        p.act(sq[0:npart, :], ps_in[0:npart, :], AF.Square)
        p.mm(self.ps[3][0:npart, :], [(self.onesb[0:npart, 0:npart], sq[0:npart, :])])
        p.act(lnv[0:npart, :], self.ps[3][0:npart, :], AF.Ln, bias=self.eps_col[0:npart, 0:1], scale=1.0 / npart)
        p.act(rstd[0:npart, :], lnv[0:npart, :], AF.Exp, scale=-0.5)

    def mla_core(self, L):
        p = self.p
        ph = Phase(self.nc)
        cm = ph.tile([128, 4, 512], BF16, "cmask")
        p.dma("sp", cm.ap, self.inp["c_cmask"], w=[cm])
        self.pump_open(ph)
        npump = len(self.pending) * 9 // (10 * 128) + 1
        kr = ph.tile([64, S], BF16, "krT")
        p.dma("sp", kr.ap, self.PT[48][0:64, :], w=[kr])
        qT = [ph.tile([128, S], BF16, "qT") for _ in range(2)]
        qR = [ph.tile([64, S], BF16, "qR") for _ in range(2)]
        kT = [ph.tile([128, S], BF16, "kT") for _ in range(2)]
        vt = [ph.tile([128, 32, 128], BF16, "vt") for _ in range(2)]
        ptt = [ph.tile([128, 512], BF16, "pt") for _ in range(2)]
        rec = ph.tile([128, 512], F32, "rec")
        ob = [ph.tile([128, 512], BF16, "ob") for _ in range(2)]
        ko = 0
        for h in range(16):
            q_ = qT[h % 2]
            r_ = qR[h % 2]
            k_ = kT[h % 2]
            v_ = vt[h % 2]
            p.dma("sp", q_.ap, self.PT[h], w=[q_])
            p.dma("sp", r_.ap, self.PT[16 + h][0:64, :], w=[r_])
            p.dma("sp", k_.ap, self.PT[32 + h], w=[k_])
            p.dma("sp", v_.ap, self.VTM[:, h * 128:(h + 1) * 128].rearrange("(j p) d -> p j d", p=128), w=[v_])
            for c in range(NSUB):
                qs = slice(c * 512, (c + 1) * 512)

                def qpairs(j, k_=k_, q_=q_, r_=r_, qs=qs):
                    ks = slice(j * 128, (j + 1) * 128)
                    return [(k_[:, ks], q_[:, qs]), (kr[:, ks], r_[:, qs])]

                def extra(j, c=c):
                    if j >= 4 * c:
                        return [(self.identb, cm[:, j - 4 * c, :])]
                    return []

                o = ob[ko % 2]
                ko += 1
                self.attn_chunk(4 * c + 4, lambda j, v_=v_: v_[:, j, :], qpairs, extra, lambda j: None, ptt,
                                (self.ps[0], self.ps[1]), self.ps[2], self.ps[3], rec, o)
                p.dma("sp", self.AT[h][:, qs], o.ap, r=[o])
                self.pump(npump)
        self.pump_flush()
        p.barrier()
        ph.close()

    def dsa_core(self, L):
        p = self.p
        nc = self.nc
        ph = Phase(nc)
        tab = ph.tile([32, 16], F32, "tab")
        p.dma("sp", tab.ap, self.inp["t5_table"], w=[tab])
        ohd = ph.tile([32, 2688], F32, "ohd")
        p.dma("sp", ohd.ap, self.inp["c_ohd"], w=[ohd])
        bv = ph.tile([16, 2688], F32, "bv")
        for i in range(6):
            w_ = min(512, 2688 - i * 512)
            ps = self.ps[i % 2]
            p.mm(ps[0:16, 0:w_], [(tab, ohd[:, i * 512:i * 512 + w_])])
            p.copy(bv[:, i * 512:i * 512 + w_], ps[0:16, 0:w_])
        p.dma("sp", self.BEXT, bv.ap, r=[bv])
        p.barrier()
        hk = [ph.tile([128, 2560], F32, "hk") for _ in range(2)]
        hkb = [ph.tile([128, 2560], BF16, "hkb") for _ in range(2)]
        tzb = [ph.tile([128, 2560], BF16, "tzb") for _ in range(2)]
        for h in range(16):
            a = hk[h % 2]
            b = hkb[h % 2]
            t = tzb[h % 2]
            p.dma("sp", a.ap, bass.AP(tensor=self.BEXT_h, offset=h * 2688, ap=[[1, 128], [1, 2560]]), w=[a])
            p.copy(b, a, E=("act" if h % 2 else "dve"))
            for i in range(5):
                ps = self.ps[2 + i % 2]
                p.mm(ps, [(self.antib, b[:, i * 512:(i + 1) * 512])])
                p.copy(t[:, i * 512:(i + 1) * 512], ps, E=("dve" if i % 2 else "act"))
            p.dma("sp", self.TZ[h], t.ap, r=[t])
        p.barrier()
        ph.close()
        ph = Phase(nc)
        cm = ph.tile([128, 4, 512], BF16, "cmask")
        p.dma("sp", cm.ap, self.inp["c_cmask"], w=[cm])
        sel = ph.tile([16, 16, 128], BF16, "sel16")
        p.dma("sp", sel.ap, self.inp["c_sel16"], w=[sel])
        kiT = ph.tile([64, S], BF16, "kiT")
        p.dma("sp", kiT.ap, self.PT[36][0:64, :], w=[kiT])
        wiT = ph.tile([16, 512], BF16, "wiT")
        idx = [ph.tile([128, 512], F32, "idx") for _ in range(32)]
        selb = [ph.tile([128, 512], BF16, "selb") for _ in range(32)]
        wrep = [ph.tile([128, 512], F32, "wrep") for _ in range(2)]
        qi = [ph.tile([64, 512], BF16, "qi") for _ in range(2)]
        tmp = [ph.tile([128, 512], F32, "itmp") for _ in range(2)]
        cmp = [ph.tile([128, 512], BF16, "cmp") for _ in range(2)]
        lo = ph.tile([128, 512], F32, "lo")
        mid = ph.tile([128, 512], F32, "mid")
        tsel = ph.tile([128, 512], F32, "tsel")
        tz = [ph.tile([128, 2560], BF16, "tz") for _ in range(2)]
        qT = [ph.tile([128, 512], BF16, "qT") for _ in range(2)]
        kT = [ph.tile([128, S], BF16, "kT") for _ in range(2)]
        vt = [ph.tile([128, 32, 128], BF16, "vt") for _ in range(2)]
        ptt = [ph.tile([128, 512], BF16, "pt") for _ in range(2)]
        rec = ph.tile([128, 512], F32, "rec")
        ob = [ph.tile([128, 512], BF16, "ob") for _ in range(2)]
        NIT = 21
        ko = 0
        kq = 0
        kg = 0
        for c in range(NSUB):
            qs = slice(c * 512, (c + 1) * 512)
            nk = 4 * c + 4
            p.dma("sp", wiT.ap, self.PT[37][0:16, qs], w=[wiT])
            for h in range(16):
                wr = wrep[h % 2]
                ps = self.ps[0]
                p.mm(ps, [(sel[:, h, :], wiT)])
                p.copy(wr, ps, E="act")
                q_ = qi[h % 2]
                p.dma("sp", q_.ap, self.PT[20 + h][0:64, qs], w=[q_])
                for j in range(nk):
                    ps = self.ps[1 + j % 2]
                    p.mm(ps, [(kiT[:, j * 128:(j + 1) * 128], q_)])
                    if h == 0:
                        p.stt(idx[j], ps, 0.0, wr, ALU.max, ALU.mult)
                    else:
                        t_ = tmp[j % 2]
                        p.stt(t_, ps, 0.0, wr, ALU.max, ALU.mult)
                        p.tt(idx[j], idx[j], t_, ALU.add, E="pool")
            for j in range(4 * c, nk):
                p.tt(idx[j], idx[j], cm[:, j - 4 * c, :], ALU.add, E="pool")
            p.op("dve", lambda e: e.memset(lo.ap, -64.0), w=[lo])
            for it in range(NIT):
                ck = 64.0 / (2 ** it)
                p.ts(mid, lo, ck, ALU.add)
                pc = self.ps[3 + it % 2]
                for j in range(nk):
                    cp = cmp[j % 2]
                    p.tt(cp, idx[j], mid, ALU.is_ge)
                    p.mm(pc, [(self.onesb, cp)], first=(j == 0), last=(j == nk - 1))
                p.ts(tsel, pc, 255.5, ALU.is_ge, ck, ALU.mult)
                p.tt(lo, lo, tsel, ALU.add)
            for j in range(nk):
                cp = tmp[j % 2]
                p.tt(cp, idx[j], lo, ALU.is_ge)
                p.ts(selb[j], cp, -1.0, ALU.add, -NEG, ALU.mult, E="pool")
            tzw = min(512 * c, 1664) + 896
            for g in range(4):
                k_ = kT[kg % 2]
                v_ = vt[kg % 2]
                kg += 1
                p.dma("sp", k_.ap[:, 0:nk * 128], self.PT[16 + g][:, 0:nk * 128], w=[k_])
                p.dma("sp", v_.ap[:, 0:nk, :],
                      self.VTM[0:nk * 128, g * 128:(g + 1) * 128].rearrange("(j p) d -> p j d", p=128), w=[v_])
                for r in range(4):
                    h = g * 4 + r
                    q_ = qT[kq % 2]
                    z_ = tz[kq % 2]
                    kq += 1
                    p.dma("sp", q_.ap, self.PT[h][:, qs], w=[q_])
                    p.dma("sp", z_.ap[:, 0:tzw], self.TZ[h][:, 0:tzw], w=[z_])

                    def qpairs(j, k_=k_, q_=q_):
                        return [(k_[:, j * 128:(j + 1) * 128], q_)]

                    def extra(j, c=c, z_=z_):
                        d0 = min(512 * c - 128 * j, 1664)
                        m0 = d0 + 384
                        return [(self.identb, z_[:, m0:m0 + 512]), (self.identb, selb[j])]

                    o = ob[ko % 2]
                    ko += 1
                    self.attn_chunk(nk, lambda j, v_=v_: v_[:, j, :], qpairs, extra, lambda j: None, ptt,
                                    (self.ps[5], self.ps[6]), self.ps[7], self.ps[0], rec, o)
                    p.dma("sp", self.AT[h][:, qs], o.ap, r=[o])
        p.barrier()
        ph.close()

    def out_proj(self, L):
        p = self.p
        ph = Phase(self.nc)
        W = self.WB[f"w_out_{L}"]
        m = L % 4
        nch = 20
        at = [[ph.tile([128, 512], BF16, "at") for _ in range(nch)] for _ in range(2)]
        wo = ph.tile([128, 20, D], BF16, "wo")
        for i in range(4):
            p.dma("sp", wo.ap[:, i * 5:(i + 1) * 5, :], W[:, i * 5:(i + 1) * 5, :], w=[wo])
        xs = [ph.tile([128, 512], F32, "xs") for _ in range(2)]
        xo = [ph.tile([128, 512], F32, "xo") for _ in range(2)]
        self.pump_open(ph)
        c_start = 0
        for tb in range(NSUB):
            tok = slice(tb * 512, (tb + 1) * 512)
            a = at[tb % 2]
            for c in range(c_start, nch):
                p.dma("sp", a[c].ap, self.AT[c][:, tok], w=[a[c]])
            for dc in range(16):
                ps = self.ps[dc % 2]
                p.mm(ps, [(wo[:, c, dc * 128:(dc + 1) * 128], a[c]) for c in range(c_start, nch)])
                self.pump(2)
                xt = xs[dc % 2]
                p.dma("sp", xt.ap, self.XT[dc][:, tok], w=[xt])
                o = xo[dc % 2]
                p.tt(o, ps, xt, ALU.add)
                p.dma("sp", self.XT[dc][:, tok], o.ap, r=[o])
        self.pump(10 ** 9)
        self.pump_flush()
        p.barrier()
        ph.close()

    def build(self):
        p = self.p
        self.declare()
        self.load_consts()
        self.eps_col = self.cst.tile([128, 1], F32, "eps")
        self.memT = [self.cst.tile([128, 256], F32, "memT") for _ in range(16)]
        self.mem_rstd = self.cst.tile([128, 256], F32, "memrstd")

        p.op("dve", lambda e: e.memset(self.eps_col.ap, EPS), w=[self.eps_col])
        if self.stop != "xt":
            self.convert_all()
        if self.stop not in ("xt", "cv", "ffn0"):
            self.mem_setup()
        import os
        if not os.environ.get("SKIP_XT"):
            self.x_to_xt()
        self.pending = []
        lay = () if self.stop in ("xt", "cv") else self.layers
        for li, L in enumerate(lay):
            self.ffn(L, 0)
            if li + 1 < len(lay) and self.stop is None:
                self.pending = self.conv_jobs(lay[li + 1])
            if self.stop == "ffn0":
                break
            self.attention(L)
            if self.stop == f"att{L}":
                break
            self.ffn(L, 1)
            if self.stop == f"ffn1_{L}":
                break
        if not os.environ.get("SKIP_OUT"):
            self.xt_to_out()
        p.wait_all_dma("sp")
        return self.nc


def t5_bucket_np(dist):
    n = np.maximum(dist, 0)
    nf = np.maximum(n, 1).astype(np.float32)
    large = 16 + (np.log(nf / np.float32(16)) / np.float32(math.log(2048 / 16)) * np.float32(16)).astype(np.int32)
    large = np.minimum(large, 31)
    return np.where(n < 16, n, large)


def host_consts():
    bf = ml_dtypes.bfloat16
    c = {}
    c["c_ident"] = np.eye(128, dtype=np.float32)
    c["c_identb"] = np.eye(128, dtype=np.float32).astype(bf)
    c["c_antib"] = np.eye(128, dtype=np.float32)[::-1].copy().astype(bf)
    c["c_onesb"] = np.ones((128, 128), dtype=np.float32).astype(bf)
    k = np.arange(128)[:, None, None]
    o = np.arange(4)[None, :, None]
    q = np.arange(512)[None, None, :]
    c["c_cmask"] = np.where(128 * o + k <= q, 0.0, NEG).astype(np.float32).astype(bf)
    sel = np.zeros((16, 16, 128), dtype=np.float32)
    for h in range(16):
        sel[h, h, :] = 1.0
    c["c_sel16"] = sel.astype(bf)
    sel80 = np.zeros((80, 16, 128), dtype=np.float32)
    for i in range(3):
        sel80[32 * i:32 * i + 16] = sel
    c["c_sel80"] = sel80.astype(bf)
    i = np.arange(2688)
    dist = i - 511
    oh = np.zeros((32, 2688), dtype=np.float32)
    b = t5_bucket_np(dist)
    valid = dist >= 0
    oh[b[valid], i[valid]] = 1.0
    c["c_ohd"] = oh
    ohg = np.zeros((3, 33, 384), dtype=np.float32)
    for g, dil in enumerate((1, 4, 16)):
        for ii in range(384):
            rel = ii - 127
            if 0 <= rel <= 128:
                ohg[g, int(t5_bucket_np(np.array(rel * dil))), ii] = 1.0
            else:
                ohg[g, 32, ii] = 1.0
    c["c_ohg"] = ohg
    half = 32
    inv = (np.float32(10000.0) ** (-np.arange(half, dtype=np.float32) / np.float32(half))).astype(np.float32)
    rope = np.zeros((64, 2), dtype=np.float32)
    rope[:, 0] = np.concatenate([inv, inv])
    rope[:, 1] = np.concatenate([-np.ones(32), np.ones(32)])
    c["c_rope"] = rope
    return c


def prep_inputs(inputs, b):
    f = lambda a: np.ascontiguousarray(a)
    m = {}
    m["x"] = f(inputs["x"][b])
    m["mem"] = f(inputs["mem"][b])
    m["positions"] = f(inputs["positions"][b].reshape(1, S).astype(np.int32))
    m["t5_table"] = f(inputs["t5_table"])
    m["ffn_norm"] = f(inputs["ffn_norm"].reshape(DEPTH * 2 * 16, 128))
    for n in ("ffn_w_gate", "ffn_w_up", "ffn_w_down", "mem_w_kv", "w_out", "a_w_in", "b_w_in", "b_w_uq",
              "b_w_ukv", "c_w_in", "d_w_in"):
        m[n] = inputs[n]
    m["attn_norm"] = f(inputs["attn_norm"].reshape(DEPTH * 16, 128))
    m["mem_norm"] = f(inputs["mem_norm"].reshape(DEPTH * 16, 128))
    m["mem_qk_g"] = f(inputs["mem_qk_g"].reshape(DEPTH * 2, 128))
    m["a_b_f"] = f(inputs["a_b_f"].reshape(16, 1))
    m["a_qk_g"] = f(inputs["a_qk_g"].reshape(2, 128))
    m["b_q_norm"] = f(inputs["b_q_norm"].reshape(4, 128))
    m["b_kv_norm"] = f(inputs["b_kv_norm"].reshape(4, 128))
    m["b_nope_g"] = f(inputs["b_nope_g"].reshape(2, 128))
    m["b_rope_g"] = f(inputs["b_rope_g"].reshape(2, 64))
    m["c_qk_g"] = f(inputs["c_qk_g"].reshape(6, 128))
    m["d_qk_g"] = f(inputs["d_qk_g"].reshape(2, 128))
    return m


_CACHE = {}


def kernel(**inputs):
    inputs = {k: np.asarray(v) for k, v in inputs.items()}
    if "nc" not in _CACHE:
        _CACHE["nc"] = K().build()
    nc = _CACHE["nc"]
    consts = host_consts()
    in_maps = []
    for b in range(8):
        m = prep_inputs(inputs, b)
        m.update(consts)
        in_maps.append(m)
    res = run_bass_kernel_spmd(nc, in_maps, core_ids=list(range(8)))
    out = np.stack([np.asarray(r["out"]) for r in res.results], axis=0)
    return out.astype(np.float32)
```
